# Optimizing a Trainium2 kernel written in Bass

```python
import jax, jax.numpy as jnp
from jax import lax
import numpy as np

D_MODEL = 1024
BATCH = 32
SEQ = 2048
DEPTH = 1

POOL_WIDTH = D_MODEL
POOL_WINDOWS = (2, 4, 8, 16)
POOL_GROUPS = len(POOL_WINDOWS)
POOL_GROUP_WIDTH = POOL_WIDTH // POOL_GROUPS
SSD_EXPAND = 2
SSD_INNER = SSD_EXPAND * D_MODEL
SSD_HEAD_DIM = 64
SSD_HEADS = SSD_INNER // SSD_HEAD_DIM
SSD_GROUPS = 8
SSD_HEADS_PER_GROUP = SSD_HEADS // SSD_GROUPS
SSD_STATE = 128
SSD_CONV = 4
SSD_CHUNK = 256
SSD_CONV_DIM = SSD_INNER + 2 * SSD_GROUPS * SSD_STATE
SSD_NORM_GROUP = SSD_INNER // SSD_GROUPS
D_FF = 4 * D_MODEL
N_BRANCHES = 2
OFF_POOL = POOL_WIDTH
OFF_Z = OFF_POOL + SSD_INNER
OFF_XBC = OFF_Z + SSD_CONV_DIM
OFF_DT = OFF_XBC + SSD_HEADS
IN_PROJ_WIDTH = OFF_DT + N_BRANCHES * D_MODEL
DEEPNORM_ALPHA = (2.0 * DEPTH) ** 0.25
DEEPNORM_BETA = (8.0 * DEPTH) ** -0.25
LN_EPS = 1e-5
RMS_EPS = 1e-5

kernel_name = "pool_ssd_gated_hybrid_deepnorm"


def layer_norm(x, g, b):
    xf = x.astype(jnp.float32)
    mu = jnp.mean(xf, axis=-1, keepdims=True)
    var = jnp.mean(jnp.square(xf - mu), axis=-1, keepdims=True)
    return ((xf - mu) * lax.rsqrt(var + LN_EPS) * g + b).astype(x.dtype)


def causal_multiscale_pool(u):
    bsz, s, _ = u.shape
    uf = u.astype(jnp.float32).reshape(bsz, s, POOL_GROUPS, POOL_GROUP_WIDTH)
    cs = jnp.cumsum(uf, axis=1)
    pos = jnp.arange(1, s + 1, dtype=jnp.float32)
    outs = []
    for gi, w in enumerate(POOL_WINDOWS):
        csg = cs[:, :, gi]
        lag = jnp.pad(csg, ((0, 0), (w, 0), (0, 0)))[:, :s]
        cnt = jnp.minimum(pos, float(w))[None, :, None]
        outs.append((csg - lag) / cnt)
    pooled = jnp.stack(outs, axis=2)
    return pooled - uf


def causal_depthwise_conv(u, w, b):
    k_width = w.shape[0]
    s = u.shape[1]
    up = jnp.pad(u, ((0, 0), (k_width - 1, 0), (0, 0)))
    y = up[:, 0:s] * w[0]
    for k in range(1, k_width):
        y = y + up[:, k:k + s] * w[k]
    return y + b


def segsum_exp(a):
    t = a.shape[-1]
    cs = jnp.cumsum(a, axis=-1)
    seg = cs[..., :, None] - cs[..., None, :]
    mask = jnp.tril(jnp.ones((t, t), dtype=bool))
    return jnp.exp(jnp.where(mask, seg, -jnp.inf))


def ssd_chunked(xdt, da, bm, cm):
    bsz, s = xdt.shape[:2]
    n_chunks = -(-s // SSD_CHUNK)
    pad = n_chunks * SSD_CHUNK - s

    def to_chunks(t):
        t = jnp.pad(t, ((0, 0), (0, pad)) + ((0, 0),) * (t.ndim - 2))
        return t.reshape((bsz, n_chunks, SSD_CHUNK) + t.shape[2:])

    xc, ac, bc, cc = to_chunks(xdt), to_chunks(da), to_chunks(bm), to_chunks(cm)
    a_cs = jnp.cumsum(ac, axis=2)
    lmat = segsum_exp(jnp.moveaxis(ac, 2, -1))
    cb = jnp.einsum('bclgn,bcsgn->bcgls', cc, bc)
    y_diag = jnp.einsum('bcgls,bcgrls,bcsgrp->bclgrp', cb, lmat, xc)
    decay_to_end = jnp.exp(a_cs[:, :, -1:] - a_cs)
    states = jnp.einsum('bclgn,bclgr,bclgrp->bcgrpn', bc, decay_to_end, xc)
    chunk_decay = jnp.exp(a_cs[:, :, -1])

    def step(h, inp):
        s_c, d_c = inp
        return d_c[..., None, None] * h + s_c, h

    h0 = jnp.zeros_like(states[:, 0])
    _, prev = lax.scan(step, h0, (jnp.moveaxis(states, 1, 0), jnp.moveaxis(chunk_decay, 1, 0)))
    prev = jnp.moveaxis(prev, 0, 1)
    y_off = jnp.einsum('bclgn,bcgrpn,bclgr->bclgrp', cc, prev, jnp.exp(a_cs))
    y = (y_diag + y_off).reshape((bsz, n_chunks * SSD_CHUNK) + xdt.shape[2:])
    return y[:, :s]


def hybrid_layer(h, w_in, b_gates, conv_w, conv_b, dt_bias, a_log, d_skip,
                 ssd_norm_w, w_ssd_proj, w_pool_group, pool_scale, w_out,
                 ln1_g, ln1_b, w_up, w_down, ln2_g, ln2_b):
    bsz, s, _ = h.shape
    proj = h @ w_in
    u_pool, z, xbc, dt_raw, gate_logits = jnp.split(
        proj, [OFF_POOL, OFF_Z, OFF_XBC, OFF_DT], axis=-1)
    gates = jax.nn.sigmoid((gate_logits + b_gates).astype(jnp.float32))
    gates = gates.reshape(bsz, s, N_BRANCHES, D_MODEL)

    pooled = causal_multiscale_pool(u_pool)
    y_pool = jnp.einsum('bsgc,gcd->bsgd', pooled, w_pool_group)
    y_pool = y_pool.reshape(bsz, s, POOL_WIDTH) * pool_scale

    xbc = jax.nn.silu(causal_depthwise_conv(xbc, conv_w, conv_b))
    xs, bm, cm = jnp.split(xbc, [SSD_INNER, SSD_INNER + SSD_GROUPS * SSD_STATE], axis=-1)
    xs = xs.astype(jnp.float32).reshape(bsz, s, SSD_GROUPS, SSD_HEADS_PER_GROUP, SSD_HEAD_DIM)
    bm = bm.astype(jnp.float32).reshape(bsz, s, SSD_GROUPS, SSD_STATE)
    cm = cm.astype(jnp.float32).reshape(bsz, s, SSD_GROUPS, SSD_STATE)
    dt = jax.nn.softplus(dt_raw.astype(jnp.float32) + dt_bias)
    dt = dt.reshape(bsz, s, SSD_GROUPS, SSD_HEADS_PER_GROUP)
    a = -jnp.exp(a_log.astype(jnp.float32)).reshape(SSD_GROUPS, SSD_HEADS_PER_GROUP)
    d = d_skip.astype(jnp.float32).reshape(SSD_GROUPS, SSD_HEADS_PER_GROUP)
    y = ssd_chunked(xs * dt[..., None], dt * a, bm, cm) + d[..., None] * xs
    y = y.reshape(bsz, s, SSD_INNER)
    yg = (y * jax.nn.silu(z.astype(jnp.float32))).reshape(bsz, s, SSD_GROUPS, SSD_NORM_GROUP)
    yg = yg * lax.rsqrt(jnp.mean(jnp.square(yg), axis=-1, keepdims=True) + RMS_EPS)
    yg = yg.reshape(bsz, s, SSD_INNER) * ssd_norm_w
    y_ssd = yg.astype(h.dtype) @ w_ssd_proj

    merged = gates[:, :, 0] * y_pool + gates[:, :, 1] * y_ssd
    mix = merged.astype(h.dtype) @ w_out
    h = layer_norm(DEEPNORM_ALPHA * h + mix, ln1_g, ln1_b)

    ff = jnp.square(jax.nn.relu(h @ w_up)) @ w_down
    h = layer_norm(DEEPNORM_ALPHA * h + ff, ln2_g, ln2_b)
    return h


def setup_inputs(seed: int = 0) -> dict:
    key = jax.random.key(seed)
    ks = jax.random.split(key, 20)
    nrm = lambda k, shape: jax.random.normal(k, shape, dtype=jnp.float32)
    x = nrm(ks[0], (BATCH, SEQ, D_MODEL))
    w_in = nrm(ks[1], (DEPTH, D_MODEL, IN_PROJ_WIDTH)) * D_MODEL ** -0.5
    b_gates = 0.1 * nrm(ks[2], (DEPTH, N_BRANCHES * D_MODEL))
    conv_w = nrm(ks[3], (DEPTH, SSD_CONV, SSD_CONV_DIM)) * SSD_CONV ** -0.5
    conv_b = 0.02 * nrm(ks[4], (DEPTH, SSD_CONV_DIM))
    dt0 = jnp.exp(jax.random.uniform(ks[5], (DEPTH, SSD_HEADS), dtype=jnp.float32,
                                     minval=np.log(1e-3), maxval=np.log(1e-1)))
    dt_bias = dt0 + jnp.log(-jnp.expm1(-dt0))
    a_log = jnp.log(jax.random.uniform(ks[6], (DEPTH, SSD_HEADS), dtype=jnp.float32,
                                       minval=1.0, maxval=16.0))
    d_skip = 1.0 + 0.1 * nrm(ks[7], (DEPTH, SSD_HEADS))
    ssd_norm_w = 1.0 + 0.02 * nrm(ks[8], (DEPTH, SSD_INNER))
    w_ssd_proj = nrm(ks[9], (DEPTH, SSD_INNER, D_MODEL)) * SSD_INNER ** -0.5
    w_pool_group = nrm(ks[10], (DEPTH, POOL_GROUPS, POOL_GROUP_WIDTH, POOL_GROUP_WIDTH)) * POOL_GROUP_WIDTH ** -0.5
    pool_scale = 1.0 + 0.02 * nrm(ks[11], (DEPTH, POOL_WIDTH))
    w_out = nrm(ks[12], (DEPTH, D_MODEL, D_MODEL)) * (D_MODEL ** -0.5 * DEEPNORM_BETA)
    ln1_g = 1.0 + 0.02 * nrm(ks[13], (DEPTH, D_MODEL))
    ln1_b = 0.02 * nrm(ks[14], (DEPTH, D_MODEL))
    w_up = nrm(ks[15], (DEPTH, D_MODEL, D_FF)) * D_MODEL ** -0.5
    w_down = nrm(ks[16], (DEPTH, D_FF, D_MODEL)) * (D_FF ** -0.5 * DEEPNORM_BETA)
    ln2_g = 1.0 + 0.02 * nrm(ks[17], (DEPTH, D_MODEL))
    ln2_b = 0.02 * nrm(ks[18], (DEPTH, D_MODEL))
    return {"x": x, "w_in": w_in, "b_gates": b_gates, "conv_w": conv_w, "conv_b": conv_b,
            "dt_bias": dt_bias, "a_log": a_log, "d_skip": d_skip, "ssd_norm_w": ssd_norm_w,
            "w_ssd_proj": w_ssd_proj, "w_pool_group": w_pool_group, "pool_scale": pool_scale,
            "w_out": w_out, "ln1_g": ln1_g, "ln1_b": ln1_b, "w_up": w_up, "w_down": w_down,
            "ln2_g": ln2_g, "ln2_b": ln2_b}


def reference(x, w_in, b_gates, conv_w, conv_b, dt_bias, a_log, d_skip, ssd_norm_w,
              w_ssd_proj, w_pool_group, pool_scale, w_out, ln1_g, ln1_b, w_up, w_down,
              ln2_g, ln2_b):
    h = x
    for layer in range(DEPTH):
        h = hybrid_layer(h, w_in[layer], b_gates[layer], conv_w[layer], conv_b[layer],
                         dt_bias[layer], a_log[layer], d_skip[layer], ssd_norm_w[layer],
                         w_ssd_proj[layer], w_pool_group[layer], pool_scale[layer],
                         w_out[layer], ln1_g[layer], ln1_b[layer], w_up[layer],
                         w_down[layer], ln2_g[layer], ln2_b[layer])
    return h
```

```python
import numpy as np
from contextlib import ExitStack
import concourse.bass as bass
import concourse.mybir as mybir
from concourse.bass_utils import run_bass_kernel_spmd

F32 = mybir.dt.float32
BF16 = mybir.dt.bfloat16
AF = mybir.ActivationFunctionType
ALU = mybir.AluOpType

NCORES = 8
SEQ = 2048
D = 1024
T = 256
NCH = SEQ // T
NSEQ = 4
NT_FULL = NSEQ * NCH
ALPHA = float(2.0 ** 0.25)
LN_EPS = 1e-5
RMS_EPS = 1e-5
import os
NSLOT = 5
NM_SLOT = int(os.environ.get('NM_SLOT', '3'))
SEQ_E = bool(int(os.environ.get('SEQ_E', '0')))
TG = os.environ.get('TG', 'p')
HOP_LAT = float(os.environ.get('HOP_LAT', '0.6'))
SAME_LAT = float(os.environ.get('SAME_LAT', '0.1'))
ACT_SCALE = float(os.environ.get('ACT_SCALE', '1.25'))
PRIO_MODE = os.environ.get('PRIO_MODE', 'blevel')
MM_CHUNK = int(os.environ.get('MM_CHUNK', '8'))
DN_CHUNK = int(os.environ.get('DN_CHUNK', '4'))
LIST_SCHED = bool(int(os.environ.get('LIST_SCHED', '1')))

WB = [("U", 0), ("U", 1)]
for _j in range(4):
    WB += [("Z", _j), ("X", 2 * _j), ("X", 2 * _j + 1)]
for _q in range(4):
    WB += [("GG", _q), ("S", _q)]
WB += [("O", 0), ("O", 1)] + [("UP", i) for i in range(8)] + [("DN", i) for i in range(8)]
NB = len(WB)


class Res:
    __slots__ = ("name", "last_w", "readers")

    def __init__(self, name):
        self.name = name
        self.last_w = None
        self.readers = []


class Op:
    __slots__ = ("eng", "fn", "deps", "signal", "sigval", "owner", "inc", "is_dma", "idx", "cost", "users", "nbytes", "tag", "t0", "t1", "prio")


class _CostProbe:
    def __init__(self, eng):
        self.eng = eng
        self.cost = 0.0
        self.nbytes = 0

    def __getattr__(self, name):
        def f(*a, **kw):
            if name == "matmul":
                r = kw["rhs"]
                mult = 4.0 if r.dtype == F32 else 1.0
                self.cost += r.free_size() * mult / 2200.0 + 0.02
            elif name == "transpose":
                self.cost += 0.27 if kw["in_"].dtype == F32 else 0.1
            elif name == "dma_start":
                self.nbytes += kw["out"].nbytes()
            elif name == "nop":
                self.cost += 0.05
            else:
                out = kw.get("out", None)
                if out is None:
                    out = kw.get("ap", a[0] if a else None)
                cols = out.free_size() if out is not None else 256
                if self.eng == "act":
                    self.cost += (0.25 + cols / 1400.0) * ACT_SCALE
                elif self.eng == "dve":
                    self.cost += 0.15 + cols / 800.0
                else:
                    self.cost += 0.3 + cols * 0.0032
            return self
        return f


class Sched:
    ENGS = ("pe", "act", "dve", "pool", "sp")
    PRIO = {"BCD": 0, "E": 0, "F": 1, "G": 1, "I": 2, "J": 2, "PRE": 3}
    DEFAULT_COST = {"pe": 0.15, "act": 0.45, "dve": 0.4, "pool": 0.6, "sp": 0.05}

    def __init__(self, nc, es):
        self.nc = nc
        self.es = es
        self.engs = {k: [] for k in self.ENGS}
        self.sems = {}
        self.all_ops = []
        for k in self.ENGS:
            self.sems[k] = es.enter_context(nc.semaphore("s_" + k))

    def dma_sem(self, name):
        s = self.es.enter_context(self.nc.semaphore("d_" + name))
        self.sems["d_" + name] = s
        return "d_" + name

    def op(self, eng, fn, reads=(), writes=(), dma=None, cost=None, nbytes=0):
        if eng == 'pool' and 'p' in TG:
            eng = 'dve'
        if eng == 'pool!':
            eng = 'pool' if 'q' in TG else 'dve'
        o = Op()
        if cost is None:
            pr = _CostProbe(eng)
            fn(pr)
            cost = max(pr.cost, 0.03)
            nbytes = pr.nbytes
        o.cost = cost
        o.nbytes = nbytes
        o.users = []
        o.tag = getattr(self, 'tag', None)
        o.prio = self.PRIO.get(o.tag[2], 1) if o.tag else 1
        o.t0 = o.t1 = 0.0
        o.eng = eng
        o.fn = fn
        o.is_dma = dma is not None
        o.owner = dma if dma is not None else eng
        o.inc = 16 if dma is not None else 1
        o.signal = dma is not None
        o.sigval = None
        deps = []
        for r in reads:
            if r.last_w is not None:
                deps.append(r.last_w)
        for w in writes:
            if w.last_w is not None:
                deps.append(w.last_w)
            deps.extend(w.readers)
        for r in reads:
            r.readers.append(o)
        for w in writes:
            w.last_w = o
            w.readers = []
        o.deps = [d for d in set(deps) if d is not o]
        o.idx = len(self.all_ops)
        self.all_ops.append(o)
        self.engs[eng].append(o)
        return o

    @staticmethod
    def _needs_wait(o, d):
        if d.is_dma or o.is_dma:
            return True
        if d.eng != o.eng:
            return True
        return o.eng != "pe"

    def finalize(self):
        for o in self.all_ops:
            for d in o.deps:
                if self._needs_wait(o, d):
                    d.signal = True
        cnt = {}
        for eng in self.ENGS:
            for o in self.engs[eng]:
                if o.signal:
                    cnt[o.owner] = cnt.get(o.owner, 0) + o.inc
                    o.sigval = cnt[o.owner]
        self.final_counts = cnt

    def schedule(self):
        ops = self.all_ops
        for o in ops:
            for d in o.deps:
                d.users.append(o)
        unmet = [len(o.deps) for o in ops]
        blevel = [0.0] * len(ops)
        if PRIO_MODE == "blevel":
            for o in reversed(ops):
                b = 0.0
                for u in o.users:
                    lat = SAME_LAT if (u.eng == o.eng and not o.is_dma) else HOP_LAT
                    if blevel[u.idx] + lat > b:
                        b = blevel[u.idx] + lat
                blevel[o.idx] = b + (2.0 + o.nbytes / 280e3 if o.is_dma else o.cost)
            for o in ops:
                o.prio = -blevel[o.idx]
        ready_t = [0.0] * len(ops)
        fin_t = [0.0] * len(ops)
        free = {k: 0.0 for k in self.ENGS}
        ready = {k: [] for k in self.ENGS}
        for o in ops:
            if unmet[o.idx] == 0:
                ready[o.eng].append(o)
        newq = {k: [] for k in self.ENGS}
        dma_free = 0.0
        n_done = 0
        total = len(ops)
        while n_done < total:
            best = None
            for k in self.ENGS:
                if not ready[k]:
                    continue
                f = free[k]
                cand = min(ready[k], key=lambda o: (max(f, ready_t[o.idx]), o.prio, o.idx))
                st = max(f, ready_t[cand.idx])
                if best is None or (st, cand.idx) < (best[0], best[1].idx):
                    best = (st, cand)
            st, o = best
            ready[o.eng].remove(o)
            newq[o.eng].append(o)
            if o.is_dma:
                free[o.eng] = st + 0.06
                t0 = max(st, dma_free)
                dma_free = t0 + o.nbytes / 280e3
                fin = dma_free + 2.0
            else:
                fin = st + o.cost
                free[o.eng] = fin
            fin_t[o.idx] = fin
            o.t0, o.t1 = st, fin
            n_done += 1
            for u in o.users:
                unmet[u.idx] -= 1
                lat = (0.0 if o.eng == 'pe' else SAME_LAT) if (u.eng == o.eng and not o.is_dma) else HOP_LAT
                if fin + lat > ready_t[u.idx]:
                    ready_t[u.idx] = fin + lat
                if unmet[u.idx] == 0:
                    ready[u.eng].append(u)
        self.engs = newq
        self.sim_time = max(fin_t) if fin_t else 0.0

    def check_deadlock(self):
        pos = {k: 0 for k in self.engs}
        val = {}
        progress = True
        while progress:
            progress = False
            for k, ops in self.engs.items():
                while pos[k] < len(ops):
                    o = ops[pos[k]]
                    ok = all((not self._needs_wait(o, d)) or val.get(d.owner, 0) >= d.sigval for d in o.deps)
                    if not ok:
                        break
                    if o.signal:
                        val[o.owner] = val.get(o.owner, 0) + o.inc
                        assert val[o.owner] == o.sigval, (o.owner, val[o.owner], o.sigval)
                    pos[k] += 1
                    progress = True
        stuck = {k: (pos[k], len(ops)) for k, ops in self.engs.items() if pos[k] < len(ops)}
        assert not stuck, "deadlock: %s" % stuck

    def emit(self, block):
        sems = self.sems

        def run(eng_name, e):
            waited = {}
            for o in self.engs[eng_name]:
                need = {}
                for d in o.deps:
                    if not self._needs_wait(o, d):
                        continue
                    if d.sigval > waited.get(d.owner, 0):
                        need[d.owner] = max(need.get(d.owner, 0), d.sigval)
                for owner, v in need.items():
                    e.wait_ge(sems[owner], v)
                    waited[owner] = v
                ins = o.fn(e)
                if o.signal:
                    ins.then_inc(sems[o.owner], o.inc)

        @block.tensor
        def _(e):
            run("pe", e)

        @block.scalar
        def _(e):
            run("act", e)

        @block.vector
        def _(e):
            run("dve", e)

        @block.gpsimd
        def _(e):
            run("pool", e)

        @block.sync
        def _(e):
            run("sp", e)


class Buf:
    __slots__ = ("t", "r", "tb")

    def __init__(self, t, name):
        self.t = t
        self.r = Res(name)
        self.tb = None


def build_nc(NT=NT_FULL, PH=99):
    nc = bass.Bass("TRN2", target_bir_lowering=False)

    def din(name, shape, dt=F32):
        return nc.dram_tensor(name, shape, dt, kind="ExternalInput").ap()

    xin = din("xin", [NT * T, D])
    wall = din("wall", [NB, 128, 4096])
    wdt_h = din("wdt_h", [128, 256])
    wpool_h = din("wpool_h", [128, 2048])
    pmat_h = din("pmat_h", [128, 12 * 128])
    maskc_h = din("maskc_h", [128, 384])
    ident_h = din("ident_h", [128, 128])
    convw_h = din("convw_h", [128, 128])
    convb_h = din("convb_h", [128, 32])
    bg_h = din("bg_h", [128, 16])
    pscale_h = din("pscale_h", [128, 8])
    normw_h = din("normw_h", [128, 16])
    dtb_h = din("dtb_h", [32, 1])
    alog_h = din("alog_h", [32, 1])
    drep_h = din("drep_h", [128, 64])
    ln_h = din("ln_h", [4, 1024])
    out = nc.dram_tensor("out", [NT * T, D], F32, kind="ExternalOutput").ap()
    wsc = nc.dram_tensor("wsc", [NB, 128, 4096], BF16).ap()
    acs_scr = nc.dram_tensor("acs_scr", [NT, 32, 256], F32).ap()

    with ExitStack() as es:
        S = Sched(nc, es)

        def sb(name, shape, dt=F32):
            return Buf(es.enter_context(nc.sbuf_tensor(name, shape, dt)), name)

        ident = sb("ident", [128, 128])
        identb = sb("identb", [128, 128], BF16)
        pmat = sb("pmat", [128, 12 * 128], BF16)
        maskc = sb("maskc", [128, 384])
        convw = sb("convw", [128, 128])
        convb = sb("convb", [128, 32])
        bg = sb("bg", [128, 16])
        pscale = sb("pscale", [128, 8])
        normw = sb("normw", [128, 16])
        dtb = sb("dtb", [32, 1])
        aneg = sb("aneg", [32, 1])
        drep = sb("drep", [128, 64])
        lnrep = sb("lnrep", [128, 4, 1024])
        wdt = sb("wdt", [128, 8, 32], BF16)
        wpool = sb("wpool", [128, 8, 256], BF16)
        ones32 = sb("ones32", [32, 256])
        CONST = Res("consts_ready")

        ring = [sb("wring%d" % i, [128, 4096], BF16) for i in range(NSLOT)]
        NXT = int(os.environ.get("NXT", "3"))
        xt = [sb("xt%d" % i, [128, 2, 1024]) for i in range(NXT)]
        hT = sb("hT", [128, 8, 256], BF16)
        h1T = sb("h1T", [128, 8, 256], BF16)
        h2T = sb("h2T", [128, 8, 256], BF16)
        hbufs = [hT, h1T, h2T]
        u_tm = sb("u_tm", [128, 3, 1024], BF16)
        uslot = [Res("uslot%d" % i) for i in range(3)]
        pooledT = sb("pooledT", [128, 8, 256], BF16)
        zs = [sb("zs%d" % i, [128, 2, 256], BF16) for i in range(3)]
        xr = [sb("xr%d" % i, [128, 2, 259]) for i in range(2)]
        xr_h = [Res("xr_h%d" % h) for h in range(2)]
        xr_c = [Res("xr_c%d" % h) for h in range(2)]
        carry = sb("carry", [128, 8, 4, 3])
        carry_r = [Res("carry%d" % g) for g in range(8)]
        cacc = [sb("cacc0", [128, 4, 256])] * 2
        cacc_r = [[Res("cacc_%d" % c) for c in range(4)]] * 2
        xc = [sb("xc0", [128, 4, 256], BF16)] * 2
        xs_tm = [sb("xs_tm%d" % i, [128, 2, 256], BF16) for i in range(2)]
        xw_tm = [sb("xw_tm%d" % i, [128, 2, 256], BF16) for i in range(2)]
        xd_tm = [sb("xd_tm%d" % i, [128, 2, 256], BF16) for i in range(2)]
        b_tm = [sb("b_tm%d" % i, [128, 2, 128], BF16) for i in range(2)]
        dt_e = sb("dt_e", [32, 256])
        dt_v = sb("dt_v", [32, 256])
        dt_ecs = sb("dt_ecs", [32, 256])
        dt_q = sb("dt_q", [32, 3, 256])
        dgm = sb("dgm", [32, 32])
        tmq2 = [sb("tmq%d" % i, [128, 3, 8, 2, 4]) for i in range(2)]
        cdB2 = [sb("cdB%d" % i, [128, 32]) for i in range(2)]
        acsB = [sb("acsB%d" % i, [128, 2, 256]) for i in range(2)]
        cbm = [sb("cbm%d" % i, [128, 384]) for i in range(2)]
        lt = [sb("lt%d" % i, [128, 384]) for i in range(2)]
        lt_r = [[Res("lt%d_%d" % (i, c)) for c in range(2)] for i in range(2)]
        mt = [sb("mt%d" % i, [128, 384], BF16) for i in range(2)]
        Sst = sb("Sst", [128, 2048])
        Sst_r = [Res("Sst%d" % g) for g in range(8)]
        Sbf_r = [Res("Sbf%d" % g) for g in range(8)]
        Sbf = sb("Sbf", [128, 2048], BF16)
        ytmp = [sb("ytmp%d" % i, [128, 2, 256]) for i in range(2)]
        ybuf = [sb("ybuf0", [128, 2, 256])] * 2
        sqj = sb("sqj", [128, 256], BF16)
        ssq = [sb("ssq%d" % i, [128, 2]) for i in range(2)]
        rstd = [sb("rstd%d" % i, [128, 2]) for i in range(2)]
        ygs = [sb("ygs%d" % i, [128, 2, 256], BF16) for i in range(2)]
        ygs_r = [[Res("ygs%d_%d" % (i, c)) for c in range(2)] for i in range(2)]
        ygT = sb("ygT", [128, 16, 256], BF16)
        ygT_r = [Res("ygT%d" % g) for g in range(8)]
        sg = sb("sg", [128, 2, 256])
        mP = sb("mP", [128, 256])
        mS = sb("mS", [128, 256])
        mergedT = sb("mergedT", [128, 8, 256], BF16)
        aT_f32 = sb("aT", [128, 4096])
        aT = Buf(aT_f32.t[:].bitcast(BF16).rearrange("p (a b) -> p a b", a=32), "aT_bf")
        aT.r = aT_f32.r
        lnst = sb("lnst", [128, 2, 6])
        lnmv = sb("lnmv", [128, 2])
        lnr = sb("lnr", [128, 1])

        banks = []
        for i in range(8):
            b = Buf(es.enter_context(nc.psum_tensor("psb%d" % i, [128, 512], F32)), "psb%d" % i)
            b.tb = b.t[:].bitcast(BF16)
            banks.append(b)
        POOLS = {'b': [0], 'e': [1, 2, 3, 4, 5], 'f': [6, 7]}
        bank_ctr = {k: 0 for k in POOLS}
        bank_pool = ["e"]

        def bank():
            p = bank_pool[0]
            lst = POOLS[p]
            b = banks[lst[bank_ctr[p] % len(lst)]]
            bank_ctr[p] += 1
            return b

        d_ps = [S.dma_sem("ps%d" % i) for i in range(NSLOT)]
        d_c = S.dma_sem("c")
        d_w = [S.dma_sem("w%d" % i) for i in range(NSLOT)]
        d_x = [S.dma_sem("x%d" % i) for i in range(3)]
        d_o = [S.dma_sem("o%d" % i) for i in range(3)]
        d_aw = S.dma_sem("aw")
        d_ab = [S.dma_sem("ab%d" % i) for i in range(2)]

        pre_res = []

        def cload(eng, dst_ap, src_ap, sem, name):
            r = Res(name)
            pre_res.append(r)
            S.op(eng, lambda e: e.dma_start(out=dst_ap, in_=src_ap), writes=[r], dma=sem)

        cload("sp", ident.t[:], ident_h, d_c, "c_ident")
        cload("sp", maskc.t[:], maskc_h, d_c, "c_maskc")
        cload("sp", convw.t[:], convw_h, d_c, "c_convw")
        cload("sp", convb.t[:], convb_h, d_c, "c_convb")
        cload("sp", bg.t[:], bg_h, d_c, "c_bg")
        cload("sp", pscale.t[:], pscale_h, d_c, "c_pscale")
        cload("sp", normw.t[:], normw_h, d_c, "c_normw")
        cload("sp", dtb.t[:], dtb_h, d_c, "c_dtb")
        cload("sp", aneg.t[:], alog_h, d_c, "c_alog")
        cload("sp", drep.t[:], drep_h, d_c, "c_drep")
        for i in range(4):
            cload("sp", lnrep.t[:, i, :], ln_h[i, :].partition_broadcast(128), d_c, "c_ln%d" % i)
        def stage_cast(dst_ap, src_ap, ncols, name):
            stv = xt[0].t[:].rearrange("p a b -> p (a b)")[:, 0:ncols]
            S.op("sp", lambda e: e.dma_start(out=stv, in_=src_ap), writes=[xt[0].r], dma=d_x[0])
            r = Res(name)
            pre_res.append(r)
            S.op("dve", lambda e: e.tensor_copy(out=dst_ap, in_=stv), reads=[xt[0].r], writes=[r])

        stage_cast(wdt.t[:].rearrange("p a b -> p (a b)"), wdt_h, 256, "c_wdt")
        stage_cast(wpool.t[:].rearrange("p a b -> p (a b)"), wpool_h, 2048, "c_wpool")
        stage_cast(pmat.t[:], pmat_h, 1536, "c_pmat")
        wres = [[Res("wsc%d_%d" % (b, hf)) for hf in range(2)] for b in range(NB)]
        S.tag = ('p', 0, 'PRE')
        st_f32 = aT_f32.t[:]
        st_r = [Res("stg%d" % i) for i in range(2)]
        xt2b = xt[2].t[:].rearrange("p a b -> p (a b)").bitcast(BF16)
        cast_dst = [xt2b[:, 0:2048], xt2b[:, 2048:4096], h1T.t[:].rearrange("p a b -> p (a b)")]
        if 'B' in TG:
            cast_dst = [ring[0].t[:, 0:2048], ring[1].t[:, 0:2048], ring[2].t[:, 0:2048]]
            cast_r = [ring[0].r, ring[1].r, ring[2].r]
        cast_r = [Res("cst%d" % i) for i in range(3)]
        if 'B' in TG:
            cast_r = [ring[0].r, ring[1].r, ring[2].r]
        d_st = [S.dma_sem("st%d" % i) for i in range(2)]
        cast_engs = ("act", "dve")
        for b in range(NB):
            for hf in range(2):
                i = 2 * b + hf
                stv = st_f32[:, (i % 2) * 2048:(i % 2 + 1) * 2048]
                S.op("sp", lambda e, stv=stv, b=b, hf=hf: e.dma_start(out=stv, in_=wall[b][:, hf * 2048:(hf + 1) * 2048]),
                     writes=[st_r[i % 2]], dma=d_st[i % 2])
                cs = i % 3
                dstv = cast_dst[cs]
                eng = cast_engs[i % 2]
                if eng == "act":
                    S.op("act", lambda e, dstv=dstv, stv=stv: e.activation(out=dstv, in_=stv, func=AF.Copy),
                         reads=[st_r[i % 2]], writes=[cast_r[cs]])
                else:
                    S.op("dve", lambda e, dstv=dstv, stv=stv: e.tensor_copy(out=dstv, in_=stv), reads=[st_r[i % 2]], writes=[cast_r[cs]])
                S.op("sp", lambda e, dstv=dstv, b=b, hf=hf: e.dma_start(out=wsc[b][:, hf * 2048:(hf + 1) * 2048], in_=dstv),
                     reads=[cast_r[cs]], writes=[wres[b][hf]], dma=d_ps[cs])
        S.op("dve", lambda e: e.nop(nofuse=True), writes=st_r + [aT.r])
        S.op("dve", lambda e: e.nop(nofuse=True), writes=cast_r[0:2] + [xt[2].r])
        S.op("dve", lambda e: e.nop(nofuse=True), writes=cast_r[2:3] + [h1T.r])
        S.tag = None
        S.op("sp", lambda e: e.nop(nofuse=True), reads=pre_res, writes=[CONST])
        S.op("dve", lambda e: e.tensor_copy(out=identb.t[:], in_=ident.t[:]), reads=[CONST], writes=[identb.r])
        S.op("act", lambda e: e.activation(out=aneg.t[:], in_=aneg.t[:], func=AF.Exp), reads=[CONST], writes=[aneg.r])
        S.op("dve", lambda e: e.tensor_scalar(out=aneg.t[:], in0=aneg.t[:], scalar1=-1.0, scalar2=None, op0=ALU.mult),
             reads=[aneg.r], writes=[aneg.r])
        S.op("dve", lambda e: e.memset(ones32.t[:], 1.0), writes=[ones32.r])

        wstate = {"m": 0, "f": 0}

        def w_acquire(kind, idx):
            st = "f" if kind in ("UP", "DN") else "m"
            k = wstate[st]
            wstate[st] = k + 1
            slot = (k % NM_SLOT) if st == "m" else NM_SLOT + (k % (NSLOT - NM_SLOT))
            bidx = WB.index((kind, idx))
            src = wsc[bidx]
            dst = ring[slot].t[:]
            S.op("sp", lambda e: e.dma_start(out=dst, in_=src), reads=wres[bidx], writes=[ring[slot].r], dma=d_w[slot])
            return ring[slot], k

        def w_done(k):
            pass

        def mm(out_ap, pairs, reads, writes):
            pairs = list(pairs)
            n = len(pairs)
            last = None
            for c0 in range(0, n, MM_CHUNK):
                sub = list(enumerate(pairs))[c0:c0 + MM_CHUNK]

                def fn(e, sub=sub):
                    ins = None
                    for i, (l, r) in sub:
                        ins = e.matmul(out=out_ap, lhsT=l, rhs=r, start=(i == 0), stop=(i == n - 1))
                    return ins
                last = S.op("pe", fn, reads=reads, writes=writes)
            return last

        def tr(out_ap, in_ap, idt_ep, reads, writes):
            return S.op("pe", lambda e: e.transpose(out=out_ap, in_=in_ap, identity=idt_ep), reads=reads, writes=writes)

        def x_load(ti):
            xb = xt[ti % NXT]
            src = xin[ti * T:(ti + 1) * T, :].rearrange("(s p) d -> p s d", p=128)
            S.op("sp", lambda e: e.dma_start(out=xb.t[:], in_=src), reads=([aT.r, h1T.r] if (ti == 0 and 'W' in TG) else []),
                 writes=[xb.r], dma=d_x[ti % NXT])

        def transpose_to_hT(xb, hdst):
            for pair in range(4):
                bk = bank()
                for kk in range(2):
                    kc = pair * 2 + kk
                    for s in range(2):
                        tr(bk.t[:, kk * 256 + s * 128: kk * 256 + (s + 1) * 128], xb.t[:, s, kc * 128:(kc + 1) * 128],
                           ident.t[:], [xb.r, CONST], [bk.r])
                dst = hdst.t[:, pair * 2:pair * 2 + 2, :].rearrange("p a b -> p (a b)")
                S.op("act", lambda e, dst=dst, bk=bk: e.activation(out=dst, in_=bk.t[:], func=AF.Copy),
                     reads=[bk.r], writes=[hdst.r])

        def layer_norm(xb, gi):
            for s in range(2):
                row = xb.t[:, s, :]
                for hh in range(2):
                    S.op("dve", lambda e, hh=hh, row=row: e.bn_stats(out=lnst.t[:, hh, :], in_=row[:, hh * 512:(hh + 1) * 512]),
                         reads=[xb.r], writes=[lnst.r])
                S.op("dve", lambda e: e.bn_aggr(out=lnmv.t[:], in_=lnst.t[:].rearrange("p a b -> p (a b)")),
                     reads=[lnst.r], writes=[lnmv.r])
                S.op("act", lambda e: e.activation(out=lnr.t[:], in_=lnmv.t[:, 1:2], func=AF.Ln, bias=LN_EPS),
                     reads=[lnmv.r], writes=[lnr.r])
                S.op("act", lambda e: e.activation(out=lnr.t[:], in_=lnr.t[:], func=AF.Exp, scale=-0.5),
                     reads=[lnr.r], writes=[lnr.r])
                S.op("dve", lambda e: e.tensor_scalar(out=lnmv.t[:, 0:1], in0=lnmv.t[:, 0:1], scalar1=lnr.t[:, 0:1], scalar2=-1.0,
                                                      op0=ALU.mult, op1=ALU.mult), reads=[lnmv.r, lnr.r], writes=[lnmv.r])
                S.op("act", lambda e, row=row: e.activation(out=row, in_=row, func=AF.Identity, scale=lnr.t[:, 0:1], bias=lnmv.t[:, 0:1]),
                     reads=[xb.r, lnmv.r, lnr.r], writes=[xb.r])
                S.op("pool!", lambda e, row=row: e.tensor_tensor(out=row, in0=row, in1=lnrep.t[:, gi, :], op=ALU.mult),
                     reads=[xb.r, CONST], writes=[xb.r])
                S.op("pool!", lambda e, row=row: e.tensor_tensor(out=row, in0=row, in1=lnrep.t[:, gi + 1, :], op=ALU.add),
                     reads=[xb.r, CONST], writes=[xb.r])

        def finish(ti, xb):
            dst = out[ti * T:(ti + 1) * T, :].rearrange("(s p) d -> p s d", p=128)
            S.op("sp", lambda e: e.dma_start(out=dst, in_=xb.t[:]), reads=[xb.r], dma=d_o[ti % 2])
            if ti + 1 < NT and PH < 6:
                x_load(ti + 1)

        def mixer_gen(ti):
            seq, ch = divmod(ti, NCH)
            first = ch == 0
            xb = xt[ti % NXT]
            S.tag = ('m', ti, 'BCD')
            bank_pool[0] = 'b'
            tmq = tmq2[ti % 2]
            cdB = cdB2[ti % 2]
            hT = hbufs[ti % 3]
            h1T = hT
            x_load(ti)
            if first:
                S.op("pool", lambda e: e.memset(Sst.t[:], 0.0), writes=Sst_r)
                S.op("pool", lambda e: e.memset(Sbf.t[:], 0.0), writes=Sbf_r)
                S.op("pool", lambda e: e.memset(carry.t[:], 0.0), writes=carry_r)

            transpose_to_hT(xb, hT)

            bk = bank()
            mm(bk.t[0:32, 0:256], [(wdt.t[:, kc, :], hT.t[:, kc, :]) for kc in range(8)], [CONST, hT.r], [bk.r])
            S.op("act", lambda e, bk=bk: e.activation(out=dt_e.t[:], in_=bk.t[0:32, 0:256], func=AF.Exp, bias=dtb.t[:, 0:1]),
                 reads=[bk.r, CONST], writes=[dt_e.r])
            S.op("act", lambda e: e.activation(out=dt_v.t[:], in_=dt_e.t[:], func=AF.Ln, bias=1.0),
                 reads=[dt_e.r], writes=[dt_v.r])
            S.op("dve", lambda e: e.tensor_scalar(out=dt_e.t[:], in0=dt_v.t[:], scalar1=aneg.t[:, 0:1], scalar2=None, op0=ALU.mult),
                 reads=[dt_v.r, aneg.r], writes=[dt_e.r])
            S.op("dve", lambda e: e.tensor_tensor_scan(out=dt_ecs.t[:], data0=ones32.t[:], data1=dt_e.t[:], initial=0.0,
                                                       op0=ALU.mult, op1=ALU.add),
                 reads=[ones32.r, dt_e.r], writes=[dt_ecs.r])
            scr = acs_scr[ti]
            scr_r = Res("acs_scr%d" % ti)
            S.op("sp", lambda e, scr=scr: e.dma_start(out=scr, in_=dt_ecs.t[:]), reads=[dt_ecs.r], writes=[scr_r], dma=d_aw)

            def bcast_load(gh, scr=scr, scr_r=scr_r):
                ab = acsB[gh % 2]
                src = scr[2 * gh:2 * gh + 2, :].rearrange("h t -> (h t)").partition_broadcast(128)
                S.op("sp", lambda e: e.dma_start(out=ab.t[:].rearrange("p h t -> p (h t)"), in_=src),
                     reads=[scr_r], writes=[ab.r], dma=d_ab[gh % 2])

            S.op("act", lambda e: e.activation(out=dt_e.t[:], in_=dt_v.t[:], func=AF.Ln), reads=[dt_v.r], writes=[dt_e.r])
            S.op("dve", lambda e: e.tensor_tensor(out=dt_q.t[:, 0, :], in0=dt_e.t[:], in1=dt_ecs.t[:], op=ALU.subtract),
                 reads=[dt_e.r, dt_ecs.r], writes=[dt_q.r])
            S.op("act", lambda e: e.activation(out=dt_q.t[:, 1, :], in_=dt_ecs.t[:], func=AF.Exp), reads=[dt_ecs.r], writes=[dt_q.r])
            S.op("act", lambda e: e.activation(out=dt_e.t[:], in_=dt_ecs.t[:], func=AF.Exp, bias=dt_ecs.t[:, 255:256], scale=-1.0),
                 reads=[dt_ecs.r, dt_e.r], writes=[dt_e.r])
            S.op("dve", lambda e: e.tensor_tensor(out=dt_q.t[:, 2, :], in0=dt_e.t[:], in1=dt_v.t[:], op=ALU.mult),
                 reads=[dt_e.r, dt_v.r], writes=[dt_q.r])
            bk = bank()
            for s in range(2):
                for q in range(3):
                    tr(bk.t[:, (s * 3 + q) * 32:(s * 3 + q + 1) * 32], dt_q.t[:, q, s * 128:(s + 1) * 128], ident.t[0:32, 0:32],
                       [dt_q.r, CONST], [bk.r])
            for s in range(2):
                for q in range(3):
                    S.op("dve", lambda e, bk=bk, s=s, q=q: e.tensor_copy(
                        out=tmq.t[:, q, :, s, :], in_=bk.t[:, (s * 3 + q) * 32:(s * 3 + q + 1) * 32].rearrange("p (g h) -> p g h", g=8)),
                         reads=[bk.r], writes=[tmq.r])
            S.op("dve", lambda e: e.tensor_scalar(out=dgm.t[:], in0=ident.t[0:32, 0:32], scalar1=dt_q.t[:, 1, 255:256], scalar2=None,
                                                  op0=ALU.mult), reads=[CONST, dt_q.r], writes=[dgm.r])
            bk = bank()
            mm(bk.t[:, 0:32], [(ones32.t[:, 0:128], dgm.t[:])], [ones32.r, dgm.r], [bk.r])
            S.op("dve", lambda e, bk=bk: e.tensor_copy(out=cdB.t[:], in_=bk.t[:, 0:32]), reads=[bk.r], writes=[cdB.r])

            yield
            zinfo = {}

            def stage_a(g):
                i2 = g % 2
                if g % 2 == 0:
                    zinfo["z"] = w_acquire("Z", g // 2)
                zblk, zk = zinfo["z"]
                zv = zblk.t[:].rearrange("p (k n) -> p k n", k=8)
                bkz = bank()
                for s in range(2):
                    mm(bkz.t[:, s * 256:(s + 1) * 256],
                       [(hT.t[:, kc, s * 128:(s + 1) * 128], zv[:, kc, (g % 2) * 256:(g % 2 + 1) * 256]) for kc in range(8)],
                       [hT.r, zblk.r], [bkz.r])
                zsb = zs[g % 3]
                S.op("act", lambda e: e.activation(out=zsb.t[:].rearrange("p a b -> p (a b)"), in_=bkz.t[:], func=AF.Silu),
                     reads=[bkz.r], writes=[zsb.r])
                xblk, xk = w_acquire("X", g)
                xv = xblk.t[:].rearrange("p (k n) -> p k n", k=8)
                for half in range(2):
                    xrb = xr[half]
                    S.op("dve", lambda e, xrb=xrb, half=half: e.tensor_copy(out=xrb.t[:, :, 0:3], in_=carry.t[:, g, 2 * half:2 * half + 2, :]),
                         reads=[carry_r[g]], writes=[xr_c[half]])
                    bkx = bank()
                    for kk in range(2):
                        ct = half * 2 + kk
                        mm(bkx.t[:, kk * 256:(kk + 1) * 256],
                           [(xv[:, kc, ct * 128:(ct + 1) * 128], hT.t[:, kc, :]) for kc in range(8)], [hT.r, xblk.r], [bkx.r])
                    S.op("act", lambda e, xrb=xrb, bkx=bkx: e.activation(
                        out=xrb.t[:, :, 3:259], in_=bkx.t[:].rearrange("p (a b) -> p a b", a=2), func=AF.Copy),
                         reads=[bkx.r], writes=[xr_h[half]])
                w_done(xk)
                if g % 2 == 1:
                    w_done(zk)
                for half in range(2):
                    xrb = xr[half]
                    S.op("dve", lambda e, xrb=xrb, half=half: e.tensor_copy(out=carry.t[:, g, 2 * half:2 * half + 2, :], in_=xrb.t[:, :, 256:259]),
                         reads=[xr_h[half]], writes=[carry_r[g]])
                cab = cacc[i2]

                def cch_of(ct):
                    return (g * 2 + ct) if ct < 2 else (16 + g if ct == 2 else 24 + g)
                for ct in range(4):
                    cch = cch_of(ct)
                    xrb = xr[ct // 2]
                    S.op("act", lambda e, ct=ct, cch=cch, xrb=xrb: e.activation(
                        out=cab.t[:, ct, :], in_=xrb.t[:, ct % 2, 3:259], func=AF.Identity,
                        bias=convb.t[:, cch:cch + 1], scale=convw.t[:, cch * 4 + 3:cch * 4 + 4]),
                         reads=[xr_h[ct // 2], CONST], writes=[cacc_r[i2][ct]])
                for k in range(3):
                    for ct in range(4):
                        cch = cch_of(ct)
                        xrb = xr[ct // 2]
                        S.op("dve", lambda e, ct=ct, cch=cch, k=k, xrb=xrb: e.scalar_tensor_tensor(
                            out=cab.t[:, ct, :], in0=xrb.t[:, ct % 2, k:k + 256], scalar=convw.t[:, cch * 4 + k:cch * 4 + k + 1],
                            in1=cab.t[:, ct, :], op0=ALU.mult, op1=ALU.add),
                             reads=[xr_h[ct // 2], xr_c[ct // 2], CONST, cacc_r[i2][ct]], writes=[cacc_r[i2][ct]])
                xcb = xc[i2]
                S.op("act", lambda e: e.activation(out=xcb.t[:].rearrange("p a b -> p (a b)"),
                                                   in_=cab.t[:].rearrange("p a b -> p (a b)"), func=AF.Silu),
                     reads=cacc_r[i2], writes=[xcb.r])

            def stage_b(g):
                i2 = g % 2
                xcb = xc[i2]
                xsb, xwb, xdb, btb = xs_tm[i2], xw_tm[i2], xd_tm[i2], b_tm[i2]
                bkt = bank()
                for s in range(2):
                    for ct in range(2):
                        tr(bkt.tb[:, s * 256 + ct * 128: s * 256 + (ct + 1) * 128], xcb.t[:, ct, s * 128:(s + 1) * 128], identb.t[:],
                           [xcb.r, identb.r], [bkt.r])
                    tr(bkt.tb[:, 512 + s * 128: 512 + (s + 1) * 128], xcb.t[:, 2, s * 128:(s + 1) * 128], identb.t[:],
                       [xcb.r, identb.r], [bkt.r])
                S.op("act", lambda e: e.activation(out=xsb.t[:].rearrange("p a b -> p (a b)"), in_=bkt.tb[:, 0:512], func=AF.Copy),
                     reads=[bkt.r], writes=[xsb.r])
                if True:
                    S.op("act", lambda e: e.activation(out=btb.t[:].rearrange("p a b -> p (a b)"), in_=bkt.tb[:, 512:768], func=AF.Copy),
                         reads=[bkt.r], writes=[btb.r])
                else:
                    S.op("dve", lambda e: e.tensor_copy(out=btb.t[:].rearrange("p a b -> p (a b)"), in_=bkt.tb[:, 512:768]),
                         reads=[bkt.r], writes=[btb.r])
                S.op("pool", lambda e: e.tensor_tensor(
                    out=xwb.t[:].rearrange("p s (h d) -> p (s h) d", h=4),
                    in0=xsb.t[:].rearrange("p s (h d) -> p (s h) d", h=4),
                    in1=tmq.t[:, 2, g, :, :].rearrange("p s h -> p (s h)").unsqueeze(2).to_broadcast([128, 8, 64]), op=ALU.mult),
                     reads=[xsb.r, tmq.r], writes=[xwb.r])
                S.op("pool", lambda e: e.tensor_tensor(
                    out=xdb.t[:].rearrange("p s (h d) -> p (s h) d", h=4),
                    in0=xsb.t[:].rearrange("p s (h d) -> p (s h) d", h=4),
                    in1=drep.t[:, g * 8:(g + 1) * 8].unsqueeze(2).to_broadcast([128, 8, 64]), op=ALU.mult),
                     reads=[xsb.r, CONST], writes=[xdb.r])
                bkc = bank()
                mm(bkc.t[:, 0:256], [(xcb.t[:, 2, 0:128], xcb.t[:, 3, 0:256])], [xcb.r], [bkc.r])
                mm(bkc.t[:, 256:384], [(xcb.t[:, 2, 128:256], xcb.t[:, 3, 128:256])], [xcb.r], [bkc.r])
                cb = cbm[i2]
                S.op("dve", lambda e: e.tensor_tensor(out=cb.t[:], in0=bkc.t[:, 0:384], in1=maskc.t[:], op=ALU.mult),
                     reads=[bkc.r, CONST], writes=[cb.r])
                bko = bank()
                for ls in range(2):
                    mm(bko.t[:, ls * 256:(ls + 1) * 256], [(xcb.t[:, 3, ls * 128:(ls + 1) * 128], Sbf.t[:, g * 256:(g + 1) * 256])],
                       [xcb.r, Sbf_r[g]], [bko.r])
                yt = ytmp[g % 2]
                S.op("dve", lambda e: e.tensor_tensor(
                    out=yt.t[:].rearrange("p s (h d) -> p (s h) d", h=4),
                    in0=bko.t[:].rearrange("p (a d) -> p a d", a=8),
                    in1=tmq.t[:, 1, g, :, :].rearrange("p s h -> p (s h)").unsqueeze(2).to_broadcast([128, 8, 64]), op=ALU.mult),
                     reads=[bko.r, tmq.r], writes=[yt.r])
                bks = bank()
                mm(bks.t[:, 0:256], [(btb.t[:, sc, :], xwb.t[:, sc, :]) for sc in range(2)], [btb.r, xwb.r], [bks.r])
                Sg = Sst.t[:, g * 256:(g + 1) * 256]
                S.op("dve", lambda e: e.tensor_tensor(
                    out=Sg.rearrange("p (h d) -> p h d", h=4), in0=Sg.rearrange("p (h d) -> p h d", h=4),
                    in1=cdB.t[:, 4 * g:4 * g + 4].unsqueeze(2).to_broadcast([128, 4, 64]), op=ALU.mult),
                     reads=[Sst_r[g], cdB.r], writes=[Sst_r[g]])
                S.op("dve", lambda e: e.tensor_tensor(out=Sg, in0=Sg, in1=bks.t[:, 0:256], op=ALU.add),
                     reads=[Sst_r[g], bks.r], writes=[Sst_r[g]])
                S.op("act", lambda e: e.activation(out=Sbf.t[:, g * 256:(g + 1) * 256], in_=Sg, func=AF.Copy),
                     reads=[Sst_r[g]], writes=[Sbf_r[g]])

            ydiag_banks = {}

            def stage_c(g):
                i2 = g % 2
                xsb, xdb = xs_tm[i2], xd_tm[i2]
                cb = cbm[i2]
                bcast_load(2 * g)
                bcast_load(2 * g + 1)
                bkd = bank()
                ydiag_banks[g] = bkd
                for h in range(4):
                    L = lt[h % 2]
                    M = mt[h % 2]
                    ab = acsB[h // 2]
                    S.op("act", lambda e, L=L, h=h, ab=ab: e.activation(
                        out=L.t[:, 0:256], in_=ab.t[:, h % 2, :], func=AF.Exp, bias=tmq.t[:, 0, g, 0, h:h + 1]),
                         reads=[ab.r, tmq.r], writes=[lt_r[h % 2][0]])
                    S.op("act", lambda e, L=L, h=h, ab=ab: e.activation(
                        out=L.t[:, 256:384], in_=ab.t[:, h % 2, 128:256], func=AF.Exp, bias=tmq.t[:, 0, g, 1, h:h + 1]),
                         reads=[ab.r, tmq.r], writes=[lt_r[h % 2][1]])
                    S.op("dve", lambda e, L=L, M=M: e.scalar_tensor_tensor(out=M.t[:], in0=L.t[:], scalar=1e30, in1=cb.t[:],
                                                                          op0=ALU.min, op1=ALU.mult),
                         reads=[lt_r[h % 2][0], lt_r[h % 2][1], cb.r], writes=[M.r])
                    hs = slice(h * 64, (h + 1) * 64)
                    mm(bkd.t[:, h * 64:(h + 1) * 64],
                       [(M.t[:, 0:128], xsb.t[:, 0, hs]), (identb.t[:], xdb.t[:, 0, hs])],
                       [M.r, xsb.r, xdb.r, identb.r], [bkd.r])
                    mm(bkd.t[:, 256 + h * 64:256 + (h + 1) * 64],
                       [(M.t[:, 128:256], xsb.t[:, 0, hs]), (M.t[:, 256:384], xsb.t[:, 1, hs]), (identb.t[:], xdb.t[:, 1, hs])],
                       [M.r, xsb.r, xdb.r, identb.r], [bkd.r])

            def stage_d(g):
                i2 = g % 2
                bkd = ydiag_banks.pop(g)
                yt = ytmp[g % 2]
                yb = ybuf[i2]
                zsb = zs[g % 3]
                S.op("dve", lambda e: e.tensor_tensor(out=yb.t[:].rearrange("p a b -> p (a b)"),
                                                      in0=yt.t[:].rearrange("p a b -> p (a b)"), in1=bkd.t[:], op=ALU.add),
                     reads=[yt.r, bkd.r], writes=[yb.r])
                S.op("pool", lambda e: e.tensor_tensor(out=yb.t[:], in0=yb.t[:], in1=zsb.t[:], op=ALU.mult),
                     reads=[yb.r, zsb.r], writes=[yb.r])
                sq = ssq[i2]
                rs = rstd[i2]
                for ls in range(2):
                    S.op("act", lambda e, ls=ls: e.activation(out=sqj.t[:], in_=yb.t[:, ls, :], func=AF.Square,
                                                              accum_out=sq.t[:, ls:ls + 1]),
                         reads=[yb.r], writes=[sqj.r, sq.r])
                S.op("act", lambda e: e.activation(out=rs.t[:], in_=sq.t[:], func=AF.Ln, scale=1.0 / 256.0, bias=RMS_EPS),
                     reads=[sq.r], writes=[rs.r])
                S.op("act", lambda e: e.activation(out=rs.t[:], in_=rs.t[:], func=AF.Exp, scale=-0.5),
                     reads=[rs.r], writes=[rs.r])
                yg = ygs[i2]
                for ls in range(2):
                    S.op("act", lambda e, ls=ls: e.activation(out=yg.t[:, ls, :], in_=yb.t[:, ls, :], func=AF.Identity,
                                                              scale=rs.t[:, ls:ls + 1]),
                         reads=[yb.r, rs.r], writes=[ygs_r[i2][ls]])
                bky = bank()
                for fc in range(2):
                    for ls in range(2):
                        tr(bky.tb[:, fc * 256 + ls * 128: fc * 256 + (ls + 1) * 128], yg.t[:, ls, fc * 128:(fc + 1) * 128], identb.t[:],
                           [ygs_r[i2][ls], identb.r], [bky.r])
                for fc in range(2):
                    S.op("act", lambda e, fc=fc: e.activation(
                        out=ygT.t[:, 2 * g + fc, :], in_=bky.tb[:, fc * 256:(fc + 1) * 256], func=AF.Identity,
                        scale=normw.t[:, 2 * g + fc:2 * g + fc + 1]),
                         reads=[bky.r, CONST], writes=[ygT_r[g]])

            for g in range(8):
                S.tag = ('m', ti, 'E', g, 'a')
                bank_pool[0] = 'e'
                stage_a(g)
                S.tag = ('m', ti, 'E', g, 'b')
                stage_b(g)
                S.tag = ('m', ti, 'E', g, 'c')
                stage_c(g)
                S.tag = ('m', ti, 'E', g, 'd')
                stage_d(g)
                yield

            S.tag = ('m', ti, 'F')
            bank_pool[0] = 'f'
            us = [(2 * ti) % 3, (2 * ti + 1) % 3, (2 * ti + 2) % 3]
            for ub in range(2):
                wb, wk = w_acquire("U", ub)
                wv = wb.t[:].rearrange("p (k n) -> p k n", k=8)
                for s in range(2):
                    bk = bank()
                    mm(bk.t[:], [(hT.t[:, kc, s * 128:(s + 1) * 128], wv[:, kc, :]) for kc in range(8)], [hT.r, wb.r], [bk.r])
                    dst = u_tm.t[:, us[1 + s], ub * 512:(ub + 1) * 512]
                    S.op("act", lambda e, dst=dst, bk=bk: e.activation(out=dst, in_=bk.t[:], func=AF.Copy),
                         reads=[bk.r], writes=[uslot[us[1 + s]]])
                w_done(wk)
            for pair in range(4):
                bk = bank()
                for kk in range(2):
                    cc = pair * 2 + kk
                    wi = cc // 2
                    for s in range(2):
                        o_ap = bk.t[:, kk * 256 + s * 128: kk * 256 + (s + 1) * 128]
                        pairs = []
                        rds = [CONST]
                        src_prev = us[s]
                        src_cur = us[1 + s]
                        if not (first and s == 0):
                            pairs.append((u_tm.t[:, src_prev, cc * 128:(cc + 1) * 128], pmat.t[:, (8 + wi) * 128:(9 + wi) * 128]))
                            rds.append(uslot[src_prev])
                            dk = 0
                        else:
                            dk = 4
                        pairs.append((u_tm.t[:, src_cur, cc * 128:(cc + 1) * 128], pmat.t[:, (dk + wi) * 128:(dk + wi + 1) * 128]))
                        rds.append(uslot[src_cur])
                        mm(o_ap, pairs, rds, [bk.r])
                dst = pooledT.t[:, pair * 2:pair * 2 + 2, :].rearrange("p a b -> p (a b)")
                S.op("act", lambda e, dst=dst, bk=bk: e.activation(out=dst, in_=bk.t[:], func=AF.Copy),
                     reads=[bk.r], writes=[pooledT.r])

            yield
            for q in range(4):
                S.tag = ('m', ti, 'F')
                bank_pool[0] = 'f'
                gblk, gk = w_acquire("GG", q)
                gv = gblk.t[:].rearrange("p (k n) -> p k n", k=8)
                sblk, sk = w_acquire("S", q)
                sv = sblk.t[:].rearrange("p (k n) -> p k n", k=16)
                for jj in range(2):
                    j = 2 * q + jj
                    bkg = bank()
                    mm(bkg.t[:, 0:256], [(gv[:, kc, jj * 128:(jj + 1) * 128], hT.t[:, kc, :]) for kc in range(8)], [gblk.r, hT.r], [bkg.r])
                    mm(bkg.t[:, 256:512], [(gv[:, kc, 256 + jj * 128:256 + (jj + 1) * 128], hT.t[:, kc, :]) for kc in range(8)],
                       [gblk.r, hT.r], [bkg.r])
                    bkm = bank()
                    gi, dd = divmod(j, 2)
                    mm(bkm.t[:, 0:256], [(wpool.t[:, gi * 2 + c2, dd * 128:(dd + 1) * 128], pooledT.t[:, gi * 2 + c2, :]) for c2 in range(2)],
                       [CONST, pooledT.r], [bkm.r])
                    mm(bkm.t[:, 256:512], [(sv[:, fc, jj * 128:(jj + 1) * 128], ygT.t[:, fc, :]) for fc in range(16)],
                       [sblk.r] + ygT_r, [bkm.r])
                    S.op("act", lambda e, bkg=bkg, j=j: e.activation(out=sg.t[:, 0, :], in_=bkg.t[:, 0:256], func=AF.Sigmoid,
                                                                     bias=bg.t[:, j:j + 1]), reads=[bkg.r, CONST], writes=[sg.r])
                    S.op("act", lambda e, bkg=bkg, j=j: e.activation(out=sg.t[:, 1, :], in_=bkg.t[:, 256:512], func=AF.Sigmoid,
                                                                     bias=bg.t[:, 8 + j:9 + j]), reads=[bkg.r, CONST], writes=[sg.r])
                    S.op("dve", lambda e, bkm=bkm, j=j: e.scalar_tensor_tensor(out=mP.t[:], in0=bkm.t[:, 0:256], scalar=pscale.t[:, j:j + 1],
                                                                               in1=sg.t[:, 0, :], op0=ALU.mult, op1=ALU.mult),
                         reads=[bkm.r, sg.r, CONST], writes=[mP.r])
                    S.op("dve", lambda e, bkm=bkm: e.tensor_tensor(out=mS.t[:], in0=bkm.t[:, 256:512], in1=sg.t[:, 1, :], op=ALU.mult),
                         reads=[bkm.r, sg.r], writes=[mS.r])
                    S.op("pool", lambda e, j=j: e.tensor_tensor(out=mergedT.t[:, j, :], in0=mP.t[:], in1=mS.t[:], op=ALU.add),
                         reads=[mP.r, mS.r], writes=[mergedT.r])
                yield

            S.tag = ('m', ti, 'G')
            bank_pool[0] = 'f'
            for half in range(2):
                oblk, ok_ = w_acquire("O", half)
                ov = oblk.t[:].rearrange("p (k n) -> p k n", k=8)
                for s in range(2):
                    bk = bank()
                    mm(bk.t[:], [(mergedT.t[:, dc, s * 128:(s + 1) * 128], ov[:, dc, :]) for dc in range(8)], [mergedT.r, oblk.r], [bk.r])
                    seg = xb.t[:, s, half * 512:(half + 1) * 512]
                    S.op("dve", lambda e, seg=seg, bk=bk: e.scalar_tensor_tensor(out=seg, in0=seg, scalar=ALPHA, in1=bk.t[:],
                                                                                 op0=ALU.mult, op1=ALU.add),
                         reads=[xb.r, bk.r], writes=[xb.r])
                w_done(ok_)
            layer_norm(xb, 0)
            transpose_to_hT(xb, h1T)
            yield

        def ffn_gen(ti):
            xb = xt[ti % NXT]
            h1T = hbufs[ti % 3]

            for ub in range(8):
                S.tag = ('f', ti, 'I')
                bank_pool[0] = 'f'
                ublk, uk = w_acquire("UP", ub)
                uv = ublk.t[:].rearrange("p (k n) -> p k n", k=8)
                for pr in range(2):
                    bk = bank()
                    for kk in range(2):
                        fcl = pr * 2 + kk
                        mm(bk.t[:, kk * 256:(kk + 1) * 256], [(uv[:, kc, fcl * 128:(fcl + 1) * 128], h1T.t[:, kc, :]) for kc in range(8)],
                           [ublk.r, h1T.r], [bk.r])
                    fc0 = ub * 4 + pr * 2
                    rt = sg.t[:].rearrange("p a b -> p (a b)")
                    S.op("act", lambda e, bk=bk, rt=rt: e.activation(out=rt, in_=bk.t[:], func=AF.Relu), reads=[bk.r], writes=[sg.r])
                    S.op("dve", lambda e, fc0=fc0, rt=rt: e.tensor_tensor(out=aT.t[:, fc0:fc0 + 2, :].rearrange("p a b -> p (a b)"),
                                                                        in0=rt, in1=rt, op=ALU.mult),
                         reads=[sg.r], writes=[aT.r])
                yield

            for half in range(2):
                S.tag = ('f', ti, 'J')
                bank_pool[0] = 'f'
                accs = [bank() for _ in range(2)]
                for j in range(4):
                    dblk, dk_ = w_acquire("DN", half * 4 + j)
                    dv = dblk.t[:].rearrange("p (k n) -> p k n", k=8)
                    for s in range(2):
                        ab_ = accs[s]

                        for f0 in range(0, 8, DN_CHUNK):
                            def fn(e, ab_=ab_, j=j, s=s, dv=dv, f0=f0):
                                ins = None
                                for f in range(f0, f0 + DN_CHUNK):
                                    ins = e.matmul(out=ab_.t[:], lhsT=aT.t[:, j * 8 + f, s * 128:(s + 1) * 128], rhs=dv[:, f, :],
                                                   start=(j == 0 and f == 0), stop=(j == 3 and f == 7))
                                return ins
                            S.op("pe", fn, reads=[aT.r, dblk.r], writes=[ab_.r])
                for s in range(2):
                    seg = xb.t[:, s, half * 512:(half + 1) * 512]
                    ab_ = accs[s]
                    S.op("dve", lambda e, seg=seg, ab_=ab_: e.scalar_tensor_tensor(out=seg, in0=seg, scalar=ALPHA, in1=ab_.t[:],
                                                                                   op0=ALU.mult, op1=ALU.add),
                         reads=[xb.r, ab_.r], writes=[xb.r])
                yield
            layer_norm(xb, 2)
            dst = out[ti * T:(ti + 1) * T, :].rearrange("(s p) d -> p s d", p=128)
            S.op("sp", lambda e, dst=dst, xb=xb: e.dma_start(out=dst, in_=xb.t[:]), reads=[xb.r], dma=d_o[ti % NXT])

            yield

        for it in range(NT + 1):
            gens = []
            if it < NT:
                gens.append(mixer_gen(it))
            if it >= 1:
                gens.append(ffn_gen(it - 1))
            while gens:
                for gnr in list(gens):
                    try:
                        next(gnr)
                    except StopIteration:
                        gens.remove(gnr)

        if os.environ.get("DBGW"):
            dbg_w = nc.dram_tensor("dbg_w", [NB, 128, 4096], BF16, kind="ExternalOutput").ap()
            d_dbg = S.dma_sem("dbg")
            S.op("sp", lambda e: e.dma_start(out=dbg_w, in_=wsc), reads=[r for pair in wres for r in pair], dma=d_dbg)
        if LIST_SCHED:
            S.schedule()
        S.finalize()
        S.check_deadlock()
        with nc.Block() as block:
            S.emit(block)
        with nc.Block() as block2:
            @block2.sync
            def _(e):
                if os.environ.get("DBGW"):
                    e.wait_ge(S.sems[d_dbg], S.final_counts[d_dbg])
                for i in range(3):
                    if d_o[i] in S.final_counts:
                        e.wait_ge(S.sems[d_o[i]], S.final_counts[d_o[i]])
    return nc


def _pool_mats():
    pm = np.zeros((12, 128, 128), np.float32)
    tt = np.arange(128)
    for wi, w in enumerate((2, 4, 8, 16)):
        d = tt[None, :] - tt[:, None]
        inwin = (d >= 0) & (d < w)
        pm[wi] = inwin * (1.0 / w) - np.eye(128)
        cnt = np.minimum(tt + 1, w).astype(np.float32)
        pm[4 + wi] = inwin * (1.0 / cnt)[None, :] - np.eye(128)
        dp = tt[None, :] + 128 - tt[:, None]
        pm[8 + wi] = ((dp >= 0) & (dp < w)) * (1.0 / w)
    return np.ascontiguousarray(pm.transpose(1, 0, 2).reshape(128, 12 * 128))


def _mask_c():
    tt = np.arange(128)
    tri = (tt[None, :] >= tt[:, None]).astype(np.float32)
    return np.ascontiguousarray(np.concatenate([tri, np.ones((128, 128), np.float32), tri], axis=1))


def _kblock(w, cols):
    sub = w[:, cols]
    kcs = sub.shape[0] // 128
    return np.ascontiguousarray(sub.reshape(kcs, 128, -1).transpose(1, 0, 2).reshape(128, -1))


def _prep_shared(inp):
    w_in = inp["w_in"][0]
    ar = np.arange
    blocks = []
    for kind, i in WB:
        if kind == "U":
            blk = _kblock(w_in, 512 * i + ar(512))
        elif kind == "Z":
            blk = _kblock(w_in, 1024 + 512 * i + ar(512))
        elif kind == "X":
            cols = np.concatenate([3072 + 256 * i + ar(256), 5120 + 128 * i + ar(128), 6144 + 128 * i + ar(128)])
            blk = _kblock(w_in, cols)
        elif kind == "GG":
            cols = np.concatenate([7200 + 256 * i + ar(256), 8224 + 256 * i + ar(256)])
            blk = _kblock(w_in, cols)
        elif kind == "S":
            blk = _kblock(inp["w_ssd_proj"][0], 256 * i + ar(256))
        elif kind == "O":
            blk = _kblock(inp["w_out"][0], 512 * i + ar(512))
        elif kind == "UP":
            blk = _kblock(inp["w_up"][0], 512 * i + ar(512))
        elif kind == "DN":
            half, j = divmod(i, 4)
            blk = _kblock(inp["w_down"][0][1024 * j:1024 * (j + 1), :], 512 * half + ar(512))
        assert blk.shape == (128, 4096), (kind, blk.shape)
        blocks.append(blk)
    wall = np.ascontiguousarray(np.stack(blocks, 0), dtype=np.float32)
    wdt_h = _kblock(w_in, 7168 + ar(32))
    wp = inp["w_pool_group"][0]
    wpool_h = np.ascontiguousarray(wp.reshape(4, 2, 128, 256).transpose(2, 0, 1, 3).reshape(128, 2048))
    cw = inp["conv_w"][0]
    convw_h = np.ascontiguousarray(cw.reshape(4, 32, 128).transpose(2, 1, 0).reshape(128, 128))
    convb_h = np.ascontiguousarray(inp["conv_b"][0].reshape(32, 128).T)
    bg_h = np.ascontiguousarray(inp["b_gates"][0].reshape(16, 128).T)
    pscale_h = np.ascontiguousarray(inp["pool_scale"][0].reshape(8, 128).T)
    normw_h = np.ascontiguousarray(inp["ssd_norm_w"][0].reshape(16, 128).T)
    dtb_h = np.ascontiguousarray(inp["dt_bias"][0].reshape(32, 1))
    alog_h = np.ascontiguousarray(inp["a_log"][0].reshape(32, 1))
    drep_h = np.ascontiguousarray(np.broadcast_to(inp["d_skip"][0].reshape(1, 8, 1, 4), (128, 8, 2, 4)).reshape(128, 64))
    ln_h = np.ascontiguousarray(np.stack([inp["ln1_g"][0], inp["ln1_b"][0], inp["ln2_g"][0], inp["ln2_b"][0]], 0))
    shared = dict(wall=wall, wdt_h=wdt_h, wpool_h=wpool_h, pmat_h=_pool_mats(), maskc_h=_mask_c(),
                  ident_h=np.eye(128, dtype=np.float32), convw_h=convw_h, convb_h=convb_h, bg_h=bg_h, pscale_h=pscale_h,
                  normw_h=normw_h, dtb_h=dtb_h, alog_h=alog_h, drep_h=drep_h, ln_h=ln_h)
    return {k: np.ascontiguousarray(v, dtype=np.float32) for k, v in shared.items()}


_NC_CACHE = {}


def kernel(**inputs):
    inp = {k: np.asarray(v, dtype=np.float32) for k, v in inputs.items()}
    x = inp["x"]
    shared = _prep_shared(inp)
    if "nc" not in _NC_CACHE:
        _NC_CACHE["nc"] = build_nc(NT_FULL)
    nc = _NC_CACHE["nc"]
    in_maps = []
    for c in range(NCORES):
        m = dict(shared)
        m["xin"] = np.ascontiguousarray(x[NSEQ * c:NSEQ * (c + 1)].reshape(NSEQ * SEQ, D))
        in_maps.append(m)
    res = run_bass_kernel_spmd(nc, in_maps, core_ids=list(range(NCORES)))
    outs = [np.asarray(r["out"]).reshape(NSEQ, SEQ, D) for r in res.results]
    return np.concatenate(outs, axis=0).astype(np.float32)
```

```python
import numpy as np
from contextlib import ExitStack
import concourse.bass as bass
import concourse.mybir as mybir
from concourse.bass_utils import run_bass_kernel_spmd

F32 = mybir.dt.float32
BF16 = mybir.dt.bfloat16
AF = mybir.ActivationFunctionType
ALU = mybir.AluOpType

NCORES = 8
SEQ = 2048
D = 1024
T = 256
NCH = SEQ // T
NSEQ = 4
NT_FULL = NSEQ * NCH
ALPHA = float(2.0 ** 0.25)
LN_EPS = 1e-5
RMS_EPS = 1e-5
import os
NSLOT = 5
NM_SLOT = int(os.environ.get('NM_SLOT', '3'))
SEQ_E = bool(int(os.environ.get('SEQ_E', '0')))
TG = os.environ.get('TG', 'p')
HOP_LAT = float(os.environ.get('HOP_LAT', '0.6'))
SAME_LAT = float(os.environ.get('SAME_LAT', '0.1'))
ACT_SCALE = float(os.environ.get('ACT_SCALE', '1.25'))
PRIO_MODE = os.environ.get('PRIO_MODE', 'blevel')
MM_CHUNK = int(os.environ.get('MM_CHUNK', '4'))
TABLE_AWARE = int(os.environ.get('TABLE_AWARE', '1'))
TBL_US = 1.28
LIST_SCHED = bool(int(os.environ.get('LIST_SCHED', '1')))

WB = [("U", 0), ("U", 1)]
for _j in range(4):
    WB += [("Z", _j), ("X", 2 * _j), ("X", 2 * _j + 1)]
for _q in range(4):
    WB += [("GG", _q), ("S", _q)]
WB += [("O", 0), ("O", 1)] + [("UP", i) for i in range(8)] + [("DN", i) for i in range(8)]
NB = len(WB)


class Res:
    __slots__ = ("name", "last_w", "readers")

    def __init__(self, name):
        self.name = name
        self.last_w = None
        self.readers = []


class Op:
    __slots__ = ("eng", "fn", "deps", "signal", "sigval", "owner", "inc", "is_dma", "idx", "cost", "users", "nbytes", "tag", "t0", "t1", "prio", "family")


class _CostProbe:
    def __init__(self, eng):
        self.eng = eng
        self.cost = 0.0
        self.nbytes = 0
        self.family = None

    def __getattr__(self, name):
        def f(*a, **kw):
            if name == "matmul":
                r = kw["rhs"]
                mult = 4.0 if r.dtype == F32 else 1.0
                self.cost += r.free_size() * mult / 2200.0 + 0.02
            elif name == "transpose":
                self.cost += 0.27 if kw["in_"].dtype == F32 else 0.1
            elif name == "dma_start":
                self.nbytes += kw["out"].nbytes()
            elif name == "nop":
                self.cost += 0.05
            else:
                if name == "activation":
                    fnm = str(kw.get("func", a[2] if len(a) > 2 else "")).split(".")[-1]
                    fam = {"Exp": "E", "Ln": "E", "Silu": "S", "Sigmoid": "G"}.get(fnm)
                    if fam is not None:
                        self.family = fam
                out = kw.get("out", None)
                if out is None:
                    out = kw.get("ap", a[0] if a else None)
                cols = out.free_size() if out is not None else 256
                if self.eng == "act":
                    self.cost += (0.25 + cols / 1400.0) * ACT_SCALE
                elif self.eng == "dve":
                    self.cost += 0.15 + cols / 800.0
                else:
                    self.cost += 0.3 + cols * 0.0032
            return self
        return f


class Sched:
    ENGS = ("pe", "act", "dve", "pool", "sp")
    PRIO = {"BCD": 0, "E": 0, "F": 1, "G": 1, "I": 2, "J": 2, "PRE": 3}
    DEFAULT_COST = {"pe": 0.15, "act": 0.45, "dve": 0.4, "pool": 0.6, "sp": 0.05}

    def __init__(self, nc, es):
        self.nc = nc
        self.es = es
        self.engs = {k: [] for k in self.ENGS}
        self.sems = {}
        self.all_ops = []
        for k in self.ENGS:
            self.sems[k] = es.enter_context(nc.semaphore("s_" + k))

    def dma_sem(self, name):
        s = self.es.enter_context(self.nc.semaphore("d_" + name))
        self.sems["d_" + name] = s
        return "d_" + name

    def op(self, eng, fn, reads=(), writes=(), dma=None, cost=None, nbytes=0):
        if eng == 'pool' and 'p' in TG:
            eng = 'dve'
        if eng == 'pool!':
            eng = 'pool' if 'q' in TG else 'dve'
        o = Op()
        o.family = None
        if cost is None:
            pr = _CostProbe(eng)
            fn(pr)
            cost = max(pr.cost, 0.03)
            nbytes = pr.nbytes
            o.family = pr.family
        o.cost = cost
        o.nbytes = nbytes
        o.users = []
        o.tag = getattr(self, 'tag', None)
        o.prio = self.PRIO.get(o.tag[2], 1) if o.tag else 1
        o.t0 = o.t1 = 0.0
        o.eng = eng
        o.fn = fn
        o.is_dma = dma is not None
        o.owner = dma if dma is not None else eng
        o.inc = 16 if dma is not None else 1
        o.signal = dma is not None
        o.sigval = None
        deps = []
        for r in reads:
            if r.last_w is not None:
                deps.append(r.last_w)
        for w in writes:
            if w.last_w is not None:
                deps.append(w.last_w)
            deps.extend(w.readers)
        for r in reads:
            r.readers.append(o)
        for w in writes:
            w.last_w = o
            w.readers = []
        o.deps = [d for d in set(deps) if d is not o]
        o.idx = len(self.all_ops)
        self.all_ops.append(o)
        self.engs[eng].append(o)
        return o

    @staticmethod
    def _needs_wait(o, d):
        if d.is_dma or o.is_dma:
            return True
        if d.eng != o.eng:
            return True
        return o.eng != "pe"

    def finalize(self):
        for o in self.all_ops:
            for d in o.deps:
                if self._needs_wait(o, d):
                    d.signal = True
        cnt = {}
        for eng in self.ENGS:
            for o in self.engs[eng]:
                if o.signal:
                    cnt[o.owner] = cnt.get(o.owner, 0) + o.inc
                    o.sigval = cnt[o.owner]
        self.final_counts = cnt

    def schedule(self):
        ops = self.all_ops
        for o in ops:
            for d in o.deps:
                d.users.append(o)
        unmet = [len(o.deps) for o in ops]
        blevel = [0.0] * len(ops)
        if PRIO_MODE == "blevel":
            for o in reversed(ops):
                b = 0.0
                for u in o.users:
                    lat = SAME_LAT if (u.eng == o.eng and not o.is_dma) else HOP_LAT
                    if blevel[u.idx] + lat > b:
                        b = blevel[u.idx] + lat
                blevel[o.idx] = b + (2.0 + o.nbytes / 280e3 if o.is_dma else o.cost)
            for o in ops:
                o.prio = -blevel[o.idx]
        ready_t = [0.0] * len(ops)
        fin_t = [0.0] * len(ops)
        free = {k: 0.0 for k in self.ENGS}
        ready = {k: [] for k in self.ENGS}
        for o in ops:
            if unmet[o.idx] == 0:
                ready[o.eng].append(o)
        newq = {k: [] for k in self.ENGS}
        dma_free = 0.0
        act_fam = [None]
        n_done = 0
        total = len(ops)
        while n_done < total:
            best = None
            for k in self.ENGS:
                if not ready[k]:
                    continue
                f = free[k]
                if k == "act" and TABLE_AWARE:
                    cand = min(ready[k], key=lambda o: (max(f, ready_t[o.idx]) + (TBL_US if (TABLE_AWARE == 1 and o.family and o.family != act_fam[0]) else 0.0),
                                                        o.prio, o.idx))
                    st = max(f, ready_t[cand.idx]) + (TBL_US if (cand.family and cand.family != act_fam[0]) else 0.0)
                else:
                    cand = min(ready[k], key=lambda o: (max(f, ready_t[o.idx]), o.prio, o.idx))
                    st = max(f, ready_t[cand.idx])
                if best is None or (st, cand.idx) < (best[0], best[1].idx):
                    best = (st, cand)
            st, o = best
            if o.eng == 'act' and TABLE_AWARE and o.family:
                act_fam[0] = o.family
            ready[o.eng].remove(o)
            newq[o.eng].append(o)
            if o.is_dma:
                free[o.eng] = st + 0.06
                t0 = max(st, dma_free)
                dma_free = t0 + o.nbytes / 280e3
                fin = dma_free + 2.0
            else:
                fin = st + o.cost
                free[o.eng] = fin
            fin_t[o.idx] = fin
            o.t0, o.t1 = st, fin
            n_done += 1
            for u in o.users:
                unmet[u.idx] -= 1
                lat = (0.0 if o.eng == 'pe' else SAME_LAT) if (u.eng == o.eng and not o.is_dma) else HOP_LAT
                if fin + lat > ready_t[u.idx]:
                    ready_t[u.idx] = fin + lat
                if unmet[u.idx] == 0:
                    ready[u.eng].append(u)
        self.engs = newq
        self.sim_time = max(fin_t) if fin_t else 0.0

    def check_deadlock(self):
        pos = {k: 0 for k in self.engs}
        val = {}
        progress = True
        while progress:
            progress = False
            for k, ops in self.engs.items():
                while pos[k] < len(ops):
                    o = ops[pos[k]]
                    ok = all((not self._needs_wait(o, d)) or val.get(d.owner, 0) >= d.sigval for d in o.deps)
                    if not ok:
                        break
                    if o.signal:
                        val[o.owner] = val.get(o.owner, 0) + o.inc
                        assert val[o.owner] == o.sigval, (o.owner, val[o.owner], o.sigval)
                    pos[k] += 1
                    progress = True
        stuck = {k: (pos[k], len(ops)) for k, ops in self.engs.items() if pos[k] < len(ops)}
        assert not stuck, "deadlock: %s" % stuck

    def emit(self, block):
        sems = self.sems

        def run(eng_name, e):
            waited = {}
            for o in self.engs[eng_name]:
                need = {}
                for d in o.deps:
                    if not self._needs_wait(o, d):
                        continue
                    if d.sigval > waited.get(d.owner, 0):
                        need[d.owner] = max(need.get(d.owner, 0), d.sigval)
                for owner, v in need.items():
                    e.wait_ge(sems[owner], v)
                    waited[owner] = v
                ins = o.fn(e)
                if o.signal:
                    ins.then_inc(sems[o.owner], o.inc)

        @block.tensor
        def _(e):
            run("pe", e)

        @block.scalar
        def _(e):
            run("act", e)

        @block.vector
        def _(e):
            run("dve", e)

        @block.gpsimd
        def _(e):
            run("pool", e)

        @block.sync
        def _(e):
            run("sp", e)


class Buf:
    __slots__ = ("t", "r", "tb")

    def __init__(self, t, name):
        self.t = t
        self.r = Res(name)
        self.tb = None


def build_nc(NT=NT_FULL, PH=99):
    nc = bass.Bass("TRN2", target_bir_lowering=False)

    def din(name, shape, dt=F32):
        return nc.dram_tensor(name, shape, dt, kind="ExternalInput").ap()

    xin = din("xin", [NT * T, D])
    wall = din("wall", [NB, 128, 4096])
    wdt_h = din("wdt_h", [128, 256])
    wpool_h = din("wpool_h", [128, 2048])
    pmat_h = din("pmat_h", [128, 12 * 128])
    maskc_h = din("maskc_h", [128, 384])
    ident_h = din("ident_h", [128, 128])
    convw_h = din("convw_h", [128, 128])
    convb_h = din("convb_h", [128, 32])
    bg_h = din("bg_h", [128, 16])
    pscale_h = din("pscale_h", [128, 8])
    normw_h = din("normw_h", [128, 16])
    dtb_h = din("dtb_h", [32, 1])
    alog_h = din("alog_h", [32, 1])
    drep_h = din("drep_h", [128, 64])
    ln_h = din("ln_h", [4, 1024])
    out = nc.dram_tensor("out", [NT * T, D], F32, kind="ExternalOutput").ap()
    wsc = nc.dram_tensor("wsc", [NB, 128, 4096], BF16).ap()
    acs_scr = nc.dram_tensor("acs_scr", [NT, 32, 256], F32).ap()

    with ExitStack() as es:
        S = Sched(nc, es)

        def sb(name, shape, dt=F32):
            return Buf(es.enter_context(nc.sbuf_tensor(name, shape, dt)), name)

        ident = sb("ident", [128, 128])
        identb = sb("identb", [128, 128], BF16)
        pmat = sb("pmat", [128, 12 * 128], BF16)
        maskc = sb("maskc", [128, 384])
        convw = sb("convw", [128, 128])
        convb = sb("convb", [128, 32])
        bg = sb("bg", [128, 16])
        pscale = sb("pscale", [128, 8])
        normw = sb("normw", [128, 16])
        dtb = sb("dtb", [32, 1])
        aneg = sb("aneg", [32, 1])
        drep = sb("drep", [128, 64])
        lnrep = sb("lnrep", [128, 4, 1024])
        wdt = sb("wdt", [128, 8, 32], BF16)
        wpool = sb("wpool", [128, 8, 256], BF16)
        ones32 = sb("ones32", [32, 256])
        CONST = Res("consts_ready")

        ring = [sb("wring%d" % i, [128, 4096], BF16) for i in range(NSLOT)]
        NXT = int(os.environ.get("NXT", "3"))
        xt = [sb("xt%d" % i, [128, 2, 1024]) for i in range(NXT)]
        hT = sb("hT", [128, 8, 256], BF16)
        h1T = sb("h1T", [128, 8, 256], BF16)
        h2T = sb("h2T", [128, 8, 256], BF16)
        hbufs = [hT, h1T, h2T]
        u_tm = sb("u_tm", [128, 3, 1024], BF16)
        uslot = [Res("uslot%d" % i) for i in range(3)]
        pooledT = sb("pooledT", [128, 8, 256], BF16)
        zs = [sb("zs%d" % i, [128, 2, 256], BF16) for i in range(3)]
        xr = [sb("xr%d" % i, [128, 2, 259]) for i in range(2)]
        xr_h = [Res("xr_h%d" % h) for h in range(2)]
        xr_c = [Res("xr_c%d" % h) for h in range(2)]
        carry = sb("carry", [128, 8, 4, 3])
        carry_r = [Res("carry%d" % g) for g in range(8)]
        cacc = [sb("cacc0", [128, 4, 256])] * 2
        cacc_r = [[Res("cacc_%d" % c) for c in range(4)]] * 2
        xc = [sb("xc0", [128, 4, 256], BF16)] * 2
        xs_tm = [sb("xs_tm%d" % i, [128, 2, 256], BF16) for i in range(2)]
        xw_tm = [sb("xw_tm%d" % i, [128, 2, 256], BF16) for i in range(2)]
        xd_tm = [sb("xd_tm%d" % i, [128, 2, 256], BF16) for i in range(2)]
        b_tm = [sb("b_tm%d" % i, [128, 2, 128], BF16) for i in range(2)]
        dt_e = sb("dt_e", [32, 256])
        dt_v = sb("dt_v", [32, 256])
        dt_ecs = sb("dt_ecs", [32, 256])
        dt_q = sb("dt_q", [32, 3, 256])
        dgm = sb("dgm", [32, 32])
        tmq2 = [sb("tmq%d" % i, [128, 3, 8, 2, 4]) for i in range(2)]
        cdB2 = [sb("cdB%d" % i, [128, 32]) for i in range(2)]
        acsB = [sb("acsB%d" % i, [128, 2, 256]) for i in range(2)]
        cbm = [sb("cbm%d" % i, [128, 384]) for i in range(2)]
        lt = [sb("lt%d" % i, [128, 384]) for i in range(2)]
        lt_r = [[Res("lt%d_%d" % (i, c)) for c in range(2)] for i in range(2)]
        mt = [sb("mt%d" % i, [128, 384], BF16) for i in range(2)]
        Sst = sb("Sst", [128, 2048])
        Sst_r = [Res("Sst%d" % g) for g in range(8)]
        Sbf_r = [Res("Sbf%d" % g) for g in range(8)]
        Sbf = sb("Sbf", [128, 2048], BF16)
        ytmp = [sb("ytmp%d" % i, [128, 2, 256]) for i in range(2)]
        ybuf = [sb("ybuf0", [128, 2, 256])] * 2
        sqj = sb("sqj", [128, 256], BF16)
        ssq = [sb("ssq%d" % i, [128, 2]) for i in range(2)]
        rstd = [sb("rstd%d" % i, [128, 2]) for i in range(2)]
        ygs = [sb("ygs%d" % i, [128, 2, 256], BF16) for i in range(2)]
        ygs_r = [[Res("ygs%d_%d" % (i, c)) for c in range(2)] for i in range(2)]
        ygT = sb("ygT", [128, 16, 256], BF16)
        ygT_r = [Res("ygT%d" % g) for g in range(8)]
        sg = sb("sg", [128, 2, 256])
        mP = sb("mP", [128, 256])
        mS = sb("mS", [128, 256])
        mergedT = sb("mergedT", [128, 8, 256], BF16)
        aT_f32 = sb("aT", [128, 4096])
        aT = Buf(aT_f32.t[:].bitcast(BF16).rearrange("p (a b) -> p a b", a=32), "aT_bf")
        aT.r = aT_f32.r
        lnst = sb("lnst", [128, 2, 6])
        lnmv = sb("lnmv", [128, 2])
        lnr = sb("lnr", [128, 1])

        banks = []
        for i in range(8):
            b = Buf(es.enter_context(nc.psum_tensor("psb%d" % i, [128, 512], F32)), "psb%d" % i)
            b.tb = b.t[:].bitcast(BF16)
            banks.append(b)
        POOLS = {'b': [0], 'e': [1, 2, 3, 4, 5], 'f': [6, 7]}
        bank_ctr = {k: 0 for k in POOLS}
        bank_pool = ["e"]

        def bank():
            p = bank_pool[0]
            lst = POOLS[p]
            b = banks[lst[bank_ctr[p] % len(lst)]]
            bank_ctr[p] += 1
            return b

        d_ps = [S.dma_sem("ps%d" % i) for i in range(NSLOT)]
        d_c = S.dma_sem("c")
        d_w = [S.dma_sem("w%d" % i) for i in range(NSLOT)]
        d_x = [S.dma_sem("x%d" % i) for i in range(3)]
        d_o = [S.dma_sem("o%d" % i) for i in range(3)]
        d_aw = S.dma_sem("aw")
        d_ab = [S.dma_sem("ab%d" % i) for i in range(2)]

        pre_res = []

        def cload(eng, dst_ap, src_ap, sem, name):
            r = Res(name)
            pre_res.append(r)
            S.op(eng, lambda e: e.dma_start(out=dst_ap, in_=src_ap), writes=[r], dma=sem)

        cload("sp", ident.t[:], ident_h, d_c, "c_ident")
        cload("sp", maskc.t[:], maskc_h, d_c, "c_maskc")
        cload("sp", convw.t[:], convw_h, d_c, "c_convw")
        cload("sp", convb.t[:], convb_h, d_c, "c_convb")
        cload("sp", bg.t[:], bg_h, d_c, "c_bg")
        cload("sp", pscale.t[:], pscale_h, d_c, "c_pscale")
        cload("sp", normw.t[:], normw_h, d_c, "c_normw")
        cload("sp", dtb.t[:], dtb_h, d_c, "c_dtb")
        cload("sp", aneg.t[:], alog_h, d_c, "c_alog")
        cload("sp", drep.t[:], drep_h, d_c, "c_drep")
        for i in range(4):
            cload("sp", lnrep.t[:, i, :], ln_h[i, :].partition_broadcast(128), d_c, "c_ln%d" % i)
        def stage_cast(dst_ap, src_ap, ncols, name):
            stv = xt[0].t[:].rearrange("p a b -> p (a b)")[:, 0:ncols]
            S.op("sp", lambda e: e.dma_start(out=stv, in_=src_ap), writes=[xt[0].r], dma=d_x[0])
            r = Res(name)
            pre_res.append(r)
            S.op("dve", lambda e: e.tensor_copy(out=dst_ap, in_=stv), reads=[xt[0].r], writes=[r])

        stage_cast(wdt.t[:].rearrange("p a b -> p (a b)"), wdt_h, 256, "c_wdt")
        stage_cast(wpool.t[:].rearrange("p a b -> p (a b)"), wpool_h, 2048, "c_wpool")
        stage_cast(pmat.t[:], pmat_h, 1536, "c_pmat")
        wres = [[Res("wsc%d_%d" % (b, hf)) for hf in range(2)] for b in range(NB)]
        S.tag = ('p', 0, 'PRE')
        st_f32 = aT_f32.t[:]
        st_r = [Res("stg%d" % i) for i in range(2)]
        xt2b = xt[2].t[:].rearrange("p a b -> p (a b)").bitcast(BF16)
        cast_dst = [xt2b[:, 0:2048], xt2b[:, 2048:4096], h1T.t[:].rearrange("p a b -> p (a b)")]
        if 'B' in TG:
            cast_dst = [ring[0].t[:, 0:2048], ring[1].t[:, 0:2048], ring[2].t[:, 0:2048]]
            cast_r = [ring[0].r, ring[1].r, ring[2].r]
        cast_r = [Res("cst%d" % i) for i in range(3)]
        if 'B' in TG:
            cast_r = [ring[0].r, ring[1].r, ring[2].r]
        d_st = [S.dma_sem("st%d" % i) for i in range(2)]
        cast_engs = ("act", "dve")
        for b in range(NB):
            for hf in range(2):
                i = 2 * b + hf
                stv = st_f32[:, (i % 2) * 2048:(i % 2 + 1) * 2048]
                S.op("sp", lambda e, stv=stv, b=b, hf=hf: e.dma_start(out=stv, in_=wall[b][:, hf * 2048:(hf + 1) * 2048]),
                     writes=[st_r[i % 2]], dma=d_st[i % 2])
                cs = i % 3
                dstv = cast_dst[cs]
                eng = cast_engs[i % 2]
                if eng == "act":
                    S.op("act", lambda e, dstv=dstv, stv=stv: e.activation(out=dstv, in_=stv, func=AF.Copy),
                         reads=[st_r[i % 2]], writes=[cast_r[cs]])
                else:
                    S.op("dve", lambda e, dstv=dstv, stv=stv: e.tensor_copy(out=dstv, in_=stv), reads=[st_r[i % 2]], writes=[cast_r[cs]])
                S.op("sp", lambda e, dstv=dstv, b=b, hf=hf: e.dma_start(out=wsc[b][:, hf * 2048:(hf + 1) * 2048], in_=dstv),
                     reads=[cast_r[cs]], writes=[wres[b][hf]], dma=d_ps[cs])
        S.op("dve", lambda e: e.nop(nofuse=True), writes=st_r + [aT.r])
        S.op("dve", lambda e: e.nop(nofuse=True), writes=cast_r[0:2] + [xt[2].r])
        S.op("dve", lambda e: e.nop(nofuse=True), writes=cast_r[2:3] + [h1T.r])
        S.tag = None
        S.op("sp", lambda e: e.nop(nofuse=True), reads=pre_res, writes=[CONST])
        S.op("dve", lambda e: e.tensor_copy(out=identb.t[:], in_=ident.t[:]), reads=[CONST], writes=[identb.r])
        S.op("act", lambda e: e.activation(out=aneg.t[:], in_=aneg.t[:], func=AF.Exp), reads=[CONST], writes=[aneg.r])
        S.op("dve", lambda e: e.tensor_scalar(out=aneg.t[:], in0=aneg.t[:], scalar1=-1.0, scalar2=None, op0=ALU.mult),
             reads=[aneg.r], writes=[aneg.r])
        S.op("dve", lambda e: e.memset(ones32.t[:], 1.0), writes=[ones32.r])

        wstate = {"m": 0, "f": 0}

        def w_acquire(kind, idx):
            st = "f" if kind in ("UP", "DN") else "m"
            k = wstate[st]
            wstate[st] = k + 1
            slot = (k % NM_SLOT) if st == "m" else NM_SLOT + (k % (NSLOT - NM_SLOT))
            bidx = WB.index((kind, idx))
            src = wsc[bidx]
            dst = ring[slot].t[:]
            S.op("sp", lambda e: e.dma_start(out=dst, in_=src), reads=wres[bidx], writes=[ring[slot].r], dma=d_w[slot])
            return ring[slot], k

        def w_done(k):
            pass

        def mm(out_ap, pairs, reads, writes):
            pairs = list(pairs)
            n = len(pairs)
            last = None
            for c0 in range(0, n, MM_CHUNK):
                sub = list(enumerate(pairs))[c0:c0 + MM_CHUNK]

                def fn(e, sub=sub):
                    ins = None
                    for i, (l, r) in sub:
                        ins = e.matmul(out=out_ap, lhsT=l, rhs=r, start=(i == 0), stop=(i == n - 1))
                    return ins
                last = S.op("pe", fn, reads=reads, writes=writes)
            return last

        def tr(out_ap, in_ap, idt_ep, reads, writes):
            return S.op("pe", lambda e: e.transpose(out=out_ap, in_=in_ap, identity=idt_ep), reads=reads, writes=writes)

        def x_load(ti):
            xb = xt[ti % NXT]
            src = xin[ti * T:(ti + 1) * T, :].rearrange("(s p) d -> p s d", p=128)
            S.op("sp", lambda e: e.dma_start(out=xb.t[:], in_=src), reads=([aT.r, h1T.r] if (ti == 0 and 'W' in TG) else []),
                 writes=[xb.r], dma=d_x[ti % NXT])

        def transpose_to_hT(xb, hdst):
            for pair in range(4):
                bk = bank()
                for kk in range(2):
                    kc = pair * 2 + kk
                    for s in range(2):
                        tr(bk.t[:, kk * 256 + s * 128: kk * 256 + (s + 1) * 128], xb.t[:, s, kc * 128:(kc + 1) * 128],
                           ident.t[:], [xb.r, CONST], [bk.r])
                dst = hdst.t[:, pair * 2:pair * 2 + 2, :].rearrange("p a b -> p (a b)")
                S.op("act", lambda e, dst=dst, bk=bk: e.activation(out=dst, in_=bk.t[:], func=AF.Copy),
                     reads=[bk.r], writes=[hdst.r])

        def layer_norm(xb, gi):
            for s in range(2):
                row = xb.t[:, s, :]
                for hh in range(2):
                    S.op("dve", lambda e, hh=hh, row=row: e.bn_stats(out=lnst.t[:, hh, :], in_=row[:, hh * 512:(hh + 1) * 512]),
                         reads=[xb.r], writes=[lnst.r])
                S.op("dve", lambda e: e.bn_aggr(out=lnmv.t[:], in_=lnst.t[:].rearrange("p a b -> p (a b)")),
                     reads=[lnst.r], writes=[lnmv.r])
                S.op("act", lambda e: e.activation(out=lnr.t[:], in_=lnmv.t[:, 1:2], func=AF.Ln, bias=LN_EPS),
                     reads=[lnmv.r], writes=[lnr.r])
                S.op("act", lambda e: e.activation(out=lnr.t[:], in_=lnr.t[:], func=AF.Exp, scale=-0.5),
                     reads=[lnr.r], writes=[lnr.r])
                S.op("dve", lambda e: e.tensor_scalar(out=lnmv.t[:, 0:1], in0=lnmv.t[:, 0:1], scalar1=lnr.t[:, 0:1], scalar2=-1.0,
                                                      op0=ALU.mult, op1=ALU.mult), reads=[lnmv.r, lnr.r], writes=[lnmv.r])
                S.op("act", lambda e, row=row: e.activation(out=row, in_=row, func=AF.Identity, scale=lnr.t[:, 0:1], bias=lnmv.t[:, 0:1]),
                     reads=[xb.r, lnmv.r, lnr.r], writes=[xb.r])
                S.op("pool!", lambda e, row=row: e.tensor_tensor(out=row, in0=row, in1=lnrep.t[:, gi, :], op=ALU.mult),
                     reads=[xb.r, CONST], writes=[xb.r])
                S.op("pool!", lambda e, row=row: e.tensor_tensor(out=row, in0=row, in1=lnrep.t[:, gi + 1, :], op=ALU.add),
                     reads=[xb.r, CONST], writes=[xb.r])

        def finish(ti, xb):
            dst = out[ti * T:(ti + 1) * T, :].rearrange("(s p) d -> p s d", p=128)
            S.op("sp", lambda e: e.dma_start(out=dst, in_=xb.t[:]), reads=[xb.r], dma=d_o[ti % 2])
            if ti + 1 < NT and PH < 6:
                x_load(ti + 1)

        def mixer_gen(ti):
            seq, ch = divmod(ti, NCH)
            first = ch == 0
            xb = xt[ti % NXT]
            S.tag = ('m', ti, 'BCD')
            bank_pool[0] = 'b'
            tmq = tmq2[ti % 2]
            cdB = cdB2[ti % 2]
            hT = hbufs[ti % 3]
            h1T = hT
            x_load(ti)
            if first:
                S.op("pool", lambda e: e.memset(Sst.t[:], 0.0), writes=Sst_r)
                S.op("pool", lambda e: e.memset(Sbf.t[:], 0.0), writes=Sbf_r)
                S.op("pool", lambda e: e.memset(carry.t[:], 0.0), writes=carry_r)

            transpose_to_hT(xb, hT)

            bk = bank()
            mm(bk.t[0:32, 0:256], [(wdt.t[:, kc, :], hT.t[:, kc, :]) for kc in range(8)], [CONST, hT.r], [bk.r])
            S.op("act", lambda e, bk=bk: e.activation(out=dt_e.t[:], in_=bk.t[0:32, 0:256], func=AF.Exp, bias=dtb.t[:, 0:1]),
                 reads=[bk.r, CONST], writes=[dt_e.r])
            S.op("act", lambda e: e.activation(out=dt_v.t[:], in_=dt_e.t[:], func=AF.Ln, bias=1.0),
                 reads=[dt_e.r], writes=[dt_v.r])
            S.op("dve", lambda e: e.tensor_scalar(out=dt_e.t[:], in0=dt_v.t[:], scalar1=aneg.t[:, 0:1], scalar2=None, op0=ALU.mult),
                 reads=[dt_v.r, aneg.r], writes=[dt_e.r])
            S.op("dve", lambda e: e.tensor_tensor_scan(out=dt_ecs.t[:], data0=ones32.t[:], data1=dt_e.t[:], initial=0.0,
                                                       op0=ALU.mult, op1=ALU.add),
                 reads=[ones32.r, dt_e.r], writes=[dt_ecs.r])
            scr = acs_scr[ti]
            scr_r = Res("acs_scr%d" % ti)
            S.op("sp", lambda e, scr=scr: e.dma_start(out=scr, in_=dt_ecs.t[:]), reads=[dt_ecs.r], writes=[scr_r], dma=d_aw)

            def bcast_load(gh, scr=scr, scr_r=scr_r):
                ab = acsB[gh % 2]
                src = scr[2 * gh:2 * gh + 2, :].rearrange("h t -> (h t)").partition_broadcast(128)
                S.op("sp", lambda e: e.dma_start(out=ab.t[:].rearrange("p h t -> p (h t)"), in_=src),
                     reads=[scr_r], writes=[ab.r], dma=d_ab[gh % 2])

            S.op("act", lambda e: e.activation(out=dt_e.t[:], in_=dt_v.t[:], func=AF.Ln), reads=[dt_v.r], writes=[dt_e.r])
            S.op("dve", lambda e: e.tensor_tensor(out=dt_q.t[:, 0, :], in0=dt_e.t[:], in1=dt_ecs.t[:], op=ALU.subtract),
                 reads=[dt_e.r, dt_ecs.r], writes=[dt_q.r])
            S.op("act", lambda e: e.activation(out=dt_q.t[:, 1, :], in_=dt_ecs.t[:], func=AF.Exp), reads=[dt_ecs.r], writes=[dt_q.r])
            S.op("act", lambda e: e.activation(out=dt_e.t[:], in_=dt_ecs.t[:], func=AF.Exp, bias=dt_ecs.t[:, 255:256], scale=-1.0),
                 reads=[dt_ecs.r, dt_e.r], writes=[dt_e.r])
            S.op("dve", lambda e: e.tensor_tensor(out=dt_q.t[:, 2, :], in0=dt_e.t[:], in1=dt_v.t[:], op=ALU.mult),
                 reads=[dt_e.r, dt_v.r], writes=[dt_q.r])
            bk = bank()
            for s in range(2):
                for q in range(3):
                    tr(bk.t[:, (s * 3 + q) * 32:(s * 3 + q + 1) * 32], dt_q.t[:, q, s * 128:(s + 1) * 128], ident.t[0:32, 0:32],
                       [dt_q.r, CONST], [bk.r])
            for s in range(2):
                for q in range(3):
                    S.op("dve", lambda e, bk=bk, s=s, q=q: e.tensor_copy(
                        out=tmq.t[:, q, :, s, :], in_=bk.t[:, (s * 3 + q) * 32:(s * 3 + q + 1) * 32].rearrange("p (g h) -> p g h", g=8)),
                         reads=[bk.r], writes=[tmq.r])
            S.op("dve", lambda e: e.tensor_scalar(out=dgm.t[:], in0=ident.t[0:32, 0:32], scalar1=dt_q.t[:, 1, 255:256], scalar2=None,
                                                  op0=ALU.mult), reads=[CONST, dt_q.r], writes=[dgm.r])
            bk = bank()
            mm(bk.t[:, 0:32], [(ones32.t[:, 0:128], dgm.t[:])], [ones32.r, dgm.r], [bk.r])
            S.op("dve", lambda e, bk=bk: e.tensor_copy(out=cdB.t[:], in_=bk.t[:, 0:32]), reads=[bk.r], writes=[cdB.r])

            yield
            zinfo = {}

            def stage_a(g):
                i2 = g % 2
                if g % 2 == 0:
                    zinfo["z"] = w_acquire("Z", g // 2)
                zblk, zk = zinfo["z"]
                zv = zblk.t[:].rearrange("p (k n) -> p k n", k=8)
                bkz = bank()
                for s in range(2):
                    mm(bkz.t[:, s * 256:(s + 1) * 256],
                       [(hT.t[:, kc, s * 128:(s + 1) * 128], zv[:, kc, (g % 2) * 256:(g % 2 + 1) * 256]) for kc in range(8)],
                       [hT.r, zblk.r], [bkz.r])
                zsb = zs[g % 3]
                S.op("act", lambda e: e.activation(out=zsb.t[:].rearrange("p a b -> p (a b)"), in_=bkz.t[:], func=AF.Silu),
                     reads=[bkz.r], writes=[zsb.r])
                xblk, xk = w_acquire("X", g)
                xv = xblk.t[:].rearrange("p (k n) -> p k n", k=8)
                for half in range(2):
                    xrb = xr[half]
                    S.op("dve", lambda e, xrb=xrb, half=half: e.tensor_copy(out=xrb.t[:, :, 0:3], in_=carry.t[:, g, 2 * half:2 * half + 2, :]),
                         reads=[carry_r[g]], writes=[xr_c[half]])
                    bkx = bank()
                    for kk in range(2):
                        ct = half * 2 + kk
                        mm(bkx.t[:, kk * 256:(kk + 1) * 256],
                           [(xv[:, kc, ct * 128:(ct + 1) * 128], hT.t[:, kc, :]) for kc in range(8)], [hT.r, xblk.r], [bkx.r])
                    S.op("act", lambda e, xrb=xrb, bkx=bkx: e.activation(
                        out=xrb.t[:, :, 3:259], in_=bkx.t[:].rearrange("p (a b) -> p a b", a=2), func=AF.Copy),
                         reads=[bkx.r], writes=[xr_h[half]])
                w_done(xk)
                if g % 2 == 1:
                    w_done(zk)
                for half in range(2):
                    xrb = xr[half]
                    S.op("dve", lambda e, xrb=xrb, half=half: e.tensor_copy(out=carry.t[:, g, 2 * half:2 * half + 2, :], in_=xrb.t[:, :, 256:259]),
                         reads=[xr_h[half]], writes=[carry_r[g]])
                cab = cacc[i2]

                def cch_of(ct):
                    return (g * 2 + ct) if ct < 2 else (16 + g if ct == 2 else 24 + g)
                for ct in range(4):
                    cch = cch_of(ct)
                    xrb = xr[ct // 2]
                    S.op("act", lambda e, ct=ct, cch=cch, xrb=xrb: e.activation(
                        out=cab.t[:, ct, :], in_=xrb.t[:, ct % 2, 3:259], func=AF.Identity,
                        bias=convb.t[:, cch:cch + 1], scale=convw.t[:, cch * 4 + 3:cch * 4 + 4]),
                         reads=[xr_h[ct // 2], CONST], writes=[cacc_r[i2][ct]])
                for k in range(3):
                    for ct in range(4):
                        cch = cch_of(ct)
                        xrb = xr[ct // 2]
                        S.op("dve", lambda e, ct=ct, cch=cch, k=k, xrb=xrb: e.scalar_tensor_tensor(
                            out=cab.t[:, ct, :], in0=xrb.t[:, ct % 2, k:k + 256], scalar=convw.t[:, cch * 4 + k:cch * 4 + k + 1],
                            in1=cab.t[:, ct, :], op0=ALU.mult, op1=ALU.add),
                             reads=[xr_h[ct // 2], xr_c[ct // 2], CONST, cacc_r[i2][ct]], writes=[cacc_r[i2][ct]])
                xcb = xc[i2]
                S.op("act", lambda e: e.activation(out=xcb.t[:].rearrange("p a b -> p (a b)"),
                                                   in_=cab.t[:].rearrange("p a b -> p (a b)"), func=AF.Silu),
                     reads=cacc_r[i2], writes=[xcb.r])

            def stage_b(g):
                i2 = g % 2
                xcb = xc[i2]
                xsb, xwb, xdb, btb = xs_tm[i2], xw_tm[i2], xd_tm[i2], b_tm[i2]
                bkt = bank()
                for s in range(2):
                    for ct in range(2):
                        tr(bkt.tb[:, s * 256 + ct * 128: s * 256 + (ct + 1) * 128], xcb.t[:, ct, s * 128:(s + 1) * 128], identb.t[:],
                           [xcb.r, identb.r], [bkt.r])
                    tr(bkt.tb[:, 512 + s * 128: 512 + (s + 1) * 128], xcb.t[:, 2, s * 128:(s + 1) * 128], identb.t[:],
                       [xcb.r, identb.r], [bkt.r])
                S.op("act", lambda e: e.activation(out=xsb.t[:].rearrange("p a b -> p (a b)"), in_=bkt.tb[:, 0:512], func=AF.Copy),
                     reads=[bkt.r], writes=[xsb.r])
                if True:
                    S.op("act", lambda e: e.activation(out=btb.t[:].rearrange("p a b -> p (a b)"), in_=bkt.tb[:, 512:768], func=AF.Copy),
                         reads=[bkt.r], writes=[btb.r])
                else:
                    S.op("dve", lambda e: e.tensor_copy(out=btb.t[:].rearrange("p a b -> p (a b)"), in_=bkt.tb[:, 512:768]),
                         reads=[bkt.r], writes=[btb.r])
                S.op("pool", lambda e: e.tensor_tensor(
                    out=xwb.t[:].rearrange("p s (h d) -> p (s h) d", h=4),
                    in0=xsb.t[:].rearrange("p s (h d) -> p (s h) d", h=4),
                    in1=tmq.t[:, 2, g, :, :].rearrange("p s h -> p (s h)").unsqueeze(2).to_broadcast([128, 8, 64]), op=ALU.mult),
                     reads=[xsb.r, tmq.r], writes=[xwb.r])
                S.op("pool", lambda e: e.tensor_tensor(
                    out=xdb.t[:].rearrange("p s (h d) -> p (s h) d", h=4),
                    in0=xsb.t[:].rearrange("p s (h d) -> p (s h) d", h=4),
                    in1=drep.t[:, g * 8:(g + 1) * 8].unsqueeze(2).to_broadcast([128, 8, 64]), op=ALU.mult),
                     reads=[xsb.r, CONST], writes=[xdb.r])
                bkc = bank()
                mm(bkc.t[:, 0:256], [(xcb.t[:, 2, 0:128], xcb.t[:, 3, 0:256])], [xcb.r], [bkc.r])
                mm(bkc.t[:, 256:384], [(xcb.t[:, 2, 128:256], xcb.t[:, 3, 128:256])], [xcb.r], [bkc.r])
                cb = cbm[i2]
                S.op("dve", lambda e: e.tensor_tensor(out=cb.t[:], in0=bkc.t[:, 0:384], in1=maskc.t[:], op=ALU.mult),
                     reads=[bkc.r, CONST], writes=[cb.r])
                bko = bank()
                for ls in range(2):
                    mm(bko.t[:, ls * 256:(ls + 1) * 256], [(xcb.t[:, 3, ls * 128:(ls + 1) * 128], Sbf.t[:, g * 256:(g + 1) * 256])],
                       [xcb.r, Sbf_r[g]], [bko.r])
                yt = ytmp[g % 2]
                S.op("dve", lambda e: e.tensor_tensor(
                    out=yt.t[:].rearrange("p s (h d) -> p (s h) d", h=4),
                    in0=bko.t[:].rearrange("p (a d) -> p a d", a=8),
                    in1=tmq.t[:, 1, g, :, :].rearrange("p s h -> p (s h)").unsqueeze(2).to_broadcast([128, 8, 64]), op=ALU.mult),
                     reads=[bko.r, tmq.r], writes=[yt.r])
                bks = bank()
                mm(bks.t[:, 0:256], [(btb.t[:, sc, :], xwb.t[:, sc, :]) for sc in range(2)], [btb.r, xwb.r], [bks.r])
                Sg = Sst.t[:, g * 256:(g + 1) * 256]
                S.op("dve", lambda e: e.tensor_tensor(
                    out=Sg.rearrange("p (h d) -> p h d", h=4), in0=Sg.rearrange("p (h d) -> p h d", h=4),
                    in1=cdB.t[:, 4 * g:4 * g + 4].unsqueeze(2).to_broadcast([128, 4, 64]), op=ALU.mult),
                     reads=[Sst_r[g], cdB.r], writes=[Sst_r[g]])
                S.op("dve", lambda e: e.tensor_tensor(out=Sg, in0=Sg, in1=bks.t[:, 0:256], op=ALU.add),
                     reads=[Sst_r[g], bks.r], writes=[Sst_r[g]])
                S.op("act", lambda e: e.activation(out=Sbf.t[:, g * 256:(g + 1) * 256], in_=Sg, func=AF.Copy),
                     reads=[Sst_r[g]], writes=[Sbf_r[g]])

            ydiag_banks = {}

            def stage_c(g):
                i2 = g % 2
                xsb, xdb = xs_tm[i2], xd_tm[i2]
                cb = cbm[i2]
                bcast_load(2 * g)
                bcast_load(2 * g + 1)
                bkd = bank()
                ydiag_banks[g] = bkd
                for h in range(4):
                    L = lt[h % 2]
                    M = mt[h % 2]
                    ab = acsB[h // 2]
                    S.op("act", lambda e, L=L, h=h, ab=ab: e.activation(
                        out=L.t[:, 0:256], in_=ab.t[:, h % 2, :], func=AF.Exp, bias=tmq.t[:, 0, g, 0, h:h + 1]),
                         reads=[ab.r, tmq.r], writes=[lt_r[h % 2][0]])
                    S.op("act", lambda e, L=L, h=h, ab=ab: e.activation(
                        out=L.t[:, 256:384], in_=ab.t[:, h % 2, 128:256], func=AF.Exp, bias=tmq.t[:, 0, g, 1, h:h + 1]),
                         reads=[ab.r, tmq.r], writes=[lt_r[h % 2][1]])
                    S.op("dve", lambda e, L=L, M=M: e.scalar_tensor_tensor(out=M.t[:], in0=L.t[:], scalar=1e30, in1=cb.t[:],
                                                                          op0=ALU.min, op1=ALU.mult),
                         reads=[lt_r[h % 2][0], lt_r[h % 2][1], cb.r], writes=[M.r])
                    hs = slice(h * 64, (h + 1) * 64)
                    mm(bkd.t[:, h * 64:(h + 1) * 64],
                       [(M.t[:, 0:128], xsb.t[:, 0, hs]), (identb.t[:], xdb.t[:, 0, hs])],
                       [M.r, xsb.r, xdb.r, identb.r], [bkd.r])
                    mm(bkd.t[:, 256 + h * 64:256 + (h + 1) * 64],
                       [(M.t[:, 128:256], xsb.t[:, 0, hs]), (M.t[:, 256:384], xsb.t[:, 1, hs]), (identb.t[:], xdb.t[:, 1, hs])],
                       [M.r, xsb.r, xdb.r, identb.r], [bkd.r])

            def stage_d(g):
                i2 = g % 2
                bkd = ydiag_banks.pop(g)
                yt = ytmp[g % 2]
                yb = ybuf[i2]
                zsb = zs[g % 3]
                S.op("dve", lambda e: e.tensor_tensor(out=yb.t[:].rearrange("p a b -> p (a b)"),
                                                      in0=yt.t[:].rearrange("p a b -> p (a b)"), in1=bkd.t[:], op=ALU.add),
                     reads=[yt.r, bkd.r], writes=[yb.r])
                S.op("pool", lambda e: e.tensor_tensor(out=yb.t[:], in0=yb.t[:], in1=zsb.t[:], op=ALU.mult),
                     reads=[yb.r, zsb.r], writes=[yb.r])
                sq = ssq[i2]
                rs = rstd[i2]
                for ls in range(2):
                    S.op("act", lambda e, ls=ls: e.activation(out=sqj.t[:], in_=yb.t[:, ls, :], func=AF.Square,
                                                              accum_out=sq.t[:, ls:ls + 1]),
                         reads=[yb.r], writes=[sqj.r, sq.r])
                S.op("act", lambda e: e.activation(out=rs.t[:], in_=sq.t[:], func=AF.Ln, scale=1.0 / 256.0, bias=RMS_EPS),
                     reads=[sq.r], writes=[rs.r])
                S.op("act", lambda e: e.activation(out=rs.t[:], in_=rs.t[:], func=AF.Exp, scale=-0.5),
                     reads=[rs.r], writes=[rs.r])
                yg = ygs[i2]
                for ls in range(2):
                    S.op("act", lambda e, ls=ls: e.activation(out=yg.t[:, ls, :], in_=yb.t[:, ls, :], func=AF.Identity,
                                                              scale=rs.t[:, ls:ls + 1]),
                         reads=[yb.r, rs.r], writes=[ygs_r[i2][ls]])
                bky = bank()
                for fc in range(2):
                    for ls in range(2):
                        tr(bky.tb[:, fc * 256 + ls * 128: fc * 256 + (ls + 1) * 128], yg.t[:, ls, fc * 128:(fc + 1) * 128], identb.t[:],
                           [ygs_r[i2][ls], identb.r], [bky.r])
                for fc in range(2):
                    S.op("act", lambda e, fc=fc: e.activation(
                        out=ygT.t[:, 2 * g + fc, :], in_=bky.tb[:, fc * 256:(fc + 1) * 256], func=AF.Identity,
                        scale=normw.t[:, 2 * g + fc:2 * g + fc + 1]),
                         reads=[bky.r, CONST], writes=[ygT_r[g]])

            for g in range(8):
                S.tag = ('m', ti, 'E', g, 'a')
                bank_pool[0] = 'e'
                stage_a(g)
                S.tag = ('m', ti, 'E', g, 'b')
                stage_b(g)
                S.tag = ('m', ti, 'E', g, 'c')
                stage_c(g)
                S.tag = ('m', ti, 'E', g, 'd')
                stage_d(g)
                yield

            S.tag = ('m', ti, 'F')
            bank_pool[0] = 'f'
            us = [(2 * ti) % 3, (2 * ti + 1) % 3, (2 * ti + 2) % 3]
            for ub in range(2):
                wb, wk = w_acquire("U", ub)
                wv = wb.t[:].rearrange("p (k n) -> p k n", k=8)
                for s in range(2):
                    bk = bank()
                    mm(bk.t[:], [(hT.t[:, kc, s * 128:(s + 1) * 128], wv[:, kc, :]) for kc in range(8)], [hT.r, wb.r], [bk.r])
                    dst = u_tm.t[:, us[1 + s], ub * 512:(ub + 1) * 512]
                    S.op("act", lambda e, dst=dst, bk=bk: e.activation(out=dst, in_=bk.t[:], func=AF.Copy),
                         reads=[bk.r], writes=[uslot[us[1 + s]]])
                w_done(wk)
            for pair in range(4):
                bk = bank()
                for kk in range(2):
                    cc = pair * 2 + kk
                    wi = cc // 2
                    for s in range(2):
                        o_ap = bk.t[:, kk * 256 + s * 128: kk * 256 + (s + 1) * 128]
                        pairs = []
                        rds = [CONST]
                        src_prev = us[s]
                        src_cur = us[1 + s]
                        if not (first and s == 0):
                            pairs.append((u_tm.t[:, src_prev, cc * 128:(cc + 1) * 128], pmat.t[:, (8 + wi) * 128:(9 + wi) * 128]))
                            rds.append(uslot[src_prev])
                            dk = 0
                        else:
                            dk = 4
                        pairs.append((u_tm.t[:, src_cur, cc * 128:(cc + 1) * 128], pmat.t[:, (dk + wi) * 128:(dk + wi + 1) * 128]))
                        rds.append(uslot[src_cur])
                        mm(o_ap, pairs, rds, [bk.r])
                dst = pooledT.t[:, pair * 2:pair * 2 + 2, :].rearrange("p a b -> p (a b)")
                S.op("act", lambda e, dst=dst, bk=bk: e.activation(out=dst, in_=bk.t[:], func=AF.Copy),
                     reads=[bk.r], writes=[pooledT.r])

            yield
            for q in range(4):
                S.tag = ('m', ti, 'F')
                bank_pool[0] = 'f'
                gblk, gk = w_acquire("GG", q)
                gv = gblk.t[:].rearrange("p (k n) -> p k n", k=8)
                sblk, sk = w_acquire("S", q)
                sv = sblk.t[:].rearrange("p (k n) -> p k n", k=16)
                for jj in range(2):
                    j = 2 * q + jj
                    bkg = bank()
                    mm(bkg.t[:, 0:256], [(gv[:, kc, jj * 128:(jj + 1) * 128], hT.t[:, kc, :]) for kc in range(8)], [gblk.r, hT.r], [bkg.r])
                    mm(bkg.t[:, 256:512], [(gv[:, kc, 256 + jj * 128:256 + (jj + 1) * 128], hT.t[:, kc, :]) for kc in range(8)],
                       [gblk.r, hT.r], [bkg.r])
                    bkm = bank()
                    gi, dd = divmod(j, 2)
                    mm(bkm.t[:, 0:256], [(wpool.t[:, gi * 2 + c2, dd * 128:(dd + 1) * 128], pooledT.t[:, gi * 2 + c2, :]) for c2 in range(2)],
                       [CONST, pooledT.r], [bkm.r])
                    mm(bkm.t[:, 256:512], [(sv[:, fc, jj * 128:(jj + 1) * 128], ygT.t[:, fc, :]) for fc in range(16)],
                       [sblk.r] + ygT_r, [bkm.r])
                    S.op("act", lambda e, bkg=bkg, j=j: e.activation(out=sg.t[:, 0, :], in_=bkg.t[:, 0:256], func=AF.Sigmoid,
                                                                     bias=bg.t[:, j:j + 1]), reads=[bkg.r, CONST], writes=[sg.r])
                    S.op("act", lambda e, bkg=bkg, j=j: e.activation(out=sg.t[:, 1, :], in_=bkg.t[:, 256:512], func=AF.Sigmoid,
                                                                     bias=bg.t[:, 8 + j:9 + j]), reads=[bkg.r, CONST], writes=[sg.r])
                    S.op("dve", lambda e, bkm=bkm, j=j: e.scalar_tensor_tensor(out=mP.t[:], in0=bkm.t[:, 0:256], scalar=pscale.t[:, j:j + 1],
                                                                               in1=sg.t[:, 0, :], op0=ALU.mult, op1=ALU.mult),
                         reads=[bkm.r, sg.r, CONST], writes=[mP.r])
                    S.op("dve", lambda e, bkm=bkm: e.tensor_tensor(out=mS.t[:], in0=bkm.t[:, 256:512], in1=sg.t[:, 1, :], op=ALU.mult),
                         reads=[bkm.r, sg.r], writes=[mS.r])
                    S.op("pool", lambda e, j=j: e.tensor_tensor(out=mergedT.t[:, j, :], in0=mP.t[:], in1=mS.t[:], op=ALU.add),
                         reads=[mP.r, mS.r], writes=[mergedT.r])
                yield

            S.tag = ('m', ti, 'G')
            bank_pool[0] = 'f'
            for half in range(2):
                oblk, ok_ = w_acquire("O", half)
                ov = oblk.t[:].rearrange("p (k n) -> p k n", k=8)
                for s in range(2):
                    bk = bank()
                    mm(bk.t[:], [(mergedT.t[:, dc, s * 128:(s + 1) * 128], ov[:, dc, :]) for dc in range(8)], [mergedT.r, oblk.r], [bk.r])
                    seg = xb.t[:, s, half * 512:(half + 1) * 512]
                    S.op("dve", lambda e, seg=seg, bk=bk: e.scalar_tensor_tensor(out=seg, in0=seg, scalar=ALPHA, in1=bk.t[:],
                                                                                 op0=ALU.mult, op1=ALU.add),
                         reads=[xb.r, bk.r], writes=[xb.r])
                w_done(ok_)
            layer_norm(xb, 0)
            transpose_to_hT(xb, h1T)
            yield

        def ffn_gen(ti):
            xb = xt[ti % NXT]
            h1T = hbufs[ti % 3]

            for ub in range(8):
                S.tag = ('f', ti, 'I')
                bank_pool[0] = 'f'
                ublk, uk = w_acquire("UP", ub)
                uv = ublk.t[:].rearrange("p (k n) -> p k n", k=8)
                for pr in range(2):
                    bk = bank()
                    for kk in range(2):
                        fcl = pr * 2 + kk
                        mm(bk.t[:, kk * 256:(kk + 1) * 256], [(uv[:, kc, fcl * 128:(fcl + 1) * 128], h1T.t[:, kc, :]) for kc in range(8)],
                           [ublk.r, h1T.r], [bk.r])
                    fc0 = ub * 4 + pr * 2
                    rt = sg.t[:].rearrange("p a b -> p (a b)")
                    S.op("act", lambda e, bk=bk, rt=rt: e.activation(out=rt, in_=bk.t[:], func=AF.Relu), reads=[bk.r], writes=[sg.r])
                    S.op("dve", lambda e, fc0=fc0, rt=rt: e.tensor_tensor(out=aT.t[:, fc0:fc0 + 2, :].rearrange("p a b -> p (a b)"),
                                                                        in0=rt, in1=rt, op=ALU.mult),
                         reads=[sg.r], writes=[aT.r])
                yield

            for half in range(2):
                S.tag = ('f', ti, 'J')
                bank_pool[0] = 'f'
                accs = [bank() for _ in range(2)]
                for j in range(4):
                    dblk, dk_ = w_acquire("DN", half * 4 + j)
                    dv = dblk.t[:].rearrange("p (k n) -> p k n", k=8)
                    for s in range(2):
                        ab_ = accs[s]

                        for f0 in range(0, 8, 4):
                            def fn(e, ab_=ab_, j=j, s=s, dv=dv, f0=f0):
                                ins = None
                                for f in range(f0, f0 + 4):
                                    ins = e.matmul(out=ab_.t[:], lhsT=aT.t[:, j * 8 + f, s * 128:(s + 1) * 128], rhs=dv[:, f, :],
                                                   start=(j == 0 and f == 0), stop=(j == 3 and f == 7))
                                return ins
                            S.op("pe", fn, reads=[aT.r, dblk.r], writes=[ab_.r])
                for s in range(2):
                    seg = xb.t[:, s, half * 512:(half + 1) * 512]
                    ab_ = accs[s]
                    S.op("dve", lambda e, seg=seg, ab_=ab_: e.scalar_tensor_tensor(out=seg, in0=seg, scalar=ALPHA, in1=ab_.t[:],
                                                                                   op0=ALU.mult, op1=ALU.add),
                         reads=[xb.r, ab_.r], writes=[xb.r])
                yield
            layer_norm(xb, 2)
            dst = out[ti * T:(ti + 1) * T, :].rearrange("(s p) d -> p s d", p=128)
            S.op("sp", lambda e, dst=dst, xb=xb: e.dma_start(out=dst, in_=xb.t[:]), reads=[xb.r], dma=d_o[ti % NXT])

            yield

        for it in range(NT + 1):
            gens = []
            if it < NT:
                gens.append(mixer_gen(it))
            if it >= 1:
                gens.append(ffn_gen(it - 1))
            while gens:
                for gnr in list(gens):
                    try:
                        next(gnr)
                    except StopIteration:
                        gens.remove(gnr)

        if os.environ.get("DBGW"):
            dbg_w = nc.dram_tensor("dbg_w", [NB, 128, 4096], BF16, kind="ExternalOutput").ap()
            d_dbg = S.dma_sem("dbg")
            S.op("sp", lambda e: e.dma_start(out=dbg_w, in_=wsc), reads=[r for pair in wres for r in pair], dma=d_dbg)
        if LIST_SCHED:
            S.schedule()
        S.finalize()
        S.check_deadlock()
        with nc.Block() as block:
            S.emit(block)
        with nc.Block() as block2:
            @block2.sync
            def _(e):
                if os.environ.get("DBGW"):
                    e.wait_ge(S.sems[d_dbg], S.final_counts[d_dbg])
                for i in range(3):
                    if d_o[i] in S.final_counts:
                        e.wait_ge(S.sems[d_o[i]], S.final_counts[d_o[i]])
    return nc


def _pool_mats():
    pm = np.zeros((12, 128, 128), np.float32)
    tt = np.arange(128)
    for wi, w in enumerate((2, 4, 8, 16)):
        d = tt[None, :] - tt[:, None]
        inwin = (d >= 0) & (d < w)
        pm[wi] = inwin * (1.0 / w) - np.eye(128)
        cnt = np.minimum(tt + 1, w).astype(np.float32)
        pm[4 + wi] = inwin * (1.0 / cnt)[None, :] - np.eye(128)
        dp = tt[None, :] + 128 - tt[:, None]
        pm[8 + wi] = ((dp >= 0) & (dp < w)) * (1.0 / w)
    return np.ascontiguousarray(pm.transpose(1, 0, 2).reshape(128, 12 * 128))


def _mask_c():
    tt = np.arange(128)
    tri = (tt[None, :] >= tt[:, None]).astype(np.float32)
    return np.ascontiguousarray(np.concatenate([tri, np.ones((128, 128), np.float32), tri], axis=1))


def _kblock(w, cols):
    sub = w[:, cols]
    kcs = sub.shape[0] // 128
    return np.ascontiguousarray(sub.reshape(kcs, 128, -1).transpose(1, 0, 2).reshape(128, -1))


def _prep_shared(inp):
    w_in = inp["w_in"][0]
    ar = np.arange
    blocks = []
    for kind, i in WB:
        if kind == "U":
            blk = _kblock(w_in, 512 * i + ar(512))
        elif kind == "Z":
            blk = _kblock(w_in, 1024 + 512 * i + ar(512))
        elif kind == "X":
            cols = np.concatenate([3072 + 256 * i + ar(256), 5120 + 128 * i + ar(128), 6144 + 128 * i + ar(128)])
            blk = _kblock(w_in, cols)
        elif kind == "GG":
            cols = np.concatenate([7200 + 256 * i + ar(256), 8224 + 256 * i + ar(256)])
            blk = _kblock(w_in, cols)
        elif kind == "S":
            blk = _kblock(inp["w_ssd_proj"][0], 256 * i + ar(256))
        elif kind == "O":
            blk = _kblock(inp["w_out"][0], 512 * i + ar(512))
        elif kind == "UP":
            blk = _kblock(inp["w_up"][0], 512 * i + ar(512))
        elif kind == "DN":
            half, j = divmod(i, 4)
            blk = _kblock(inp["w_down"][0][1024 * j:1024 * (j + 1), :], 512 * half + ar(512))
        assert blk.shape == (128, 4096), (kind, blk.shape)
        blocks.append(blk)
    wall = np.ascontiguousarray(np.stack(blocks, 0), dtype=np.float32)
    wdt_h = _kblock(w_in, 7168 + ar(32))
    wp = inp["w_pool_group"][0]
    wpool_h = np.ascontiguousarray(wp.reshape(4, 2, 128, 256).transpose(2, 0, 1, 3).reshape(128, 2048))
    cw = inp["conv_w"][0]
    convw_h = np.ascontiguousarray(cw.reshape(4, 32, 128).transpose(2, 1, 0).reshape(128, 128))
    convb_h = np.ascontiguousarray(inp["conv_b"][0].reshape(32, 128).T)
    bg_h = np.ascontiguousarray(inp["b_gates"][0].reshape(16, 128).T)
    pscale_h = np.ascontiguousarray(inp["pool_scale"][0].reshape(8, 128).T)
    normw_h = np.ascontiguousarray(inp["ssd_norm_w"][0].reshape(16, 128).T)
    dtb_h = np.ascontiguousarray(inp["dt_bias"][0].reshape(32, 1))
    alog_h = np.ascontiguousarray(inp["a_log"][0].reshape(32, 1))
    drep_h = np.ascontiguousarray(np.broadcast_to(inp["d_skip"][0].reshape(1, 8, 1, 4), (128, 8, 2, 4)).reshape(128, 64))
    ln_h = np.ascontiguousarray(np.stack([inp["ln1_g"][0], inp["ln1_b"][0], inp["ln2_g"][0], inp["ln2_b"][0]], 0))
    shared = dict(wall=wall, wdt_h=wdt_h, wpool_h=wpool_h, pmat_h=_pool_mats(), maskc_h=_mask_c(),
                  ident_h=np.eye(128, dtype=np.float32), convw_h=convw_h, convb_h=convb_h, bg_h=bg_h, pscale_h=pscale_h,
                  normw_h=normw_h, dtb_h=dtb_h, alog_h=alog_h, drep_h=drep_h, ln_h=ln_h)
    return {k: np.ascontiguousarray(v, dtype=np.float32) for k, v in shared.items()}


_NC_CACHE = {}


def kernel(**inputs):
    inp = {k: np.asarray(v, dtype=np.float32) for k, v in inputs.items()}
    x = inp["x"]
    shared = _prep_shared(inp)
    if "nc" not in _NC_CACHE:
        _NC_CACHE["nc"] = build_nc(NT_FULL)
    nc = _NC_CACHE["nc"]
    in_maps = []
    for c in range(NCORES):
        m = dict(shared)
        m["xin"] = np.ascontiguousarray(x[NSEQ * c:NSEQ * (c + 1)].reshape(NSEQ * SEQ, D))
        in_maps.append(m)
    res = run_bass_kernel_spmd(nc, in_maps, core_ids=list(range(NCORES)))
    outs = [np.asarray(r["out"]).reshape(NSEQ, SEQ, D) for r in res.results]
    return np.concatenate(outs, axis=0).astype(np.float32)
```

```python
import numpy as np
from contextlib import ExitStack
import concourse.bass as bass
import concourse.mybir as mybir
from concourse.bass_utils import run_bass_kernel_spmd

F32 = mybir.dt.float32
BF16 = mybir.dt.bfloat16
AF = mybir.ActivationFunctionType
ALU = mybir.AluOpType

NCORES = 8
SEQ = 2048
D = 1024
T = 256
NCH = SEQ // T
NSEQ = 4
NT_FULL = NSEQ * NCH
ALPHA = float(2.0 ** 0.25)
LN_EPS = 1e-5
RMS_EPS = 1e-5
import os
NSLOT = 5
NM_SLOT = int(os.environ.get('NM_SLOT', '3'))
SEQ_E = bool(int(os.environ.get('SEQ_E', '0')))
TG = os.environ.get('TG', 'p')
HOP_LAT = float(os.environ.get('HOP_LAT', '0.6'))
SAME_LAT = float(os.environ.get('SAME_LAT', '0.1'))
ACT_SCALE = float(os.environ.get('ACT_SCALE', '1.0'))
PRIO_MODE = os.environ.get('PRIO_MODE', 'blevel')
MM_CHUNK = int(os.environ.get('MM_CHUNK', '4'))
TABLE_AWARE = int(os.environ.get('TABLE_AWARE', '1'))
TBL_US = 1.28
TBL_PEN = float(os.environ.get('TBL_PEN', '1.28'))
PE_SCALE = float(os.environ.get('PE_SCALE', '0.9'))
DVE_SCALE = float(os.environ.get('DVE_SCALE', '0.9'))
LIST_SCHED = bool(int(os.environ.get('LIST_SCHED', '1')))

WB = [("U", 0), ("U", 1)]
for _j in range(4):
    WB += [("Z", _j), ("X", 2 * _j), ("X", 2 * _j + 1)]
for _q in range(4):
    WB += [("GG", _q), ("S", _q)]
WB += [("O", 0), ("O", 1)] + [("UP", i) for i in range(8)] + [("DN", i) for i in range(8)]
NB = len(WB)


class Res:
    __slots__ = ("name", "last_w", "readers")

    def __init__(self, name):
        self.name = name
        self.last_w = None
        self.readers = []


class Op:
    __slots__ = ("eng", "fn", "deps", "signal", "sigval", "owner", "inc", "is_dma", "idx", "cost", "users", "nbytes", "tag", "t0", "t1", "prio", "family")


class _CostProbe:
    def __init__(self, eng):
        self.eng = eng
        self.cost = 0.0
        self.nbytes = 0
        self.family = None

    def __getattr__(self, name):
        def f(*a, **kw):
            if name == "matmul":
                r = kw["rhs"]
                mult = 4.0 if r.dtype == F32 else 1.0
                self.cost += (r.free_size() * mult / 2200.0 + 0.02) * PE_SCALE
            elif name == "transpose":
                self.cost += 0.27 if kw["in_"].dtype == F32 else 0.1
            elif name == "dma_start":
                self.nbytes += kw["out"].nbytes()
            elif name == "nop":
                self.cost += 0.05
            else:
                if name == "activation":
                    fnm = str(kw.get("func", a[2] if len(a) > 2 else "")).split(".")[-1]
                    fam = {"Exp": "E", "Ln": "E", "Silu": "S", "Sigmoid": "G"}.get(fnm)
                    if fam is not None:
                        self.family = fam
                out = kw.get("out", None)
                if out is None:
                    out = kw.get("ap", a[0] if a else None)
                cols = out.free_size() if out is not None else 256
                if self.eng == "act":
                    self.cost += (0.25 + cols / 1400.0) * ACT_SCALE
                elif self.eng == "dve":
                    self.cost += (0.15 + cols / 800.0) * DVE_SCALE
                else:
                    self.cost += 0.3 + cols * 0.0032
            return self
        return f


class Sched:
    ENGS = ("pe", "act", "dve", "pool", "sp")
    PRIO = {"BCD": 0, "E": 0, "F": 1, "G": 1, "I": 2, "J": 2, "PRE": 3}
    DEFAULT_COST = {"pe": 0.15, "act": 0.45, "dve": 0.4, "pool": 0.6, "sp": 0.05}

    def __init__(self, nc, es):
        self.nc = nc
        self.es = es
        self.engs = {k: [] for k in self.ENGS}
        self.sems = {}
        self.all_ops = []
        for k in self.ENGS:
            self.sems[k] = es.enter_context(nc.semaphore("s_" + k))

    def dma_sem(self, name):
        s = self.es.enter_context(self.nc.semaphore("d_" + name))
        self.sems["d_" + name] = s
        return "d_" + name

    def op(self, eng, fn, reads=(), writes=(), dma=None, cost=None, nbytes=0):
        if eng == 'pool' and 'p' in TG:
            eng = 'dve'
        if eng == 'pool!':
            eng = 'pool' if 'q' in TG else 'dve'
        o = Op()
        o.family = None
        if cost is None:
            pr = _CostProbe(eng)
            fn(pr)
            cost = max(pr.cost, 0.03)
            nbytes = pr.nbytes
            o.family = pr.family
        o.cost = cost
        o.nbytes = nbytes
        o.users = []
        o.tag = getattr(self, 'tag', None)
        o.prio = self.PRIO.get(o.tag[2], 1) if o.tag else 1
        o.t0 = o.t1 = 0.0
        o.eng = eng
        o.fn = fn
        o.is_dma = dma is not None
        o.owner = dma if dma is not None else eng
        o.inc = 16 if dma is not None else 1
        o.signal = dma is not None
        o.sigval = None
        deps = []
        for r in reads:
            if r.last_w is not None:
                deps.append(r.last_w)
        for w in writes:
            if w.last_w is not None:
                deps.append(w.last_w)
            deps.extend(w.readers)
        for r in reads:
            r.readers.append(o)
        for w in writes:
            w.last_w = o
            w.readers = []
        o.deps = [d for d in set(deps) if d is not o]
        o.idx = len(self.all_ops)
        self.all_ops.append(o)
        self.engs[eng].append(o)
        return o

    @staticmethod
    def _needs_wait(o, d):
        if d.is_dma or o.is_dma:
            return True
        if d.eng != o.eng:
            return True
        return o.eng != "pe"

    def finalize(self):
        for o in self.all_ops:
            for d in o.deps:
                if self._needs_wait(o, d):
                    d.signal = True
        cnt = {}
        for eng in self.ENGS:
            for o in self.engs[eng]:
                if o.signal:
                    cnt[o.owner] = cnt.get(o.owner, 0) + o.inc
                    o.sigval = cnt[o.owner]
        self.final_counts = cnt

    def schedule(self):
        ops = self.all_ops
        for o in ops:
            for d in o.deps:
                d.users.append(o)
        unmet = [len(o.deps) for o in ops]
        blevel = [0.0] * len(ops)
        if PRIO_MODE == "blevel":
            for o in reversed(ops):
                b = 0.0
                for u in o.users:
                    lat = SAME_LAT if (u.eng == o.eng and not o.is_dma) else HOP_LAT
                    if blevel[u.idx] + lat > b:
                        b = blevel[u.idx] + lat
                blevel[o.idx] = b + (2.0 + o.nbytes / 280e3 if o.is_dma else o.cost)
            for o in ops:
                o.prio = -blevel[o.idx]
        ready_t = [0.0] * len(ops)
        fin_t = [0.0] * len(ops)
        free = {k: 0.0 for k in self.ENGS}
        ready = {k: [] for k in self.ENGS}
        for o in ops:
            if unmet[o.idx] == 0:
                ready[o.eng].append(o)
        newq = {k: [] for k in self.ENGS}
        dma_free = 0.0
        act_fam = [None]
        n_done = 0
        total = len(ops)
        while n_done < total:
            best = None
            for k in self.ENGS:
                if not ready[k]:
                    continue
                f = free[k]
                if k == "act" and TABLE_AWARE:
                    cand = min(ready[k], key=lambda o: (max(f, ready_t[o.idx]) + (TBL_PEN if (TABLE_AWARE == 1 and o.family and o.family != act_fam[0]) else 0.0),
                                                        o.prio, o.idx))
                    st = max(f, ready_t[cand.idx]) + (TBL_US if (cand.family and cand.family != act_fam[0]) else 0.0)
                else:
                    cand = min(ready[k], key=lambda o: (max(f, ready_t[o.idx]), o.prio, o.idx))
                    st = max(f, ready_t[cand.idx])
                if best is None or (st, cand.idx) < (best[0], best[1].idx):
                    best = (st, cand)
            st, o = best
            if o.eng == 'act' and TABLE_AWARE and o.family:
                act_fam[0] = o.family
            ready[o.eng].remove(o)
            newq[o.eng].append(o)
            if o.is_dma:
                free[o.eng] = st + 0.06
                t0 = max(st, dma_free)
                dma_free = t0 + o.nbytes / 280e3
                fin = dma_free + 2.0
            else:
                fin = st + o.cost
                free[o.eng] = fin
            fin_t[o.idx] = fin
            o.t0, o.t1 = st, fin
            n_done += 1
            for u in o.users:
                unmet[u.idx] -= 1
                lat = (0.0 if o.eng == 'pe' else SAME_LAT) if (u.eng == o.eng and not o.is_dma) else HOP_LAT
                if fin + lat > ready_t[u.idx]:
                    ready_t[u.idx] = fin + lat
                if unmet[u.idx] == 0:
                    ready[u.eng].append(u)
        self.engs = newq
        self.sim_time = max(fin_t) if fin_t else 0.0

    def check_deadlock(self):
        pos = {k: 0 for k in self.engs}
        val = {}
        progress = True
        while progress:
            progress = False
            for k, ops in self.engs.items():
                while pos[k] < len(ops):
                    o = ops[pos[k]]
                    ok = all((not self._needs_wait(o, d)) or val.get(d.owner, 0) >= d.sigval for d in o.deps)
                    if not ok:
                        break
                    if o.signal:
                        val[o.owner] = val.get(o.owner, 0) + o.inc
                        assert val[o.owner] == o.sigval, (o.owner, val[o.owner], o.sigval)
                    pos[k] += 1
                    progress = True
        stuck = {k: (pos[k], len(ops)) for k, ops in self.engs.items() if pos[k] < len(ops)}
        assert not stuck, "deadlock: %s" % stuck

    def emit(self, block):
        sems = self.sems

        def run(eng_name, e):
            waited = {}
            for o in self.engs[eng_name]:
                need = {}
                for d in o.deps:
                    if not self._needs_wait(o, d):
                        continue
                    if d.sigval > waited.get(d.owner, 0):
                        need[d.owner] = max(need.get(d.owner, 0), d.sigval)
                for owner, v in need.items():
                    e.wait_ge(sems[owner], v)
                    waited[owner] = v
                ins = o.fn(e)
                if o.signal:
                    ins.then_inc(sems[o.owner], o.inc)

        @block.tensor
        def _(e):
            run("pe", e)

        @block.scalar
        def _(e):
            run("act", e)

        @block.vector
        def _(e):
            run("dve", e)

        @block.gpsimd
        def _(e):
            run("pool", e)

        @block.sync
        def _(e):
            run("sp", e)


class Buf:
    __slots__ = ("t", "r", "tb")

    def __init__(self, t, name):
        self.t = t
        self.r = Res(name)
        self.tb = None


def build_nc(NT=NT_FULL, PH=99):
    nc = bass.Bass("TRN2", target_bir_lowering=False)

    def din(name, shape, dt=F32):
        return nc.dram_tensor(name, shape, dt, kind="ExternalInput").ap()

    xin = din("xin", [NT * T, D])
    wall = din("wall", [NB, 128, 4096])
    wdt_h = din("wdt_h", [128, 256])
    wpool_h = din("wpool_h", [128, 2048])
    pmat_h = din("pmat_h", [128, 12 * 128])
    maskc_h = din("maskc_h", [128, 384])
    ident_h = din("ident_h", [128, 128])
    convw_h = din("convw_h", [128, 128])
    convb_h = din("convb_h", [128, 32])
    bg_h = din("bg_h", [128, 16])
    pscale_h = din("pscale_h", [128, 8])
    normw_h = din("normw_h", [128, 16])
    dtb_h = din("dtb_h", [32, 1])
    alog_h = din("alog_h", [32, 1])
    drep_h = din("drep_h", [128, 64])
    ln_h = din("ln_h", [4, 1024])
    out = nc.dram_tensor("out", [NT * T, D], F32, kind="ExternalOutput").ap()
    wsc = nc.dram_tensor("wsc", [NB, 128, 4096], BF16).ap()
    acs_scr = nc.dram_tensor("acs_scr", [NT, 32, 256], F32).ap()

    with ExitStack() as es:
        S = Sched(nc, es)

        def sb(name, shape, dt=F32):
            return Buf(es.enter_context(nc.sbuf_tensor(name, shape, dt)), name)

        ident = sb("ident", [128, 128])
        identb = sb("identb", [128, 128], BF16)
        pmat = sb("pmat", [128, 12 * 128], BF16)
        maskc = sb("maskc", [128, 384])
        convw = sb("convw", [128, 128])
        convb = sb("convb", [128, 32])
        bg = sb("bg", [128, 16])
        pscale = sb("pscale", [128, 8])
        normw = sb("normw", [128, 16])
        dtb = sb("dtb", [32, 1])
        aneg = sb("aneg", [32, 1])
        drep = sb("drep", [128, 64])
        lnrep = sb("lnrep", [128, 4, 1024])
        wdt = sb("wdt", [128, 8, 32], BF16)
        wpool = sb("wpool", [128, 8, 256], BF16)
        ones32 = sb("ones32", [32, 256])
        CONST = Res("consts_ready")

        ring = [sb("wring%d" % i, [128, 4096], BF16) for i in range(NSLOT)]
        NXT = int(os.environ.get("NXT", "3"))
        xt = [sb("xt%d" % i, [128, 2, 1024]) for i in range(NXT)]
        hT = sb("hT", [128, 8, 256], BF16)
        h1T = sb("h1T", [128, 8, 256], BF16)
        h2T = sb("h2T", [128, 8, 256], BF16)
        hbufs = [hT, h1T, h2T]
        u_tm = sb("u_tm", [128, 3, 1024], BF16)
        uslot = [Res("uslot%d" % i) for i in range(3)]
        pooledT = sb("pooledT", [128, 8, 256], BF16)
        zs = [sb("zs%d" % i, [128, 2, 256], BF16) for i in range(3)]
        xr = [sb("xr%d" % i, [128, 2, 259]) for i in range(2)]
        xr_h = [Res("xr_h%d" % h) for h in range(2)]
        xr_c = [Res("xr_c%d" % h) for h in range(2)]
        carry = sb("carry", [128, 8, 4, 3])
        carry_r = [Res("carry%d" % g) for g in range(8)]
        cacc = [sb("cacc0", [128, 4, 256])] * 2
        cacc_r = [[Res("cacc_%d" % c) for c in range(4)]] * 2
        xc = [sb("xc0", [128, 4, 256], BF16)] * 2
        xs_tm = [sb("xs_tm%d" % i, [128, 2, 256], BF16) for i in range(2)]
        xw_tm = [sb("xw_tm%d" % i, [128, 2, 256], BF16) for i in range(2)]
        xd_tm = [sb("xd_tm%d" % i, [128, 2, 256], BF16) for i in range(2)]
        b_tm = [sb("b_tm%d" % i, [128, 2, 128], BF16) for i in range(2)]
        dt_e = sb("dt_e", [32, 256])
        dt_v = sb("dt_v", [32, 256])
        dt_ecs = sb("dt_ecs", [32, 256])
        dt_q = sb("dt_q", [32, 3, 256])
        dgm = sb("dgm", [32, 32])
        tmq2 = [sb("tmq%d" % i, [128, 3, 8, 2, 4]) for i in range(2)]
        cdB2 = [sb("cdB%d" % i, [128, 32]) for i in range(2)]
        acsB = [sb("acsB%d" % i, [128, 2, 256]) for i in range(2)]
        cbm = [sb("cbm%d" % i, [128, 384]) for i in range(2)]
        lt = [sb("lt%d" % i, [128, 384]) for i in range(2)]
        lt_r = [[Res("lt%d_%d" % (i, c)) for c in range(2)] for i in range(2)]
        mt = [sb("mt%d" % i, [128, 384], BF16) for i in range(2)]
        Sst = sb("Sst", [128, 2048])
        Sst_r = [Res("Sst%d" % g) for g in range(8)]
        Sbf_r = [Res("Sbf%d" % g) for g in range(8)]
        Sbf = sb("Sbf", [128, 2048], BF16)
        ytmp = [sb("ytmp%d" % i, [128, 2, 256]) for i in range(2)]
        ybuf = [sb("ybuf0", [128, 2, 256])] * 2
        sqj = sb("sqj", [128, 256], BF16)
        ssq = [sb("ssq%d" % i, [128, 2]) for i in range(2)]
        rstd = [sb("rstd%d" % i, [128, 2]) for i in range(2)]
        ygs = [sb("ygs%d" % i, [128, 2, 256], BF16) for i in range(2)]
        ygs_r = [[Res("ygs%d_%d" % (i, c)) for c in range(2)] for i in range(2)]
        ygT = sb("ygT", [128, 16, 256], BF16)
        ygT_r = [Res("ygT%d" % g) for g in range(8)]
        sg = sb("sg", [128, 2, 256])
        mP = sb("mP", [128, 256])
        mS = sb("mS", [128, 256])
        mergedT = sb("mergedT", [128, 8, 256], BF16)
        aT_f32 = sb("aT", [128, 4096])
        aT = Buf(aT_f32.t[:].bitcast(BF16).rearrange("p (a b) -> p a b", a=32), "aT_bf")
        aT.r = aT_f32.r
        lnst = sb("lnst", [128, 2, 6])
        lnmv = sb("lnmv", [128, 2])
        lnr = sb("lnr", [128, 1])

        banks = []
        for i in range(8):
            b = Buf(es.enter_context(nc.psum_tensor("psb%d" % i, [128, 512], F32)), "psb%d" % i)
            b.tb = b.t[:].bitcast(BF16)
            banks.append(b)
        POOLS = {'b': [0], 'e': [1, 2, 3, 4, 5], 'f': [6, 7]}
        bank_ctr = {k: 0 for k in POOLS}
        bank_pool = ["e"]

        def bank():
            p = bank_pool[0]
            lst = POOLS[p]
            b = banks[lst[bank_ctr[p] % len(lst)]]
            bank_ctr[p] += 1
            return b

        d_ps = [S.dma_sem("ps%d" % i) for i in range(NSLOT)]
        d_c = S.dma_sem("c")
        d_w = [S.dma_sem("w%d" % i) for i in range(NSLOT)]
        d_x = [S.dma_sem("x%d" % i) for i in range(3)]
        d_o = [S.dma_sem("o%d" % i) for i in range(3)]
        d_aw = S.dma_sem("aw")
        d_ab = [S.dma_sem("ab%d" % i) for i in range(2)]

        pre_res = []

        def cload(eng, dst_ap, src_ap, sem, name):
            r = Res(name)
            pre_res.append(r)
            S.op(eng, lambda e: e.dma_start(out=dst_ap, in_=src_ap), writes=[r], dma=sem)

        cload("sp", ident.t[:], ident_h, d_c, "c_ident")
        cload("sp", maskc.t[:], maskc_h, d_c, "c_maskc")
        cload("sp", convw.t[:], convw_h, d_c, "c_convw")
        cload("sp", convb.t[:], convb_h, d_c, "c_convb")
        cload("sp", bg.t[:], bg_h, d_c, "c_bg")
        cload("sp", pscale.t[:], pscale_h, d_c, "c_pscale")
        cload("sp", normw.t[:], normw_h, d_c, "c_normw")
        cload("sp", dtb.t[:], dtb_h, d_c, "c_dtb")
        cload("sp", aneg.t[:], alog_h, d_c, "c_alog")
        cload("sp", drep.t[:], drep_h, d_c, "c_drep")
        for i in range(4):
            cload("sp", lnrep.t[:, i, :], ln_h[i, :].partition_broadcast(128), d_c, "c_ln%d" % i)
        def stage_cast(dst_ap, src_ap, ncols, name):
            stv = xt[0].t[:].rearrange("p a b -> p (a b)")[:, 0:ncols]
            S.op("sp", lambda e: e.dma_start(out=stv, in_=src_ap), writes=[xt[0].r], dma=d_x[0])
            r = Res(name)
            pre_res.append(r)
            S.op("dve", lambda e: e.tensor_copy(out=dst_ap, in_=stv), reads=[xt[0].r], writes=[r])

        stage_cast(wdt.t[:].rearrange("p a b -> p (a b)"), wdt_h, 256, "c_wdt")
        stage_cast(wpool.t[:].rearrange("p a b -> p (a b)"), wpool_h, 2048, "c_wpool")
        stage_cast(pmat.t[:], pmat_h, 1536, "c_pmat")
        wres = [[Res("wsc%d_%d" % (b, hf)) for hf in range(2)] for b in range(NB)]
        S.tag = ('p', 0, 'PRE')
        st_f32 = aT_f32.t[:]
        st_r = [Res("stg%d" % i) for i in range(2)]
        xt2b = xt[2].t[:].rearrange("p a b -> p (a b)").bitcast(BF16)
        cast_dst = [xt2b[:, 0:2048], xt2b[:, 2048:4096], h1T.t[:].rearrange("p a b -> p (a b)")]
        if 'B' in TG:
            cast_dst = [ring[0].t[:, 0:2048], ring[1].t[:, 0:2048], ring[2].t[:, 0:2048]]
            cast_r = [ring[0].r, ring[1].r, ring[2].r]
        cast_r = [Res("cst%d" % i) for i in range(3)]
        if 'B' in TG:
            cast_r = [ring[0].r, ring[1].r, ring[2].r]
        d_st = [S.dma_sem("st%d" % i) for i in range(2)]
        cast_engs = ("act", "dve")
        for b in range(NB):
            for hf in range(2):
                i = 2 * b + hf
                stv = st_f32[:, (i % 2) * 2048:(i % 2 + 1) * 2048]
                S.op("sp", lambda e, stv=stv, b=b, hf=hf: e.dma_start(out=stv, in_=wall[b][:, hf * 2048:(hf + 1) * 2048]),
                     writes=[st_r[i % 2]], dma=d_st[i % 2])
                cs = i % 3
                dstv = cast_dst[cs]
                eng = cast_engs[i % 2]
                if eng == "act":
                    S.op("act", lambda e, dstv=dstv, stv=stv: e.activation(out=dstv, in_=stv, func=AF.Copy),
                         reads=[st_r[i % 2]], writes=[cast_r[cs]])
                else:
                    S.op("dve", lambda e, dstv=dstv, stv=stv: e.tensor_copy(out=dstv, in_=stv), reads=[st_r[i % 2]], writes=[cast_r[cs]])
                S.op("sp", lambda e, dstv=dstv, b=b, hf=hf: e.dma_start(out=wsc[b][:, hf * 2048:(hf + 1) * 2048], in_=dstv),
                     reads=[cast_r[cs]], writes=[wres[b][hf]], dma=d_ps[cs])
        S.op("dve", lambda e: e.nop(nofuse=True), writes=st_r + [aT.r])
        S.op("dve", lambda e: e.nop(nofuse=True), writes=cast_r[0:2] + [xt[2].r])
        S.op("dve", lambda e: e.nop(nofuse=True), writes=cast_r[2:3] + [h1T.r])
        S.tag = None
        S.op("sp", lambda e: e.nop(nofuse=True), reads=pre_res, writes=[CONST])
        S.op("dve", lambda e: e.tensor_copy(out=identb.t[:], in_=ident.t[:]), reads=[CONST], writes=[identb.r])
        S.op("act", lambda e: e.activation(out=aneg.t[:], in_=aneg.t[:], func=AF.Exp), reads=[CONST], writes=[aneg.r])
        S.op("dve", lambda e: e.tensor_scalar(out=aneg.t[:], in0=aneg.t[:], scalar1=-1.0, scalar2=None, op0=ALU.mult),
             reads=[aneg.r], writes=[aneg.r])
        S.op("dve", lambda e: e.memset(ones32.t[:], 1.0), writes=[ones32.r])

        wstate = {"m": 0, "f": 0}

        def w_acquire(kind, idx):
            st = "f" if kind in ("UP", "DN") else "m"
            k = wstate[st]
            wstate[st] = k + 1
            slot = (k % NM_SLOT) if st == "m" else NM_SLOT + (k % (NSLOT - NM_SLOT))
            bidx = WB.index((kind, idx))
            src = wsc[bidx]
            dst = ring[slot].t[:]
            S.op("sp", lambda e: e.dma_start(out=dst, in_=src), reads=wres[bidx], writes=[ring[slot].r], dma=d_w[slot])
            return ring[slot], k

        def w_done(k):
            pass

        def mm(out_ap, pairs, reads, writes):
            pairs = list(pairs)
            n = len(pairs)
            last = None
            for c0 in range(0, n, MM_CHUNK):
                sub = list(enumerate(pairs))[c0:c0 + MM_CHUNK]

                def fn(e, sub=sub):
                    ins = None
                    for i, (l, r) in sub:
                        ins = e.matmul(out=out_ap, lhsT=l, rhs=r, start=(i == 0), stop=(i == n - 1))
                    return ins
                last = S.op("pe", fn, reads=reads, writes=writes)
            return last

        def tr(out_ap, in_ap, idt_ep, reads, writes):
            return S.op("pe", lambda e: e.transpose(out=out_ap, in_=in_ap, identity=idt_ep), reads=reads, writes=writes)

        def x_load(ti):
            xb = xt[ti % NXT]
            src = xin[ti * T:(ti + 1) * T, :].rearrange("(s p) d -> p s d", p=128)
            S.op("sp", lambda e: e.dma_start(out=xb.t[:], in_=src), reads=([aT.r, h1T.r] if (ti == 0 and 'W' in TG) else []),
                 writes=[xb.r], dma=d_x[ti % NXT])

        def transpose_to_hT(xb, hdst):
            for pair in range(4):
                bk = bank()
                for kk in range(2):
                    kc = pair * 2 + kk
                    for s in range(2):
                        tr(bk.t[:, kk * 256 + s * 128: kk * 256 + (s + 1) * 128], xb.t[:, s, kc * 128:(kc + 1) * 128],
                           ident.t[:], [xb.r, CONST], [bk.r])
                dst = hdst.t[:, pair * 2:pair * 2 + 2, :].rearrange("p a b -> p (a b)")
                S.op("act", lambda e, dst=dst, bk=bk: e.activation(out=dst, in_=bk.t[:], func=AF.Copy),
                     reads=[bk.r], writes=[hdst.r])

        def layer_norm(xb, gi):
            for s in range(2):
                row = xb.t[:, s, :]
                for hh in range(2):
                    S.op("dve", lambda e, hh=hh, row=row: e.bn_stats(out=lnst.t[:, hh, :], in_=row[:, hh * 512:(hh + 1) * 512]),
                         reads=[xb.r], writes=[lnst.r])
                S.op("dve", lambda e: e.bn_aggr(out=lnmv.t[:], in_=lnst.t[:].rearrange("p a b -> p (a b)")),
                     reads=[lnst.r], writes=[lnmv.r])
                S.op("act", lambda e: e.activation(out=lnr.t[:], in_=lnmv.t[:, 1:2], func=AF.Ln, bias=LN_EPS),
                     reads=[lnmv.r], writes=[lnr.r])
                S.op("act", lambda e: e.activation(out=lnr.t[:], in_=lnr.t[:], func=AF.Exp, scale=-0.5),
                     reads=[lnr.r], writes=[lnr.r])
                S.op("dve", lambda e: e.tensor_scalar(out=lnmv.t[:, 0:1], in0=lnmv.t[:, 0:1], scalar1=lnr.t[:, 0:1], scalar2=-1.0,
                                                      op0=ALU.mult, op1=ALU.mult), reads=[lnmv.r, lnr.r], writes=[lnmv.r])
                S.op("act", lambda e, row=row: e.activation(out=row, in_=row, func=AF.Identity, scale=lnr.t[:, 0:1], bias=lnmv.t[:, 0:1]),
                     reads=[xb.r, lnmv.r, lnr.r], writes=[xb.r])
                S.op("pool!", lambda e, row=row: e.tensor_tensor(out=row, in0=row, in1=lnrep.t[:, gi, :], op=ALU.mult),
                     reads=[xb.r, CONST], writes=[xb.r])
                S.op("pool!", lambda e, row=row: e.tensor_tensor(out=row, in0=row, in1=lnrep.t[:, gi + 1, :], op=ALU.add),
                     reads=[xb.r, CONST], writes=[xb.r])

        def finish(ti, xb):
            dst = out[ti * T:(ti + 1) * T, :].rearrange("(s p) d -> p s d", p=128)
            S.op("sp", lambda e: e.dma_start(out=dst, in_=xb.t[:]), reads=[xb.r], dma=d_o[ti % 2])
            if ti + 1 < NT and PH < 6:
                x_load(ti + 1)

        def mixer_gen(ti):
            seq, ch = divmod(ti, NCH)
            first = ch == 0
            xb = xt[ti % NXT]
            S.tag = ('m', ti, 'BCD')
            bank_pool[0] = 'b'
            tmq = tmq2[ti % 2]
            cdB = cdB2[ti % 2]
            hT = hbufs[ti % 3]
            h1T = hT
            x_load(ti)
            if first:
                S.op("pool", lambda e: e.memset(Sst.t[:], 0.0), writes=Sst_r)
                S.op("pool", lambda e: e.memset(Sbf.t[:], 0.0), writes=Sbf_r)
                S.op("pool", lambda e: e.memset(carry.t[:], 0.0), writes=carry_r)

            transpose_to_hT(xb, hT)

            bk = bank()
            mm(bk.t[0:32, 0:256], [(wdt.t[:, kc, :], hT.t[:, kc, :]) for kc in range(8)], [CONST, hT.r], [bk.r])
            S.op("act", lambda e, bk=bk: e.activation(out=dt_e.t[:], in_=bk.t[0:32, 0:256], func=AF.Exp, bias=dtb.t[:, 0:1]),
                 reads=[bk.r, CONST], writes=[dt_e.r])
            S.op("act", lambda e: e.activation(out=dt_v.t[:], in_=dt_e.t[:], func=AF.Ln, bias=1.0),
                 reads=[dt_e.r], writes=[dt_v.r])
            S.op("dve", lambda e: e.tensor_scalar(out=dt_e.t[:], in0=dt_v.t[:], scalar1=aneg.t[:, 0:1], scalar2=None, op0=ALU.mult),
                 reads=[dt_v.r, aneg.r], writes=[dt_e.r])
            S.op("dve", lambda e: e.tensor_tensor_scan(out=dt_ecs.t[:], data0=ones32.t[:], data1=dt_e.t[:], initial=0.0,
                                                       op0=ALU.mult, op1=ALU.add),
                 reads=[ones32.r, dt_e.r], writes=[dt_ecs.r])
            scr = acs_scr[ti]
            scr_r = Res("acs_scr%d" % ti)
            S.op("sp", lambda e, scr=scr: e.dma_start(out=scr, in_=dt_ecs.t[:]), reads=[dt_ecs.r], writes=[scr_r], dma=d_aw)

            def bcast_load(gh, scr=scr, scr_r=scr_r):
                ab = acsB[gh % 2]
                src = scr[2 * gh:2 * gh + 2, :].rearrange("h t -> (h t)").partition_broadcast(128)
                S.op("sp", lambda e: e.dma_start(out=ab.t[:].rearrange("p h t -> p (h t)"), in_=src),
                     reads=[scr_r], writes=[ab.r], dma=d_ab[gh % 2])

            S.op("act", lambda e: e.activation(out=dt_e.t[:], in_=dt_v.t[:], func=AF.Ln), reads=[dt_v.r], writes=[dt_e.r])
            S.op("dve", lambda e: e.tensor_tensor(out=dt_q.t[:, 0, :], in0=dt_e.t[:], in1=dt_ecs.t[:], op=ALU.subtract),
                 reads=[dt_e.r, dt_ecs.r], writes=[dt_q.r])
            S.op("act", lambda e: e.activation(out=dt_q.t[:, 1, :], in_=dt_ecs.t[:], func=AF.Exp), reads=[dt_ecs.r], writes=[dt_q.r])
            S.op("act", lambda e: e.activation(out=dt_e.t[:], in_=dt_ecs.t[:], func=AF.Exp, bias=dt_ecs.t[:, 255:256], scale=-1.0),
                 reads=[dt_ecs.r, dt_e.r], writes=[dt_e.r])
            S.op("dve", lambda e: e.tensor_tensor(out=dt_q.t[:, 2, :], in0=dt_e.t[:], in1=dt_v.t[:], op=ALU.mult),
                 reads=[dt_e.r, dt_v.r], writes=[dt_q.r])
            bk = bank()
            for s in range(2):
                for q in range(3):
                    tr(bk.t[:, (s * 3 + q) * 32:(s * 3 + q + 1) * 32], dt_q.t[:, q, s * 128:(s + 1) * 128], ident.t[0:32, 0:32],
                       [dt_q.r, CONST], [bk.r])
            for s in range(2):
                for q in range(3):
                    S.op("dve", lambda e, bk=bk, s=s, q=q: e.tensor_copy(
                        out=tmq.t[:, q, :, s, :], in_=bk.t[:, (s * 3 + q) * 32:(s * 3 + q + 1) * 32].rearrange("p (g h) -> p g h", g=8)),
                         reads=[bk.r], writes=[tmq.r])
            S.op("dve", lambda e: e.tensor_scalar(out=dgm.t[:], in0=ident.t[0:32, 0:32], scalar1=dt_q.t[:, 1, 255:256], scalar2=None,
                                                  op0=ALU.mult), reads=[CONST, dt_q.r], writes=[dgm.r])
            bk = bank()
            mm(bk.t[:, 0:32], [(ones32.t[:, 0:128], dgm.t[:])], [ones32.r, dgm.r], [bk.r])
            S.op("dve", lambda e, bk=bk: e.tensor_copy(out=cdB.t[:], in_=bk.t[:, 0:32]), reads=[bk.r], writes=[cdB.r])

            yield
            zinfo = {}

            def stage_a(g):
                i2 = g % 2
                if g % 2 == 0:
                    zinfo["z"] = w_acquire("Z", g // 2)
                zblk, zk = zinfo["z"]
                zv = zblk.t[:].rearrange("p (k n) -> p k n", k=8)
                bkz = bank()
                for s in range(2):
                    mm(bkz.t[:, s * 256:(s + 1) * 256],
                       [(hT.t[:, kc, s * 128:(s + 1) * 128], zv[:, kc, (g % 2) * 256:(g % 2 + 1) * 256]) for kc in range(8)],
                       [hT.r, zblk.r], [bkz.r])
                zsb = zs[g % 3]
                S.op("act", lambda e: e.activation(out=zsb.t[:].rearrange("p a b -> p (a b)"), in_=bkz.t[:], func=AF.Silu),
                     reads=[bkz.r], writes=[zsb.r])
                xblk, xk = w_acquire("X", g)
                xv = xblk.t[:].rearrange("p (k n) -> p k n", k=8)
                for half in range(2):
                    xrb = xr[half]
                    S.op("dve", lambda e, xrb=xrb, half=half: e.tensor_copy(out=xrb.t[:, :, 0:3], in_=carry.t[:, g, 2 * half:2 * half + 2, :]),
                         reads=[carry_r[g]], writes=[xr_c[half]])
                    bkx = bank()
                    for kk in range(2):
                        ct = half * 2 + kk
                        mm(bkx.t[:, kk * 256:(kk + 1) * 256],
                           [(xv[:, kc, ct * 128:(ct + 1) * 128], hT.t[:, kc, :]) for kc in range(8)], [hT.r, xblk.r], [bkx.r])
                    S.op("act", lambda e, xrb=xrb, bkx=bkx: e.activation(
                        out=xrb.t[:, :, 3:259], in_=bkx.t[:].rearrange("p (a b) -> p a b", a=2), func=AF.Copy),
                         reads=[bkx.r], writes=[xr_h[half]])
                w_done(xk)
                if g % 2 == 1:
                    w_done(zk)
                for half in range(2):
                    xrb = xr[half]
                    S.op("dve", lambda e, xrb=xrb, half=half: e.tensor_copy(out=carry.t[:, g, 2 * half:2 * half + 2, :], in_=xrb.t[:, :, 256:259]),
                         reads=[xr_h[half]], writes=[carry_r[g]])
                cab = cacc[i2]

                def cch_of(ct):
                    return (g * 2 + ct) if ct < 2 else (16 + g if ct == 2 else 24 + g)
                for ct in range(4):
                    cch = cch_of(ct)
                    xrb = xr[ct // 2]
                    S.op("act", lambda e, ct=ct, cch=cch, xrb=xrb: e.activation(
                        out=cab.t[:, ct, :], in_=xrb.t[:, ct % 2, 3:259], func=AF.Identity,
                        bias=convb.t[:, cch:cch + 1], scale=convw.t[:, cch * 4 + 3:cch * 4 + 4]),
                         reads=[xr_h[ct // 2], CONST], writes=[cacc_r[i2][ct]])
                for k in range(3):
                    for ct in range(4):
                        cch = cch_of(ct)
                        xrb = xr[ct // 2]
                        S.op("dve", lambda e, ct=ct, cch=cch, k=k, xrb=xrb: e.scalar_tensor_tensor(
                            out=cab.t[:, ct, :], in0=xrb.t[:, ct % 2, k:k + 256], scalar=convw.t[:, cch * 4 + k:cch * 4 + k + 1],
                            in1=cab.t[:, ct, :], op0=ALU.mult, op1=ALU.add),
                             reads=[xr_h[ct // 2], xr_c[ct // 2], CONST, cacc_r[i2][ct]], writes=[cacc_r[i2][ct]])
                xcb = xc[i2]
                S.op("act", lambda e: e.activation(out=xcb.t[:].rearrange("p a b -> p (a b)"),
                                                   in_=cab.t[:].rearrange("p a b -> p (a b)"), func=AF.Silu),
                     reads=cacc_r[i2], writes=[xcb.r])

            def stage_b(g):
                i2 = g % 2
                xcb = xc[i2]
                xsb, xwb, xdb, btb = xs_tm[i2], xw_tm[i2], xd_tm[i2], b_tm[i2]
                bkt = bank()
                for s in range(2):
                    for ct in range(2):
                        tr(bkt.tb[:, s * 256 + ct * 128: s * 256 + (ct + 1) * 128], xcb.t[:, ct, s * 128:(s + 1) * 128], identb.t[:],
                           [xcb.r, identb.r], [bkt.r])
                    tr(bkt.tb[:, 512 + s * 128: 512 + (s + 1) * 128], xcb.t[:, 2, s * 128:(s + 1) * 128], identb.t[:],
                       [xcb.r, identb.r], [bkt.r])
                S.op("act", lambda e: e.activation(out=xsb.t[:].rearrange("p a b -> p (a b)"), in_=bkt.tb[:, 0:512], func=AF.Copy),
                     reads=[bkt.r], writes=[xsb.r])
                if True:
                    S.op("act", lambda e: e.activation(out=btb.t[:].rearrange("p a b -> p (a b)"), in_=bkt.tb[:, 512:768], func=AF.Copy),
                         reads=[bkt.r], writes=[btb.r])
                else:
                    S.op("dve", lambda e: e.tensor_copy(out=btb.t[:].rearrange("p a b -> p (a b)"), in_=bkt.tb[:, 512:768]),
                         reads=[bkt.r], writes=[btb.r])
                S.op("pool", lambda e: e.tensor_tensor(
                    out=xwb.t[:].rearrange("p s (h d) -> p (s h) d", h=4),
                    in0=xsb.t[:].rearrange("p s (h d) -> p (s h) d", h=4),
                    in1=tmq.t[:, 2, g, :, :].rearrange("p s h -> p (s h)").unsqueeze(2).to_broadcast([128, 8, 64]), op=ALU.mult),
                     reads=[xsb.r, tmq.r], writes=[xwb.r])
                S.op("pool", lambda e: e.tensor_tensor(
                    out=xdb.t[:].rearrange("p s (h d) -> p (s h) d", h=4),
                    in0=xsb.t[:].rearrange("p s (h d) -> p (s h) d", h=4),
                    in1=drep.t[:, g * 8:(g + 1) * 8].unsqueeze(2).to_broadcast([128, 8, 64]), op=ALU.mult),
                     reads=[xsb.r, CONST], writes=[xdb.r])
                bkc = bank()
                mm(bkc.t[:, 0:256], [(xcb.t[:, 2, 0:128], xcb.t[:, 3, 0:256])], [xcb.r], [bkc.r])
                mm(bkc.t[:, 256:384], [(xcb.t[:, 2, 128:256], xcb.t[:, 3, 128:256])], [xcb.r], [bkc.r])
                cb = cbm[i2]
                S.op("dve", lambda e: e.tensor_tensor(out=cb.t[:], in0=bkc.t[:, 0:384], in1=maskc.t[:], op=ALU.mult),
                     reads=[bkc.r, CONST], writes=[cb.r])
                bko = bank()
                for ls in range(2):
                    mm(bko.t[:, ls * 256:(ls + 1) * 256], [(xcb.t[:, 3, ls * 128:(ls + 1) * 128], Sbf.t[:, g * 256:(g + 1) * 256])],
                       [xcb.r, Sbf_r[g]], [bko.r])
                yt = ytmp[g % 2]
                S.op("dve", lambda e: e.tensor_tensor(
                    out=yt.t[:].rearrange("p s (h d) -> p (s h) d", h=4),
                    in0=bko.t[:].rearrange("p (a d) -> p a d", a=8),
                    in1=tmq.t[:, 1, g, :, :].rearrange("p s h -> p (s h)").unsqueeze(2).to_broadcast([128, 8, 64]), op=ALU.mult),
                     reads=[bko.r, tmq.r], writes=[yt.r])
                bks = bank()
                mm(bks.t[:, 0:256], [(btb.t[:, sc, :], xwb.t[:, sc, :]) for sc in range(2)], [btb.r, xwb.r], [bks.r])
                Sg = Sst.t[:, g * 256:(g + 1) * 256]
                S.op("dve", lambda e: e.tensor_tensor(
                    out=Sg.rearrange("p (h d) -> p h d", h=4), in0=Sg.rearrange("p (h d) -> p h d", h=4),
                    in1=cdB.t[:, 4 * g:4 * g + 4].unsqueeze(2).to_broadcast([128, 4, 64]), op=ALU.mult),
                     reads=[Sst_r[g], cdB.r], writes=[Sst_r[g]])
                S.op("dve", lambda e: e.tensor_tensor(out=Sg, in0=Sg, in1=bks.t[:, 0:256], op=ALU.add),
                     reads=[Sst_r[g], bks.r], writes=[Sst_r[g]])
                S.op("act", lambda e: e.activation(out=Sbf.t[:, g * 256:(g + 1) * 256], in_=Sg, func=AF.Copy),
                     reads=[Sst_r[g]], writes=[Sbf_r[g]])

            ydiag_banks = {}

            def stage_c(g):
                i2 = g % 2
                xsb, xdb = xs_tm[i2], xd_tm[i2]
                cb = cbm[i2]
                bcast_load(2 * g)
                bcast_load(2 * g + 1)
                bkd = bank()
                ydiag_banks[g] = bkd
                for h in range(4):
                    L = lt[h % 2]
                    M = mt[h % 2]
                    ab = acsB[h // 2]
                    S.op("act", lambda e, L=L, h=h, ab=ab: e.activation(
                        out=L.t[:, 0:256], in_=ab.t[:, h % 2, :], func=AF.Exp, bias=tmq.t[:, 0, g, 0, h:h + 1]),
                         reads=[ab.r, tmq.r], writes=[lt_r[h % 2][0]])
                    S.op("act", lambda e, L=L, h=h, ab=ab: e.activation(
                        out=L.t[:, 256:384], in_=ab.t[:, h % 2, 128:256], func=AF.Exp, bias=tmq.t[:, 0, g, 1, h:h + 1]),
                         reads=[ab.r, tmq.r], writes=[lt_r[h % 2][1]])
                    S.op("dve", lambda e, L=L, M=M: e.scalar_tensor_tensor(out=M.t[:], in0=L.t[:], scalar=1e30, in1=cb.t[:],
                                                                          op0=ALU.min, op1=ALU.mult),
                         reads=[lt_r[h % 2][0], lt_r[h % 2][1], cb.r], writes=[M.r])
                    hs = slice(h * 64, (h + 1) * 64)
                    mm(bkd.t[:, h * 64:(h + 1) * 64],
                       [(M.t[:, 0:128], xsb.t[:, 0, hs]), (identb.t[:], xdb.t[:, 0, hs])],
                       [M.r, xsb.r, xdb.r, identb.r], [bkd.r])
                    mm(bkd.t[:, 256 + h * 64:256 + (h + 1) * 64],
                       [(M.t[:, 128:256], xsb.t[:, 0, hs]), (M.t[:, 256:384], xsb.t[:, 1, hs]), (identb.t[:], xdb.t[:, 1, hs])],
                       [M.r, xsb.r, xdb.r, identb.r], [bkd.r])

            def stage_d(g):
                i2 = g % 2
                bkd = ydiag_banks.pop(g)
                yt = ytmp[g % 2]
                yb = ybuf[i2]
                zsb = zs[g % 3]
                S.op("dve", lambda e: e.tensor_tensor(out=yb.t[:].rearrange("p a b -> p (a b)"),
                                                      in0=yt.t[:].rearrange("p a b -> p (a b)"), in1=bkd.t[:], op=ALU.add),
                     reads=[yt.r, bkd.r], writes=[yb.r])
                S.op("pool", lambda e: e.tensor_tensor(out=yb.t[:], in0=yb.t[:], in1=zsb.t[:], op=ALU.mult),
                     reads=[yb.r, zsb.r], writes=[yb.r])
                sq = ssq[i2]
                rs = rstd[i2]
                for ls in range(2):
                    S.op("act", lambda e, ls=ls: e.activation(out=sqj.t[:], in_=yb.t[:, ls, :], func=AF.Square,
                                                              accum_out=sq.t[:, ls:ls + 1]),
                         reads=[yb.r], writes=[sqj.r, sq.r])
                S.op("act", lambda e: e.activation(out=rs.t[:], in_=sq.t[:], func=AF.Ln, scale=1.0 / 256.0, bias=RMS_EPS),
                     reads=[sq.r], writes=[rs.r])
                S.op("act", lambda e: e.activation(out=rs.t[:], in_=rs.t[:], func=AF.Exp, scale=-0.5),
                     reads=[rs.r], writes=[rs.r])
                yg = ygs[i2]
                for ls in range(2):
                    S.op("act", lambda e, ls=ls: e.activation(out=yg.t[:, ls, :], in_=yb.t[:, ls, :], func=AF.Identity,
                                                              scale=rs.t[:, ls:ls + 1]),
                         reads=[yb.r, rs.r], writes=[ygs_r[i2][ls]])
                bky = bank()
                for fc in range(2):
                    for ls in range(2):
                        tr(bky.tb[:, fc * 256 + ls * 128: fc * 256 + (ls + 1) * 128], yg.t[:, ls, fc * 128:(fc + 1) * 128], identb.t[:],
                           [ygs_r[i2][ls], identb.r], [bky.r])
                for fc in range(2):
                    S.op("act", lambda e, fc=fc: e.activation(
                        out=ygT.t[:, 2 * g + fc, :], in_=bky.tb[:, fc * 256:(fc + 1) * 256], func=AF.Identity,
                        scale=normw.t[:, 2 * g + fc:2 * g + fc + 1]),
                         reads=[bky.r, CONST], writes=[ygT_r[g]])

            for g in range(8):
                S.tag = ('m', ti, 'E', g, 'a')
                bank_pool[0] = 'e'
                stage_a(g)
                S.tag = ('m', ti, 'E', g, 'b')
                stage_b(g)
                S.tag = ('m', ti, 'E', g, 'c')
                stage_c(g)
                S.tag = ('m', ti, 'E', g, 'd')
                stage_d(g)
                yield

            S.tag = ('m', ti, 'F')
            bank_pool[0] = 'f'
            us = [(2 * ti) % 3, (2 * ti + 1) % 3, (2 * ti + 2) % 3]
            for ub in range(2):
                wb, wk = w_acquire("U", ub)
                wv = wb.t[:].rearrange("p (k n) -> p k n", k=8)
                for s in range(2):
                    bk = bank()
                    mm(bk.t[:], [(hT.t[:, kc, s * 128:(s + 1) * 128], wv[:, kc, :]) for kc in range(8)], [hT.r, wb.r], [bk.r])
                    dst = u_tm.t[:, us[1 + s], ub * 512:(ub + 1) * 512]
                    S.op("act", lambda e, dst=dst, bk=bk: e.activation(out=dst, in_=bk.t[:], func=AF.Copy),
                         reads=[bk.r], writes=[uslot[us[1 + s]]])
                w_done(wk)
            for pair in range(4):
                bk = bank()
                for kk in range(2):
                    cc = pair * 2 + kk
                    wi = cc // 2
                    for s in range(2):
                        o_ap = bk.t[:, kk * 256 + s * 128: kk * 256 + (s + 1) * 128]
                        pairs = []
                        rds = [CONST]
                        src_prev = us[s]
                        src_cur = us[1 + s]
                        if not (first and s == 0):
                            pairs.append((u_tm.t[:, src_prev, cc * 128:(cc + 1) * 128], pmat.t[:, (8 + wi) * 128:(9 + wi) * 128]))
                            rds.append(uslot[src_prev])
                            dk = 0
                        else:
                            dk = 4
                        pairs.append((u_tm.t[:, src_cur, cc * 128:(cc + 1) * 128], pmat.t[:, (dk + wi) * 128:(dk + wi + 1) * 128]))
                        rds.append(uslot[src_cur])
                        mm(o_ap, pairs, rds, [bk.r])
                dst = pooledT.t[:, pair * 2:pair * 2 + 2, :].rearrange("p a b -> p (a b)")
                S.op("act", lambda e, dst=dst, bk=bk: e.activation(out=dst, in_=bk.t[:], func=AF.Copy),
                     reads=[bk.r], writes=[pooledT.r])

            yield
            for q in range(4):
                S.tag = ('m', ti, 'F')
                bank_pool[0] = 'f'
                gblk, gk = w_acquire("GG", q)
                gv = gblk.t[:].rearrange("p (k n) -> p k n", k=8)
                sblk, sk = w_acquire("S", q)
                sv = sblk.t[:].rearrange("p (k n) -> p k n", k=16)
                for jj in range(2):
                    j = 2 * q + jj
                    bkg = bank()
                    mm(bkg.t[:, 0:256], [(gv[:, kc, jj * 128:(jj + 1) * 128], hT.t[:, kc, :]) for kc in range(8)], [gblk.r, hT.r], [bkg.r])
                    mm(bkg.t[:, 256:512], [(gv[:, kc, 256 + jj * 128:256 + (jj + 1) * 128], hT.t[:, kc, :]) for kc in range(8)],
                       [gblk.r, hT.r], [bkg.r])
                    bkm = bank()
                    gi, dd = divmod(j, 2)
                    mm(bkm.t[:, 0:256], [(wpool.t[:, gi * 2 + c2, dd * 128:(dd + 1) * 128], pooledT.t[:, gi * 2 + c2, :]) for c2 in range(2)],
                       [CONST, pooledT.r], [bkm.r])
                    mm(bkm.t[:, 256:512], [(sv[:, fc, jj * 128:(jj + 1) * 128], ygT.t[:, fc, :]) for fc in range(16)],
                       [sblk.r] + ygT_r, [bkm.r])
                    S.op("act", lambda e, bkg=bkg, j=j: e.activation(out=sg.t[:, 0, :], in_=bkg.t[:, 0:256], func=AF.Sigmoid,
                                                                     bias=bg.t[:, j:j + 1]), reads=[bkg.r, CONST], writes=[sg.r])
                    S.op("act", lambda e, bkg=bkg, j=j: e.activation(out=sg.t[:, 1, :], in_=bkg.t[:, 256:512], func=AF.Sigmoid,
                                                                     bias=bg.t[:, 8 + j:9 + j]), reads=[bkg.r, CONST], writes=[sg.r])
                    S.op("dve", lambda e, bkm=bkm, j=j: e.scalar_tensor_tensor(out=mP.t[:], in0=bkm.t[:, 0:256], scalar=pscale.t[:, j:j + 1],
                                                                               in1=sg.t[:, 0, :], op0=ALU.mult, op1=ALU.mult),
                         reads=[bkm.r, sg.r, CONST], writes=[mP.r])
                    S.op("dve", lambda e, bkm=bkm: e.tensor_tensor(out=mS.t[:], in0=bkm.t[:, 256:512], in1=sg.t[:, 1, :], op=ALU.mult),
                         reads=[bkm.r, sg.r], writes=[mS.r])
                    S.op("pool", lambda e, j=j: e.tensor_tensor(out=mergedT.t[:, j, :], in0=mP.t[:], in1=mS.t[:], op=ALU.add),
                         reads=[mP.r, mS.r], writes=[mergedT.r])
                yield

            S.tag = ('m', ti, 'G')
            bank_pool[0] = 'f'
            for half in range(2):
                oblk, ok_ = w_acquire("O", half)
                ov = oblk.t[:].rearrange("p (k n) -> p k n", k=8)
                for s in range(2):
                    bk = bank()
                    mm(bk.t[:], [(mergedT.t[:, dc, s * 128:(s + 1) * 128], ov[:, dc, :]) for dc in range(8)], [mergedT.r, oblk.r], [bk.r])
                    seg = xb.t[:, s, half * 512:(half + 1) * 512]
                    S.op("dve", lambda e, seg=seg, bk=bk: e.scalar_tensor_tensor(out=seg, in0=seg, scalar=ALPHA, in1=bk.t[:],
                                                                                 op0=ALU.mult, op1=ALU.add),
                         reads=[xb.r, bk.r], writes=[xb.r])
                w_done(ok_)
            layer_norm(xb, 0)
            transpose_to_hT(xb, h1T)
            yield

        def ffn_gen(ti):
            xb = xt[ti % NXT]
            h1T = hbufs[ti % 3]

            for ub in range(8):
                S.tag = ('f', ti, 'I')
                bank_pool[0] = 'f'
                ublk, uk = w_acquire("UP", ub)
                uv = ublk.t[:].rearrange("p (k n) -> p k n", k=8)
                for pr in range(2):
                    bk = bank()
                    for kk in range(2):
                        fcl = pr * 2 + kk
                        mm(bk.t[:, kk * 256:(kk + 1) * 256], [(uv[:, kc, fcl * 128:(fcl + 1) * 128], h1T.t[:, kc, :]) for kc in range(8)],
                           [ublk.r, h1T.r], [bk.r])
                    fc0 = ub * 4 + pr * 2
                    rt = sg.t[:].rearrange("p a b -> p (a b)")
                    S.op("act", lambda e, bk=bk, rt=rt: e.activation(out=rt, in_=bk.t[:], func=AF.Relu), reads=[bk.r], writes=[sg.r])
                    S.op("dve", lambda e, fc0=fc0, rt=rt: e.tensor_tensor(out=aT.t[:, fc0:fc0 + 2, :].rearrange("p a b -> p (a b)"),
                                                                        in0=rt, in1=rt, op=ALU.mult),
                         reads=[sg.r], writes=[aT.r])
                yield

            for half in range(2):
                S.tag = ('f', ti, 'J')
                bank_pool[0] = 'f'
                accs = [bank() for _ in range(2)]
                for j in range(4):
                    dblk, dk_ = w_acquire("DN", half * 4 + j)
                    dv = dblk.t[:].rearrange("p (k n) -> p k n", k=8)
                    for s in range(2):
                        ab_ = accs[s]

                        for f0 in range(0, 8, 4):
                            def fn(e, ab_=ab_, j=j, s=s, dv=dv, f0=f0):
                                ins = None
                                for f in range(f0, f0 + 4):
                                    ins = e.matmul(out=ab_.t[:], lhsT=aT.t[:, j * 8 + f, s * 128:(s + 1) * 128], rhs=dv[:, f, :],
                                                   start=(j == 0 and f == 0), stop=(j == 3 and f == 7))
                                return ins
                            S.op("pe", fn, reads=[aT.r, dblk.r], writes=[ab_.r])
                for s in range(2):
                    seg = xb.t[:, s, half * 512:(half + 1) * 512]
                    ab_ = accs[s]
                    S.op("dve", lambda e, seg=seg, ab_=ab_: e.scalar_tensor_tensor(out=seg, in0=seg, scalar=ALPHA, in1=ab_.t[:],
                                                                                   op0=ALU.mult, op1=ALU.add),
                         reads=[xb.r, ab_.r], writes=[xb.r])
                yield
            layer_norm(xb, 2)
            dst = out[ti * T:(ti + 1) * T, :].rearrange("(s p) d -> p s d", p=128)
            S.op("sp", lambda e, dst=dst, xb=xb: e.dma_start(out=dst, in_=xb.t[:]), reads=[xb.r], dma=d_o[ti % NXT])

            yield

        for it in range(NT + 1):
            gens = []
            if it < NT:
                gens.append(mixer_gen(it))
            if it >= 1:
                gens.append(ffn_gen(it - 1))
            while gens:
                for gnr in list(gens):
                    try:
                        next(gnr)
                    except StopIteration:
                        gens.remove(gnr)

        if os.environ.get("DBGW"):
            dbg_w = nc.dram_tensor("dbg_w", [NB, 128, 4096], BF16, kind="ExternalOutput").ap()
            d_dbg = S.dma_sem("dbg")
            S.op("sp", lambda e: e.dma_start(out=dbg_w, in_=wsc), reads=[r for pair in wres for r in pair], dma=d_dbg)
        if LIST_SCHED:
            S.schedule()
        S.finalize()
        S.check_deadlock()
        with nc.Block() as block:
            S.emit(block)
        with nc.Block() as block2:
            @block2.sync
            def _(e):
                if os.environ.get("DBGW"):
                    e.wait_ge(S.sems[d_dbg], S.final_counts[d_dbg])
                for i in range(3):
                    if d_o[i] in S.final_counts:
                        e.wait_ge(S.sems[d_o[i]], S.final_counts[d_o[i]])
    return nc


def _pool_mats():
    pm = np.zeros((12, 128, 128), np.float32)
    tt = np.arange(128)
    for wi, w in enumerate((2, 4, 8, 16)):
        d = tt[None, :] - tt[:, None]
        inwin = (d >= 0) & (d < w)
        pm[wi] = inwin * (1.0 / w) - np.eye(128)
        cnt = np.minimum(tt + 1, w).astype(np.float32)
        pm[4 + wi] = inwin * (1.0 / cnt)[None, :] - np.eye(128)
        dp = tt[None, :] + 128 - tt[:, None]
        pm[8 + wi] = ((dp >= 0) & (dp < w)) * (1.0 / w)
    return np.ascontiguousarray(pm.transpose(1, 0, 2).reshape(128, 12 * 128))


def _mask_c():
    tt = np.arange(128)
    tri = (tt[None, :] >= tt[:, None]).astype(np.float32)
    return np.ascontiguousarray(np.concatenate([tri, np.ones((128, 128), np.float32), tri], axis=1))


def _kblock(w, cols):
    sub = w[:, cols]
    kcs = sub.shape[0] // 128
    return np.ascontiguousarray(sub.reshape(kcs, 128, -1).transpose(1, 0, 2).reshape(128, -1))


def _prep_shared(inp):
    w_in = inp["w_in"][0]
    ar = np.arange
    blocks = []
    for kind, i in WB:
        if kind == "U":
            blk = _kblock(w_in, 512 * i + ar(512))
        elif kind == "Z":
            blk = _kblock(w_in, 1024 + 512 * i + ar(512))
        elif kind == "X":
            cols = np.concatenate([3072 + 256 * i + ar(256), 5120 + 128 * i + ar(128), 6144 + 128 * i + ar(128)])
            blk = _kblock(w_in, cols)
        elif kind == "GG":
            cols = np.concatenate([7200 + 256 * i + ar(256), 8224 + 256 * i + ar(256)])
            blk = _kblock(w_in, cols)
        elif kind == "S":
            blk = _kblock(inp["w_ssd_proj"][0], 256 * i + ar(256))
        elif kind == "O":
            blk = _kblock(inp["w_out"][0], 512 * i + ar(512))
        elif kind == "UP":
            blk = _kblock(inp["w_up"][0], 512 * i + ar(512))
        elif kind == "DN":
            half, j = divmod(i, 4)
            blk = _kblock(inp["w_down"][0][1024 * j:1024 * (j + 1), :], 512 * half + ar(512))
        assert blk.shape == (128, 4096), (kind, blk.shape)
        blocks.append(blk)
    wall = np.ascontiguousarray(np.stack(blocks, 0), dtype=np.float32)
    wdt_h = _kblock(w_in, 7168 + ar(32))
    wp = inp["w_pool_group"][0]
    wpool_h = np.ascontiguousarray(wp.reshape(4, 2, 128, 256).transpose(2, 0, 1, 3).reshape(128, 2048))
    cw = inp["conv_w"][0]
    convw_h = np.ascontiguousarray(cw.reshape(4, 32, 128).transpose(2, 1, 0).reshape(128, 128))
    convb_h = np.ascontiguousarray(inp["conv_b"][0].reshape(32, 128).T)
    bg_h = np.ascontiguousarray(inp["b_gates"][0].reshape(16, 128).T)
    pscale_h = np.ascontiguousarray(inp["pool_scale"][0].reshape(8, 128).T)
    normw_h = np.ascontiguousarray(inp["ssd_norm_w"][0].reshape(16, 128).T)
    dtb_h = np.ascontiguousarray(inp["dt_bias"][0].reshape(32, 1))
    alog_h = np.ascontiguousarray(inp["a_log"][0].reshape(32, 1))
    drep_h = np.ascontiguousarray(np.broadcast_to(inp["d_skip"][0].reshape(1, 8, 1, 4), (128, 8, 2, 4)).reshape(128, 64))
    ln_h = np.ascontiguousarray(np.stack([inp["ln1_g"][0], inp["ln1_b"][0], inp["ln2_g"][0], inp["ln2_b"][0]], 0))
    shared = dict(wall=wall, wdt_h=wdt_h, wpool_h=wpool_h, pmat_h=_pool_mats(), maskc_h=_mask_c(),
                  ident_h=np.eye(128, dtype=np.float32), convw_h=convw_h, convb_h=convb_h, bg_h=bg_h, pscale_h=pscale_h,
                  normw_h=normw_h, dtb_h=dtb_h, alog_h=alog_h, drep_h=drep_h, ln_h=ln_h)
    return {k: np.ascontiguousarray(v, dtype=np.float32) for k, v in shared.items()}


_NC_CACHE = {}


def kernel(**inputs):
    inp = {k: np.asarray(v, dtype=np.float32) for k, v in inputs.items()}
    x = inp["x"]
    shared = _prep_shared(inp)
    if "nc" not in _NC_CACHE:
        _NC_CACHE["nc"] = build_nc(NT_FULL)
    nc = _NC_CACHE["nc"]
    in_maps = []
    for c in range(NCORES):
        m = dict(shared)
        m["xin"] = np.ascontiguousarray(x[NSEQ * c:NSEQ * (c + 1)].reshape(NSEQ * SEQ, D))
        in_maps.append(m)
    res = run_bass_kernel_spmd(nc, in_maps, core_ids=list(range(NCORES)))
    outs = [np.asarray(r["out"]).reshape(NSEQ, SEQ, D) for r in res.results]
    return np.concatenate(outs, axis=0).astype(np.float32)
```

```python
import numpy as np
from contextlib import ExitStack
import concourse.bass as bass
import concourse.mybir as mybir
from concourse.bass_utils import run_bass_kernel_spmd

F32 = mybir.dt.float32
BF16 = mybir.dt.bfloat16
AF = mybir.ActivationFunctionType
ALU = mybir.AluOpType

NCORES = 8
SEQ = 2048
D = 1024
T = 256
NCH = SEQ // T
NSEQ = 4
NT_FULL = NSEQ * NCH
ALPHA = float(2.0 ** 0.25)
LN_EPS = 1e-5
RMS_EPS = 1e-5
import os
NSLOT = 5
NM_SLOT = int(os.environ.get('NM_SLOT', '3'))
SEQ_E = bool(int(os.environ.get('SEQ_E', '0')))
TG = os.environ.get('TG', 'p')
HOP_LAT = float(os.environ.get('HOP_LAT', '0.6'))
SAME_LAT = float(os.environ.get('SAME_LAT', '0.1'))
ACT_SCALE = float(os.environ.get('ACT_SCALE', '1.0'))
PRIO_MODE = os.environ.get('PRIO_MODE', 'blevel')
MM_CHUNK = int(os.environ.get('MM_CHUNK', '4'))
TABLE_AWARE = int(os.environ.get('TABLE_AWARE', '1'))
TBL_US = 1.28
TBL_PEN = float(os.environ.get('TBL_PEN', '1.28'))
PE_SCALE = float(os.environ.get('PE_SCALE', '0.9'))
DVE_SCALE = float(os.environ.get('DVE_SCALE', '0.9'))
LIST_SCHED = bool(int(os.environ.get('LIST_SCHED', '1')))

WB = [("U", 0), ("U", 1)]
for _j in range(4):
    WB += [("Z", _j), ("X", 2 * _j), ("X", 2 * _j + 1)]
for _q in range(4):
    WB += [("GG", _q), ("S", _q)]
WB += [("O", 0), ("O", 1)] + [("UP", i) for i in range(8)] + [("DN", i) for i in range(8)]
NB = len(WB)


class Res:
    __slots__ = ("name", "last_w", "readers")

    def __init__(self, name):
        self.name = name
        self.last_w = None
        self.readers = []


class Op:
    __slots__ = ("eng", "fn", "deps", "signal", "sigval", "owner", "inc", "is_dma", "idx", "cost", "users", "nbytes", "tag", "t0", "t1", "prio", "family")


class _CostProbe:
    def __init__(self, eng):
        self.eng = eng
        self.cost = 0.0
        self.nbytes = 0
        self.family = None

    def __getattr__(self, name):
        def f(*a, **kw):
            if name == "matmul":
                r = kw["rhs"]
                mult = 4.0 if r.dtype == F32 else 1.0
                self.cost += (r.free_size() * mult / 2200.0 + 0.02) * PE_SCALE
            elif name == "transpose":
                self.cost += 0.27 if kw["in_"].dtype == F32 else 0.1
            elif name == "dma_start":
                self.nbytes += kw["out"].nbytes()
            elif name == "nop":
                self.cost += 0.05
            else:
                if name == "activation":
                    fnm = str(kw.get("func", a[2] if len(a) > 2 else "")).split(".")[-1]
                    fam = {"Exp": "E", "Ln": "E", "Silu": "S", "Sigmoid": "G"}.get(fnm)
                    if fam is not None:
                        self.family = fam
                out = kw.get("out", None)
                if out is None:
                    out = kw.get("ap", a[0] if a else None)
                cols = out.free_size() if out is not None else 256
                if self.eng == "act":
                    self.cost += (0.25 + cols / 1400.0) * ACT_SCALE
                elif self.eng == "dve":
                    self.cost += (0.15 + cols / 800.0) * DVE_SCALE
                else:
                    self.cost += 0.3 + cols * 0.0032
            return self
        return f


class Sched:
    ENGS = ("pe", "act", "dve", "pool", "sp")
    PRIO = {"BCD": 0, "E": 0, "F": 1, "G": 1, "I": 2, "J": 2, "PRE": 3}
    DEFAULT_COST = {"pe": 0.15, "act": 0.45, "dve": 0.4, "pool": 0.6, "sp": 0.05}

    def __init__(self, nc, es):
        self.nc = nc
        self.es = es
        self.engs = {k: [] for k in self.ENGS}
        self.sems = {}
        self.all_ops = []
        for k in self.ENGS:
            self.sems[k] = es.enter_context(nc.semaphore("s_" + k))

    def dma_sem(self, name):
        s = self.es.enter_context(self.nc.semaphore("d_" + name))
        self.sems["d_" + name] = s
        return "d_" + name

    def op(self, eng, fn, reads=(), writes=(), dma=None, cost=None, nbytes=0):
        if eng == 'pool' and 'p' in TG:
            eng = 'dve'
        if eng == 'pool!':
            eng = 'pool' if 'q' in TG else 'dve'
        o = Op()
        o.family = None
        if cost is None:
            pr = _CostProbe(eng)
            fn(pr)
            cost = max(pr.cost, 0.03)
            nbytes = pr.nbytes
            o.family = pr.family
        o.cost = cost
        o.nbytes = nbytes
        o.users = []
        o.tag = getattr(self, 'tag', None)
        o.prio = self.PRIO.get(o.tag[2], 1) if o.tag else 1
        o.t0 = o.t1 = 0.0
        o.eng = eng
        o.fn = fn
        o.is_dma = dma is not None
        o.owner = dma if dma is not None else eng
        o.inc = 16 if dma is not None else 1
        o.signal = dma is not None
        o.sigval = None
        deps = []
        for r in reads:
            if r.last_w is not None:
                deps.append(r.last_w)
        for w in writes:
            if w.last_w is not None:
                deps.append(w.last_w)
            deps.extend(w.readers)
        for r in reads:
            r.readers.append(o)
        for w in writes:
            w.last_w = o
            w.readers = []
        o.deps = sorted({id(d): d for d in deps if d is not o}.values(), key=lambda d: d.idx)
        o.idx = len(self.all_ops)
        self.all_ops.append(o)
        self.engs[eng].append(o)
        return o

    @staticmethod
    def _needs_wait(o, d):
        if d.is_dma or o.is_dma:
            return True
        if d.eng != o.eng:
            return True
        return o.eng != "pe"

    def finalize(self):
        for o in self.all_ops:
            for d in o.deps:
                if self._needs_wait(o, d):
                    d.signal = True
        cnt = {}
        for eng in self.ENGS:
            for o in self.engs[eng]:
                if o.signal:
                    cnt[o.owner] = cnt.get(o.owner, 0) + o.inc
                    o.sigval = cnt[o.owner]
        self.final_counts = cnt

    def schedule(self):
        ops = self.all_ops
        for o in ops:
            for d in o.deps:
                d.users.append(o)
        unmet = [len(o.deps) for o in ops]
        blevel = [0.0] * len(ops)
        if PRIO_MODE == "blevel":
            for o in reversed(ops):
                b = 0.0
                for u in o.users:
                    lat = SAME_LAT if (u.eng == o.eng and not o.is_dma) else HOP_LAT
                    if blevel[u.idx] + lat > b:
                        b = blevel[u.idx] + lat
                blevel[o.idx] = b + (2.0 + o.nbytes / 280e3 if o.is_dma else o.cost)
            for o in ops:
                o.prio = -blevel[o.idx]
        ready_t = [0.0] * len(ops)
        fin_t = [0.0] * len(ops)
        free = {k: 0.0 for k in self.ENGS}
        ready = {k: [] for k in self.ENGS}
        for o in ops:
            if unmet[o.idx] == 0:
                ready[o.eng].append(o)
        newq = {k: [] for k in self.ENGS}
        dma_free = 0.0
        act_fam = [None]
        n_done = 0
        total = len(ops)
        while n_done < total:
            best = None
            for k in self.ENGS:
                if not ready[k]:
                    continue
                f = free[k]
                if k == "act" and TABLE_AWARE:
                    cand = min(ready[k], key=lambda o: (max(f, ready_t[o.idx]) + (TBL_PEN if (TABLE_AWARE == 1 and o.family and o.family != act_fam[0]) else 0.0),
                                                        o.prio, o.idx))
                    st = max(f, ready_t[cand.idx]) + (TBL_US if (cand.family and cand.family != act_fam[0]) else 0.0)
                else:
                    cand = min(ready[k], key=lambda o: (max(f, ready_t[o.idx]), o.prio, o.idx))
                    st = max(f, ready_t[cand.idx])
                if best is None or (st, cand.idx) < (best[0], best[1].idx):
                    best = (st, cand)
            st, o = best
            if o.eng == 'act' and TABLE_AWARE and o.family:
                act_fam[0] = o.family
            ready[o.eng].remove(o)
            newq[o.eng].append(o)
            if o.is_dma:
                free[o.eng] = st + 0.06
                t0 = max(st, dma_free)
                dma_free = t0 + o.nbytes / 280e3
                fin = dma_free + 2.0
            else:
                fin = st + o.cost
                free[o.eng] = fin
            fin_t[o.idx] = fin
            o.t0, o.t1 = st, fin
            n_done += 1
            for u in o.users:
                unmet[u.idx] -= 1
                lat = (0.0 if o.eng == 'pe' else SAME_LAT) if (u.eng == o.eng and not o.is_dma) else HOP_LAT
                if fin + lat > ready_t[u.idx]:
                    ready_t[u.idx] = fin + lat
                if unmet[u.idx] == 0:
                    ready[u.eng].append(u)
        self.engs = newq
        self.sim_time = max(fin_t) if fin_t else 0.0

    def check_deadlock(self):
        pos = {k: 0 for k in self.engs}
        val = {}
        progress = True
        while progress:
            progress = False
            for k, ops in self.engs.items():
                while pos[k] < len(ops):
                    o = ops[pos[k]]
                    ok = all((not self._needs_wait(o, d)) or val.get(d.owner, 0) >= d.sigval for d in o.deps)
                    if not ok:
                        break
                    if o.signal:
                        val[o.owner] = val.get(o.owner, 0) + o.inc
                        assert val[o.owner] == o.sigval, (o.owner, val[o.owner], o.sigval)
                    pos[k] += 1
                    progress = True
        stuck = {k: (pos[k], len(ops)) for k, ops in self.engs.items() if pos[k] < len(ops)}
        assert not stuck, "deadlock: %s" % stuck

    def emit(self, block):
        sems = self.sems

        def run(eng_name, e):
            waited = {}
            for o in self.engs[eng_name]:
                need = {}
                for d in o.deps:
                    if not self._needs_wait(o, d):
                        continue
                    if d.sigval > waited.get(d.owner, 0):
                        need[d.owner] = max(need.get(d.owner, 0), d.sigval)
                for owner, v in need.items():
                    e.wait_ge(sems[owner], v)
                    waited[owner] = v
                ins = o.fn(e)
                if o.signal:
                    ins.then_inc(sems[o.owner], o.inc)

        @block.tensor
        def _(e):
            run("pe", e)

        @block.scalar
        def _(e):
            run("act", e)

        @block.vector
        def _(e):
            run("dve", e)

        @block.gpsimd
        def _(e):
            run("pool", e)

        @block.sync
        def _(e):
            run("sp", e)


class Buf:
    __slots__ = ("t", "r", "tb")

    def __init__(self, t, name):
        self.t = t
        self.r = Res(name)
        self.tb = None


def build_nc(NT=NT_FULL, PH=99):
    nc = bass.Bass("TRN2", target_bir_lowering=False)

    def din(name, shape, dt=F32):
        return nc.dram_tensor(name, shape, dt, kind="ExternalInput").ap()

    xin = din("xin", [NT * T, D])
    wall = din("wall", [NB, 128, 4096])
    wdt_h = din("wdt_h", [128, 256])
    wpool_h = din("wpool_h", [128, 2048])
    pmat_h = din("pmat_h", [128, 12 * 128])
    maskc_h = din("maskc_h", [128, 384])
    ident_h = din("ident_h", [128, 128])
    convw_h = din("convw_h", [128, 128])
    convb_h = din("convb_h", [128, 32])
    bg_h = din("bg_h", [128, 16])
    pscale_h = din("pscale_h", [128, 8])
    normw_h = din("normw_h", [128, 16])
    dtb_h = din("dtb_h", [32, 1])
    alog_h = din("alog_h", [32, 1])
    drep_h = din("drep_h", [128, 64])
    ln_h = din("ln_h", [4, 1024])
    out = nc.dram_tensor("out", [NT * T, D], F32, kind="ExternalOutput").ap()
    wsc = nc.dram_tensor("wsc", [NB, 128, 4096], BF16).ap()
    acs_scr = nc.dram_tensor("acs_scr", [NT, 32, 256], F32).ap()

    with ExitStack() as es:
        S = Sched(nc, es)

        def sb(name, shape, dt=F32):
            return Buf(es.enter_context(nc.sbuf_tensor(name, shape, dt)), name)

        ident = sb("ident", [128, 128])
        identb = sb("identb", [128, 128], BF16)
        pmat = sb("pmat", [128, 12 * 128], BF16)
        maskc = sb("maskc", [128, 384])
        convw = sb("convw", [128, 128])
        convb = sb("convb", [128, 32])
        bg = sb("bg", [128, 16])
        pscale = sb("pscale", [128, 8])
        normw = sb("normw", [128, 16])
        dtb = sb("dtb", [32, 1])
        aneg = sb("aneg", [32, 1])
        drep = sb("drep", [128, 64])
        lnrep = sb("lnrep", [128, 4, 1024])
        wdt = sb("wdt", [128, 8, 32], BF16)
        wpool = sb("wpool", [128, 8, 256], BF16)
        ones32 = sb("ones32", [32, 256])
        CONST = Res("consts_ready")

        ring = [sb("wring%d" % i, [128, 4096], BF16) for i in range(NSLOT)]
        NXT = int(os.environ.get("NXT", "3"))
        xt = [sb("xt%d" % i, [128, 2, 1024]) for i in range(NXT)]
        hT = sb("hT", [128, 8, 256], BF16)
        h1T = sb("h1T", [128, 8, 256], BF16)
        h2T = sb("h2T", [128, 8, 256], BF16)
        hbufs = [hT, h1T, h2T]
        u_tm = sb("u_tm", [128, 3, 1024], BF16)
        uslot = [Res("uslot%d" % i) for i in range(3)]
        pooledT = sb("pooledT", [128, 8, 256], BF16)
        zs = [sb("zs%d" % i, [128, 2, 256], BF16) for i in range(3)]
        xr = [sb("xr%d" % i, [128, 2, 259]) for i in range(2)]
        xr_h = [Res("xr_h%d" % h) for h in range(2)]
        xr_c = [Res("xr_c%d" % h) for h in range(2)]
        carry = sb("carry", [128, 8, 4, 3])
        carry_r = [Res("carry%d" % g) for g in range(8)]
        cacc = [sb("cacc0", [128, 4, 256])] * 2
        cacc_r = [[Res("cacc_%d" % c) for c in range(4)]] * 2
        xc = [sb("xc0", [128, 4, 256], BF16)] * 2
        xs_tm = [sb("xs_tm%d" % i, [128, 2, 256], BF16) for i in range(2)]
        xw_tm = [sb("xw_tm%d" % i, [128, 2, 256], BF16) for i in range(2)]
        xd_tm = [sb("xd_tm%d" % i, [128, 2, 256], BF16) for i in range(2)]
        b_tm = [sb("b_tm%d" % i, [128, 2, 128], BF16) for i in range(2)]
        dt_e = sb("dt_e", [32, 256])
        dt_v = sb("dt_v", [32, 256])
        dt_ecs = sb("dt_ecs", [32, 256])
        dt_q = sb("dt_q", [32, 3, 256])
        dgm = sb("dgm", [32, 32])
        tmq2 = [sb("tmq%d" % i, [128, 3, 8, 2, 4]) for i in range(2)]
        cdB2 = [sb("cdB%d" % i, [128, 32]) for i in range(2)]
        acsB = [sb("acsB%d" % i, [128, 2, 256]) for i in range(2)]
        cbm = [sb("cbm%d" % i, [128, 384]) for i in range(2)]
        lt = [sb("lt%d" % i, [128, 384]) for i in range(2)]
        lt_r = [[Res("lt%d_%d" % (i, c)) for c in range(2)] for i in range(2)]
        mt = [sb("mt%d" % i, [128, 384], BF16) for i in range(2)]
        Sst = sb("Sst", [128, 2048])
        Sst_r = [Res("Sst%d" % g) for g in range(8)]
        Sbf_r = [Res("Sbf%d" % g) for g in range(8)]
        Sbf = sb("Sbf", [128, 2048], BF16)
        ytmp = [sb("ytmp%d" % i, [128, 2, 256]) for i in range(2)]
        ybuf = [sb("ybuf0", [128, 2, 256])] * 2
        sqj = sb("sqj", [128, 256], BF16)
        ssq = [sb("ssq%d" % i, [128, 2]) for i in range(2)]
        rstd = [sb("rstd%d" % i, [128, 2]) for i in range(2)]
        ygs = [sb("ygs%d" % i, [128, 2, 256], BF16) for i in range(2)]
        ygs_r = [[Res("ygs%d_%d" % (i, c)) for c in range(2)] for i in range(2)]
        ygT = sb("ygT", [128, 16, 256], BF16)
        ygT_r = [Res("ygT%d" % g) for g in range(8)]
        sg = sb("sg", [128, 2, 256])
        mP = sb("mP", [128, 256])
        mS = sb("mS", [128, 256])
        mergedT = sb("mergedT", [128, 8, 256], BF16)
        aT_f32 = sb("aT", [128, 4096])
        aT = Buf(aT_f32.t[:].bitcast(BF16).rearrange("p (a b) -> p a b", a=32), "aT_bf")
        aT.r = aT_f32.r
        lnst = sb("lnst", [128, 2, 6])
        lnmv = sb("lnmv", [128, 2])
        lnr = sb("lnr", [128, 1])

        banks = []
        for i in range(8):
            b = Buf(es.enter_context(nc.psum_tensor("psb%d" % i, [128, 512], F32)), "psb%d" % i)
            b.tb = b.t[:].bitcast(BF16)
            banks.append(b)
        POOLS = {'b': [0], 'e': [1, 2, 3, 4, 5], 'f': [6, 7]}
        bank_ctr = {k: 0 for k in POOLS}
        bank_pool = ["e"]

        def bank():
            p = bank_pool[0]
            lst = POOLS[p]
            b = banks[lst[bank_ctr[p] % len(lst)]]
            bank_ctr[p] += 1
            return b

        d_ps = [S.dma_sem("ps%d" % i) for i in range(NSLOT)]
        d_c = S.dma_sem("c")
        d_w = [S.dma_sem("w%d" % i) for i in range(NSLOT)]
        d_x = [S.dma_sem("x%d" % i) for i in range(3)]
        d_o = [S.dma_sem("o%d" % i) for i in range(3)]
        d_aw = S.dma_sem("aw")
        d_ab = [S.dma_sem("ab%d" % i) for i in range(2)]

        pre_res = []

        def cload(eng, dst_ap, src_ap, sem, name):
            r = Res(name)
            pre_res.append(r)
            S.op(eng, lambda e: e.dma_start(out=dst_ap, in_=src_ap), writes=[r], dma=sem)

        cload("sp", ident.t[:], ident_h, d_c, "c_ident")
        cload("sp", maskc.t[:], maskc_h, d_c, "c_maskc")
        cload("sp", convw.t[:], convw_h, d_c, "c_convw")
        cload("sp", convb.t[:], convb_h, d_c, "c_convb")
        cload("sp", bg.t[:], bg_h, d_c, "c_bg")
        cload("sp", pscale.t[:], pscale_h, d_c, "c_pscale")
        cload("sp", normw.t[:], normw_h, d_c, "c_normw")
        cload("sp", dtb.t[:], dtb_h, d_c, "c_dtb")
        cload("sp", aneg.t[:], alog_h, d_c, "c_alog")
        cload("sp", drep.t[:], drep_h, d_c, "c_drep")
        for i in range(4):
            cload("sp", lnrep.t[:, i, :], ln_h[i, :].partition_broadcast(128), d_c, "c_ln%d" % i)
        def stage_cast(dst_ap, src_ap, ncols, name):
            stv = xt[0].t[:].rearrange("p a b -> p (a b)")[:, 0:ncols]
            S.op("sp", lambda e: e.dma_start(out=stv, in_=src_ap), writes=[xt[0].r], dma=d_x[0])
            r = Res(name)
            pre_res.append(r)
            S.op("dve", lambda e: e.tensor_copy(out=dst_ap, in_=stv), reads=[xt[0].r], writes=[r])

        stage_cast(wdt.t[:].rearrange("p a b -> p (a b)"), wdt_h, 256, "c_wdt")
        stage_cast(wpool.t[:].rearrange("p a b -> p (a b)"), wpool_h, 2048, "c_wpool")
        stage_cast(pmat.t[:], pmat_h, 1536, "c_pmat")
        wres = [[Res("wsc%d_%d" % (b, hf)) for hf in range(2)] for b in range(NB)]
        S.tag = ('p', 0, 'PRE')
        st_f32 = aT_f32.t[:]
        st_r = [Res("stg%d" % i) for i in range(2)]
        xt2b = xt[2].t[:].rearrange("p a b -> p (a b)").bitcast(BF16)
        cast_dst = [xt2b[:, 0:2048], xt2b[:, 2048:4096], h1T.t[:].rearrange("p a b -> p (a b)")]
        if 'B' in TG:
            cast_dst = [ring[0].t[:, 0:2048], ring[1].t[:, 0:2048], ring[2].t[:, 0:2048]]
            cast_r = [ring[0].r, ring[1].r, ring[2].r]
        cast_r = [Res("cst%d" % i) for i in range(3)]
        if 'B' in TG:
            cast_r = [ring[0].r, ring[1].r, ring[2].r]
        d_st = [S.dma_sem("st%d" % i) for i in range(2)]
        cast_engs = ("act", "dve")
        for b in range(NB):
            for hf in range(2):
                i = 2 * b + hf
                stv = st_f32[:, (i % 2) * 2048:(i % 2 + 1) * 2048]
                S.op("sp", lambda e, stv=stv, b=b, hf=hf: e.dma_start(out=stv, in_=wall[b][:, hf * 2048:(hf + 1) * 2048]),
                     writes=[st_r[i % 2]], dma=d_st[i % 2])
                cs = i % 3
                dstv = cast_dst[cs]
                eng = cast_engs[i % 2]
                if eng == "act":
                    S.op("act", lambda e, dstv=dstv, stv=stv: e.activation(out=dstv, in_=stv, func=AF.Copy),
                         reads=[st_r[i % 2]], writes=[cast_r[cs]])
                else:
                    S.op("dve", lambda e, dstv=dstv, stv=stv: e.tensor_copy(out=dstv, in_=stv), reads=[st_r[i % 2]], writes=[cast_r[cs]])
                S.op("sp", lambda e, dstv=dstv, b=b, hf=hf: e.dma_start(out=wsc[b][:, hf * 2048:(hf + 1) * 2048], in_=dstv),
                     reads=[cast_r[cs]], writes=[wres[b][hf]], dma=d_ps[cs])
        S.op("dve", lambda e: e.nop(nofuse=True), writes=st_r + [aT.r])
        S.op("dve", lambda e: e.nop(nofuse=True), writes=cast_r[0:2] + [xt[2].r])
        S.op("dve", lambda e: e.nop(nofuse=True), writes=cast_r[2:3] + [h1T.r])
        S.tag = None
        S.op("sp", lambda e: e.nop(nofuse=True), reads=pre_res, writes=[CONST])
        S.op("dve", lambda e: e.tensor_copy(out=identb.t[:], in_=ident.t[:]), reads=[CONST], writes=[identb.r])
        S.op("act", lambda e: e.activation(out=aneg.t[:], in_=aneg.t[:], func=AF.Exp), reads=[CONST], writes=[aneg.r])
        S.op("dve", lambda e: e.tensor_scalar(out=aneg.t[:], in0=aneg.t[:], scalar1=-1.0, scalar2=None, op0=ALU.mult),
             reads=[aneg.r], writes=[aneg.r])
        S.op("dve", lambda e: e.memset(ones32.t[:], 1.0), writes=[ones32.r])

        wstate = {"m": 0, "f": 0}

        def w_acquire(kind, idx):
            st = "f" if kind in ("UP", "DN") else "m"
            k = wstate[st]
            wstate[st] = k + 1
            slot = (k % NM_SLOT) if st == "m" else NM_SLOT + (k % (NSLOT - NM_SLOT))
            bidx = WB.index((kind, idx))
            src = wsc[bidx]
            dst = ring[slot].t[:]
            S.op("sp", lambda e: e.dma_start(out=dst, in_=src), reads=wres[bidx], writes=[ring[slot].r], dma=d_w[slot])
            return ring[slot], k

        def w_done(k):
            pass

        def mm(out_ap, pairs, reads, writes):
            pairs = list(pairs)
            n = len(pairs)
            last = None
            for c0 in range(0, n, MM_CHUNK):
                sub = list(enumerate(pairs))[c0:c0 + MM_CHUNK]

                def fn(e, sub=sub):
                    ins = None
                    for i, (l, r) in sub:
                        ins = e.matmul(out=out_ap, lhsT=l, rhs=r, start=(i == 0), stop=(i == n - 1))
                    return ins
                last = S.op("pe", fn, reads=reads, writes=writes)
            return last

        def tr(out_ap, in_ap, idt_ep, reads, writes):
            return S.op("pe", lambda e: e.transpose(out=out_ap, in_=in_ap, identity=idt_ep), reads=reads, writes=writes)

        def x_load(ti):
            xb = xt[ti % NXT]
            src = xin[ti * T:(ti + 1) * T, :].rearrange("(s p) d -> p s d", p=128)
            S.op("sp", lambda e: e.dma_start(out=xb.t[:], in_=src), reads=([aT.r, h1T.r] if (ti == 0 and 'W' in TG) else []),
                 writes=[xb.r], dma=d_x[ti % NXT])

        def transpose_to_hT(xb, hdst):
            for pair in range(4):
                bk = bank()
                for kk in range(2):
                    kc = pair * 2 + kk
                    for s in range(2):
                        tr(bk.t[:, kk * 256 + s * 128: kk * 256 + (s + 1) * 128], xb.t[:, s, kc * 128:(kc + 1) * 128],
                           ident.t[:], [xb.r, CONST], [bk.r])
                dst = hdst.t[:, pair * 2:pair * 2 + 2, :].rearrange("p a b -> p (a b)")
                S.op("act", lambda e, dst=dst, bk=bk: e.activation(out=dst, in_=bk.t[:], func=AF.Copy),
                     reads=[bk.r], writes=[hdst.r])

        def layer_norm(xb, gi):
            for s in range(2):
                row = xb.t[:, s, :]
                for hh in range(2):
                    S.op("dve", lambda e, hh=hh, row=row: e.bn_stats(out=lnst.t[:, hh, :], in_=row[:, hh * 512:(hh + 1) * 512]),
                         reads=[xb.r], writes=[lnst.r])
                S.op("dve", lambda e: e.bn_aggr(out=lnmv.t[:], in_=lnst.t[:].rearrange("p a b -> p (a b)")),
                     reads=[lnst.r], writes=[lnmv.r])
                S.op("act", lambda e: e.activation(out=lnr.t[:], in_=lnmv.t[:, 1:2], func=AF.Ln, bias=LN_EPS),
                     reads=[lnmv.r], writes=[lnr.r])
                S.op("act", lambda e: e.activation(out=lnr.t[:], in_=lnr.t[:], func=AF.Exp, scale=-0.5),
                     reads=[lnr.r], writes=[lnr.r])
                S.op("dve", lambda e: e.tensor_scalar(out=lnmv.t[:, 0:1], in0=lnmv.t[:, 0:1], scalar1=lnr.t[:, 0:1], scalar2=-1.0,
                                                      op0=ALU.mult, op1=ALU.mult), reads=[lnmv.r, lnr.r], writes=[lnmv.r])
                S.op("act", lambda e, row=row: e.activation(out=row, in_=row, func=AF.Identity, scale=lnr.t[:, 0:1], bias=lnmv.t[:, 0:1]),
                     reads=[xb.r, lnmv.r, lnr.r], writes=[xb.r])
                S.op("pool!", lambda e, row=row: e.tensor_tensor(out=row, in0=row, in1=lnrep.t[:, gi, :], op=ALU.mult),
                     reads=[xb.r, CONST], writes=[xb.r])
                S.op("pool!", lambda e, row=row: e.tensor_tensor(out=row, in0=row, in1=lnrep.t[:, gi + 1, :], op=ALU.add),
                     reads=[xb.r, CONST], writes=[xb.r])

        def finish(ti, xb):
            dst = out[ti * T:(ti + 1) * T, :].rearrange("(s p) d -> p s d", p=128)
            S.op("sp", lambda e: e.dma_start(out=dst, in_=xb.t[:]), reads=[xb.r], dma=d_o[ti % 2])
            if ti + 1 < NT and PH < 6:
                x_load(ti + 1)

        def mixer_gen(ti):
            seq, ch = divmod(ti, NCH)
            first = ch == 0
            xb = xt[ti % NXT]
            S.tag = ('m', ti, 'BCD')
            bank_pool[0] = 'b'
            tmq = tmq2[ti % 2]
            cdB = cdB2[ti % 2]
            hT = hbufs[ti % 3]
            h1T = hT
            x_load(ti)
            if first:
                S.op("pool", lambda e: e.memset(Sst.t[:], 0.0), writes=Sst_r)
                S.op("pool", lambda e: e.memset(Sbf.t[:], 0.0), writes=Sbf_r)
                S.op("pool", lambda e: e.memset(carry.t[:], 0.0), writes=carry_r)

            transpose_to_hT(xb, hT)

            bk = bank()
            mm(bk.t[0:32, 0:256], [(wdt.t[:, kc, :], hT.t[:, kc, :]) for kc in range(8)], [CONST, hT.r], [bk.r])
            S.op("act", lambda e, bk=bk: e.activation(out=dt_e.t[:], in_=bk.t[0:32, 0:256], func=AF.Exp, bias=dtb.t[:, 0:1]),
                 reads=[bk.r, CONST], writes=[dt_e.r])
            S.op("act", lambda e: e.activation(out=dt_v.t[:], in_=dt_e.t[:], func=AF.Ln, bias=1.0),
                 reads=[dt_e.r], writes=[dt_v.r])
            S.op("dve", lambda e: e.tensor_scalar(out=dt_e.t[:], in0=dt_v.t[:], scalar1=aneg.t[:, 0:1], scalar2=None, op0=ALU.mult),
                 reads=[dt_v.r, aneg.r], writes=[dt_e.r])
            S.op("dve", lambda e: e.tensor_tensor_scan(out=dt_ecs.t[:], data0=ones32.t[:], data1=dt_e.t[:], initial=0.0,
                                                       op0=ALU.mult, op1=ALU.add),
                 reads=[ones32.r, dt_e.r], writes=[dt_ecs.r])
            scr = acs_scr[ti]
            scr_r = Res("acs_scr%d" % ti)
            S.op("sp", lambda e, scr=scr: e.dma_start(out=scr, in_=dt_ecs.t[:]), reads=[dt_ecs.r], writes=[scr_r], dma=d_aw)

            def bcast_load(gh, scr=scr, scr_r=scr_r):
                ab = acsB[gh % 2]
                src = scr[2 * gh:2 * gh + 2, :].rearrange("h t -> (h t)").partition_broadcast(128)
                S.op("sp", lambda e: e.dma_start(out=ab.t[:].rearrange("p h t -> p (h t)"), in_=src),
                     reads=[scr_r], writes=[ab.r], dma=d_ab[gh % 2])

            S.op("act", lambda e: e.activation(out=dt_e.t[:], in_=dt_v.t[:], func=AF.Ln), reads=[dt_v.r], writes=[dt_e.r])
            S.op("dve", lambda e: e.tensor_tensor(out=dt_q.t[:, 0, :], in0=dt_e.t[:], in1=dt_ecs.t[:], op=ALU.subtract),
                 reads=[dt_e.r, dt_ecs.r], writes=[dt_q.r])
            S.op("act", lambda e: e.activation(out=dt_q.t[:, 1, :], in_=dt_ecs.t[:], func=AF.Exp), reads=[dt_ecs.r], writes=[dt_q.r])
            S.op("act", lambda e: e.activation(out=dt_e.t[:], in_=dt_ecs.t[:], func=AF.Exp, bias=dt_ecs.t[:, 255:256], scale=-1.0),
                 reads=[dt_ecs.r, dt_e.r], writes=[dt_e.r])
            S.op("dve", lambda e: e.tensor_tensor(out=dt_q.t[:, 2, :], in0=dt_e.t[:], in1=dt_v.t[:], op=ALU.mult),
                 reads=[dt_e.r, dt_v.r], writes=[dt_q.r])
            bk = bank()
            for s in range(2):
                for q in range(3):
                    tr(bk.t[:, (s * 3 + q) * 32:(s * 3 + q + 1) * 32], dt_q.t[:, q, s * 128:(s + 1) * 128], ident.t[0:32, 0:32],
                       [dt_q.r, CONST], [bk.r])
            for s in range(2):
                for q in range(3):
                    S.op("dve", lambda e, bk=bk, s=s, q=q: e.tensor_copy(
                        out=tmq.t[:, q, :, s, :], in_=bk.t[:, (s * 3 + q) * 32:(s * 3 + q + 1) * 32].rearrange("p (g h) -> p g h", g=8)),
                         reads=[bk.r], writes=[tmq.r])
            S.op("dve", lambda e: e.tensor_scalar(out=dgm.t[:], in0=ident.t[0:32, 0:32], scalar1=dt_q.t[:, 1, 255:256], scalar2=None,
                                                  op0=ALU.mult), reads=[CONST, dt_q.r], writes=[dgm.r])
            bk = bank()
            mm(bk.t[:, 0:32], [(ones32.t[:, 0:128], dgm.t[:])], [ones32.r, dgm.r], [bk.r])
            S.op("dve", lambda e, bk=bk: e.tensor_copy(out=cdB.t[:], in_=bk.t[:, 0:32]), reads=[bk.r], writes=[cdB.r])

            yield
            zinfo = {}

            def stage_a(g):
                i2 = g % 2
                if g % 2 == 0:
                    zinfo["z"] = w_acquire("Z", g // 2)
                zblk, zk = zinfo["z"]
                zv = zblk.t[:].rearrange("p (k n) -> p k n", k=8)
                bkz = bank()
                for s in range(2):
                    mm(bkz.t[:, s * 256:(s + 1) * 256],
                       [(hT.t[:, kc, s * 128:(s + 1) * 128], zv[:, kc, (g % 2) * 256:(g % 2 + 1) * 256]) for kc in range(8)],
                       [hT.r, zblk.r], [bkz.r])
                zsb = zs[g % 3]
                S.op("act", lambda e: e.activation(out=zsb.t[:].rearrange("p a b -> p (a b)"), in_=bkz.t[:], func=AF.Silu),
                     reads=[bkz.r], writes=[zsb.r])
                xblk, xk = w_acquire("X", g)
                xv = xblk.t[:].rearrange("p (k n) -> p k n", k=8)
                for half in range(2):
                    xrb = xr[half]
                    S.op("dve", lambda e, xrb=xrb, half=half: e.tensor_copy(out=xrb.t[:, :, 0:3], in_=carry.t[:, g, 2 * half:2 * half + 2, :]),
                         reads=[carry_r[g]], writes=[xr_c[half]])
                    bkx = bank()
                    for kk in range(2):
                        ct = half * 2 + kk
                        mm(bkx.t[:, kk * 256:(kk + 1) * 256],
                           [(xv[:, kc, ct * 128:(ct + 1) * 128], hT.t[:, kc, :]) for kc in range(8)], [hT.r, xblk.r], [bkx.r])
                    S.op("act", lambda e, xrb=xrb, bkx=bkx: e.activation(
                        out=xrb.t[:, :, 3:259], in_=bkx.t[:].rearrange("p (a b) -> p a b", a=2), func=AF.Copy),
                         reads=[bkx.r], writes=[xr_h[half]])
                w_done(xk)
                if g % 2 == 1:
                    w_done(zk)
                for half in range(2):
                    xrb = xr[half]
                    S.op("dve", lambda e, xrb=xrb, half=half: e.tensor_copy(out=carry.t[:, g, 2 * half:2 * half + 2, :], in_=xrb.t[:, :, 256:259]),
                         reads=[xr_h[half]], writes=[carry_r[g]])
                cab = cacc[i2]

                def cch_of(ct):
                    return (g * 2 + ct) if ct < 2 else (16 + g if ct == 2 else 24 + g)
                for ct in range(4):
                    cch = cch_of(ct)
                    xrb = xr[ct // 2]
                    S.op("act", lambda e, ct=ct, cch=cch, xrb=xrb: e.activation(
                        out=cab.t[:, ct, :], in_=xrb.t[:, ct % 2, 3:259], func=AF.Identity,
                        bias=convb.t[:, cch:cch + 1], scale=convw.t[:, cch * 4 + 3:cch * 4 + 4]),
                         reads=[xr_h[ct // 2], CONST], writes=[cacc_r[i2][ct]])
                for k in range(3):
                    for ct in range(4):
                        cch = cch_of(ct)
                        xrb = xr[ct // 2]
                        S.op("dve", lambda e, ct=ct, cch=cch, k=k, xrb=xrb: e.scalar_tensor_tensor(
                            out=cab.t[:, ct, :], in0=xrb.t[:, ct % 2, k:k + 256], scalar=convw.t[:, cch * 4 + k:cch * 4 + k + 1],
                            in1=cab.t[:, ct, :], op0=ALU.mult, op1=ALU.add),
                             reads=[xr_h[ct // 2], xr_c[ct // 2], CONST, cacc_r[i2][ct]], writes=[cacc_r[i2][ct]])
                xcb = xc[i2]
                S.op("act", lambda e: e.activation(out=xcb.t[:].rearrange("p a b -> p (a b)"),
                                                   in_=cab.t[:].rearrange("p a b -> p (a b)"), func=AF.Silu),
                     reads=cacc_r[i2], writes=[xcb.r])

            def stage_b(g):
                i2 = g % 2
                xcb = xc[i2]
                xsb, xwb, xdb, btb = xs_tm[i2], xw_tm[i2], xd_tm[i2], b_tm[i2]
                bkt = bank()
                for s in range(2):
                    for ct in range(2):
                        tr(bkt.tb[:, s * 256 + ct * 128: s * 256 + (ct + 1) * 128], xcb.t[:, ct, s * 128:(s + 1) * 128], identb.t[:],
                           [xcb.r, identb.r], [bkt.r])
                    tr(bkt.tb[:, 512 + s * 128: 512 + (s + 1) * 128], xcb.t[:, 2, s * 128:(s + 1) * 128], identb.t[:],
                       [xcb.r, identb.r], [bkt.r])
                S.op("act", lambda e: e.activation(out=xsb.t[:].rearrange("p a b -> p (a b)"), in_=bkt.tb[:, 0:512], func=AF.Copy),
                     reads=[bkt.r], writes=[xsb.r])
                if True:
                    S.op("act", lambda e: e.activation(out=btb.t[:].rearrange("p a b -> p (a b)"), in_=bkt.tb[:, 512:768], func=AF.Copy),
                         reads=[bkt.r], writes=[btb.r])
                else:
                    S.op("dve", lambda e: e.tensor_copy(out=btb.t[:].rearrange("p a b -> p (a b)"), in_=bkt.tb[:, 512:768]),
                         reads=[bkt.r], writes=[btb.r])
                S.op("pool", lambda e: e.tensor_tensor(
                    out=xwb.t[:].rearrange("p s (h d) -> p (s h) d", h=4),
                    in0=xsb.t[:].rearrange("p s (h d) -> p (s h) d", h=4),
                    in1=tmq.t[:, 2, g, :, :].rearrange("p s h -> p (s h)").unsqueeze(2).to_broadcast([128, 8, 64]), op=ALU.mult),
                     reads=[xsb.r, tmq.r], writes=[xwb.r])
                S.op("pool", lambda e: e.tensor_tensor(
                    out=xdb.t[:].rearrange("p s (h d) -> p (s h) d", h=4),
                    in0=xsb.t[:].rearrange("p s (h d) -> p (s h) d", h=4),
                    in1=drep.t[:, g * 8:(g + 1) * 8].unsqueeze(2).to_broadcast([128, 8, 64]), op=ALU.mult),
                     reads=[xsb.r, CONST], writes=[xdb.r])
                bkc = bank()
                mm(bkc.t[:, 0:256], [(xcb.t[:, 2, 0:128], xcb.t[:, 3, 0:256])], [xcb.r], [bkc.r])
                mm(bkc.t[:, 256:384], [(xcb.t[:, 2, 128:256], xcb.t[:, 3, 128:256])], [xcb.r], [bkc.r])
                cb = cbm[i2]
                S.op("dve", lambda e: e.tensor_tensor(out=cb.t[:], in0=bkc.t[:, 0:384], in1=maskc.t[:], op=ALU.mult),
                     reads=[bkc.r, CONST], writes=[cb.r])
                bko = bank()
                for ls in range(2):
                    mm(bko.t[:, ls * 256:(ls + 1) * 256], [(xcb.t[:, 3, ls * 128:(ls + 1) * 128], Sbf.t[:, g * 256:(g + 1) * 256])],
                       [xcb.r, Sbf_r[g]], [bko.r])
                yt = ytmp[g % 2]
                S.op("dve", lambda e: e.tensor_tensor(
                    out=yt.t[:].rearrange("p s (h d) -> p (s h) d", h=4),
                    in0=bko.t[:].rearrange("p (a d) -> p a d", a=8),
                    in1=tmq.t[:, 1, g, :, :].rearrange("p s h -> p (s h)").unsqueeze(2).to_broadcast([128, 8, 64]), op=ALU.mult),
                     reads=[bko.r, tmq.r], writes=[yt.r])
                bks = bank()
                mm(bks.t[:, 0:256], [(btb.t[:, sc, :], xwb.t[:, sc, :]) for sc in range(2)], [btb.r, xwb.r], [bks.r])
                Sg = Sst.t[:, g * 256:(g + 1) * 256]
                S.op("dve", lambda e: e.tensor_tensor(
                    out=Sg.rearrange("p (h d) -> p h d", h=4), in0=Sg.rearrange("p (h d) -> p h d", h=4),
                    in1=cdB.t[:, 4 * g:4 * g + 4].unsqueeze(2).to_broadcast([128, 4, 64]), op=ALU.mult),
                     reads=[Sst_r[g], cdB.r], writes=[Sst_r[g]])
                S.op("dve", lambda e: e.tensor_tensor(out=Sg, in0=Sg, in1=bks.t[:, 0:256], op=ALU.add),
                     reads=[Sst_r[g], bks.r], writes=[Sst_r[g]])
                S.op("act", lambda e: e.activation(out=Sbf.t[:, g * 256:(g + 1) * 256], in_=Sg, func=AF.Copy),
                     reads=[Sst_r[g]], writes=[Sbf_r[g]])

            ydiag_banks = {}

            def stage_c(g):
                i2 = g % 2
                xsb, xdb = xs_tm[i2], xd_tm[i2]
                cb = cbm[i2]
                bcast_load(2 * g)
                bcast_load(2 * g + 1)
                bkd = bank()
                ydiag_banks[g] = bkd
                for h in range(4):
                    L = lt[h % 2]
                    M = mt[h % 2]
                    ab = acsB[h // 2]
                    S.op("act", lambda e, L=L, h=h, ab=ab: e.activation(
                        out=L.t[:, 0:256], in_=ab.t[:, h % 2, :], func=AF.Exp, bias=tmq.t[:, 0, g, 0, h:h + 1]),
                         reads=[ab.r, tmq.r], writes=[lt_r[h % 2][0]])
                    S.op("act", lambda e, L=L, h=h, ab=ab: e.activation(
                        out=L.t[:, 256:384], in_=ab.t[:, h % 2, 128:256], func=AF.Exp, bias=tmq.t[:, 0, g, 1, h:h + 1]),
                         reads=[ab.r, tmq.r], writes=[lt_r[h % 2][1]])
                    S.op("dve", lambda e, L=L, M=M: e.scalar_tensor_tensor(out=M.t[:], in0=L.t[:], scalar=1e30, in1=cb.t[:],
                                                                          op0=ALU.min, op1=ALU.mult),
                         reads=[lt_r[h % 2][0], lt_r[h % 2][1], cb.r], writes=[M.r])
                    hs = slice(h * 64, (h + 1) * 64)
                    mm(bkd.t[:, h * 64:(h + 1) * 64],
                       [(M.t[:, 0:128], xsb.t[:, 0, hs]), (identb.t[:], xdb.t[:, 0, hs])],
                       [M.r, xsb.r, xdb.r, identb.r], [bkd.r])
                    mm(bkd.t[:, 256 + h * 64:256 + (h + 1) * 64],
                       [(M.t[:, 128:256], xsb.t[:, 0, hs]), (M.t[:, 256:384], xsb.t[:, 1, hs]), (identb.t[:], xdb.t[:, 1, hs])],
                       [M.r, xsb.r, xdb.r, identb.r], [bkd.r])

            def stage_d(g):
                i2 = g % 2
                bkd = ydiag_banks.pop(g)
                yt = ytmp[g % 2]
                yb = ybuf[i2]
                zsb = zs[g % 3]
                S.op("dve", lambda e: e.tensor_tensor(out=yb.t[:].rearrange("p a b -> p (a b)"),
                                                      in0=yt.t[:].rearrange("p a b -> p (a b)"), in1=bkd.t[:], op=ALU.add),
                     reads=[yt.r, bkd.r], writes=[yb.r])
                S.op("pool", lambda e: e.tensor_tensor(out=yb.t[:], in0=yb.t[:], in1=zsb.t[:], op=ALU.mult),
                     reads=[yb.r, zsb.r], writes=[yb.r])
                sq = ssq[i2]
                rs = rstd[i2]
                for ls in range(2):
                    S.op("act", lambda e, ls=ls: e.activation(out=sqj.t[:], in_=yb.t[:, ls, :], func=AF.Square,
                                                              accum_out=sq.t[:, ls:ls + 1]),
                         reads=[yb.r], writes=[sqj.r, sq.r])
                S.op("act", lambda e: e.activation(out=rs.t[:], in_=sq.t[:], func=AF.Ln, scale=1.0 / 256.0, bias=RMS_EPS),
                     reads=[sq.r], writes=[rs.r])
                S.op("act", lambda e: e.activation(out=rs.t[:], in_=rs.t[:], func=AF.Exp, scale=-0.5),
                     reads=[rs.r], writes=[rs.r])
                yg = ygs[i2]
                for ls in range(2):
                    S.op("act", lambda e, ls=ls: e.activation(out=yg.t[:, ls, :], in_=yb.t[:, ls, :], func=AF.Identity,
                                                              scale=rs.t[:, ls:ls + 1]),
                         reads=[yb.r, rs.r], writes=[ygs_r[i2][ls]])
                bky = bank()
                for fc in range(2):
                    for ls in range(2):
                        tr(bky.tb[:, fc * 256 + ls * 128: fc * 256 + (ls + 1) * 128], yg.t[:, ls, fc * 128:(fc + 1) * 128], identb.t[:],
                           [ygs_r[i2][ls], identb.r], [bky.r])
                for fc in range(2):
                    S.op("act", lambda e, fc=fc: e.activation(
                        out=ygT.t[:, 2 * g + fc, :], in_=bky.tb[:, fc * 256:(fc + 1) * 256], func=AF.Identity,
                        scale=normw.t[:, 2 * g + fc:2 * g + fc + 1]),
                         reads=[bky.r, CONST], writes=[ygT_r[g]])

            for g in range(8):
                S.tag = ('m', ti, 'E', g, 'a')
                bank_pool[0] = 'e'
                stage_a(g)
                S.tag = ('m', ti, 'E', g, 'b')
                stage_b(g)
                S.tag = ('m', ti, 'E', g, 'c')
                stage_c(g)
                S.tag = ('m', ti, 'E', g, 'd')
                stage_d(g)
                yield

            S.tag = ('m', ti, 'F')
            bank_pool[0] = 'f'
            us = [(2 * ti) % 3, (2 * ti + 1) % 3, (2 * ti + 2) % 3]
            for ub in range(2):
                wb, wk = w_acquire("U", ub)
                wv = wb.t[:].rearrange("p (k n) -> p k n", k=8)
                for s in range(2):
                    bk = bank()
                    mm(bk.t[:], [(hT.t[:, kc, s * 128:(s + 1) * 128], wv[:, kc, :]) for kc in range(8)], [hT.r, wb.r], [bk.r])
                    dst = u_tm.t[:, us[1 + s], ub * 512:(ub + 1) * 512]
                    S.op("act", lambda e, dst=dst, bk=bk: e.activation(out=dst, in_=bk.t[:], func=AF.Copy),
                         reads=[bk.r], writes=[uslot[us[1 + s]]])
                w_done(wk)
            for pair in range(4):
                bk = bank()
                for kk in range(2):
                    cc = pair * 2 + kk
                    wi = cc // 2
                    for s in range(2):
                        o_ap = bk.t[:, kk * 256 + s * 128: kk * 256 + (s + 1) * 128]
                        pairs = []
                        rds = [CONST]
                        src_prev = us[s]
                        src_cur = us[1 + s]
                        if not (first and s == 0):
                            pairs.append((u_tm.t[:, src_prev, cc * 128:(cc + 1) * 128], pmat.t[:, (8 + wi) * 128:(9 + wi) * 128]))
                            rds.append(uslot[src_prev])
                            dk = 0
                        else:
                            dk = 4
                        pairs.append((u_tm.t[:, src_cur, cc * 128:(cc + 1) * 128], pmat.t[:, (dk + wi) * 128:(dk + wi + 1) * 128]))
                        rds.append(uslot[src_cur])
                        mm(o_ap, pairs, rds, [bk.r])
                dst = pooledT.t[:, pair * 2:pair * 2 + 2, :].rearrange("p a b -> p (a b)")
                S.op("act", lambda e, dst=dst, bk=bk: e.activation(out=dst, in_=bk.t[:], func=AF.Copy),
                     reads=[bk.r], writes=[pooledT.r])

            yield
            for q in range(4):
                S.tag = ('m', ti, 'F')
                bank_pool[0] = 'f'
                gblk, gk = w_acquire("GG", q)
                gv = gblk.t[:].rearrange("p (k n) -> p k n", k=8)
                sblk, sk = w_acquire("S", q)
                sv = sblk.t[:].rearrange("p (k n) -> p k n", k=16)
                for jj in range(2):
                    j = 2 * q + jj
                    bkg = bank()
                    mm(bkg.t[:, 0:256], [(gv[:, kc, jj * 128:(jj + 1) * 128], hT.t[:, kc, :]) for kc in range(8)], [gblk.r, hT.r], [bkg.r])
                    mm(bkg.t[:, 256:512], [(gv[:, kc, 256 + jj * 128:256 + (jj + 1) * 128], hT.t[:, kc, :]) for kc in range(8)],
                       [gblk.r, hT.r], [bkg.r])
                    bkm = bank()
                    gi, dd = divmod(j, 2)
                    mm(bkm.t[:, 0:256], [(wpool.t[:, gi * 2 + c2, dd * 128:(dd + 1) * 128], pooledT.t[:, gi * 2 + c2, :]) for c2 in range(2)],
                       [CONST, pooledT.r], [bkm.r])
                    mm(bkm.t[:, 256:512], [(sv[:, fc, jj * 128:(jj + 1) * 128], ygT.t[:, fc, :]) for fc in range(16)],
                       [sblk.r] + ygT_r, [bkm.r])
                    S.op("act", lambda e, bkg=bkg, j=j: e.activation(out=sg.t[:, 0, :], in_=bkg.t[:, 0:256], func=AF.Sigmoid,
                                                                     bias=bg.t[:, j:j + 1]), reads=[bkg.r, CONST], writes=[sg.r])
                    S.op("act", lambda e, bkg=bkg, j=j: e.activation(out=sg.t[:, 1, :], in_=bkg.t[:, 256:512], func=AF.Sigmoid,
                                                                     bias=bg.t[:, 8 + j:9 + j]), reads=[bkg.r, CONST], writes=[sg.r])
                    S.op("dve", lambda e, bkm=bkm, j=j: e.scalar_tensor_tensor(out=mP.t[:], in0=bkm.t[:, 0:256], scalar=pscale.t[:, j:j + 1],
                                                                               in1=sg.t[:, 0, :], op0=ALU.mult, op1=ALU.mult),
                         reads=[bkm.r, sg.r, CONST], writes=[mP.r])
                    S.op("dve", lambda e, bkm=bkm: e.tensor_tensor(out=mS.t[:], in0=bkm.t[:, 256:512], in1=sg.t[:, 1, :], op=ALU.mult),
                         reads=[bkm.r, sg.r], writes=[mS.r])
                    S.op("pool", lambda e, j=j: e.tensor_tensor(out=mergedT.t[:, j, :], in0=mP.t[:], in1=mS.t[:], op=ALU.add),
                         reads=[mP.r, mS.r], writes=[mergedT.r])
                yield

            S.tag = ('m', ti, 'G')
            bank_pool[0] = 'f'
            for half in range(2):
                oblk, ok_ = w_acquire("O", half)
                ov = oblk.t[:].rearrange("p (k n) -> p k n", k=8)
                for s in range(2):
                    bk = bank()
                    mm(bk.t[:], [(mergedT.t[:, dc, s * 128:(s + 1) * 128], ov[:, dc, :]) for dc in range(8)], [mergedT.r, oblk.r], [bk.r])
                    seg = xb.t[:, s, half * 512:(half + 1) * 512]
                    S.op("dve", lambda e, seg=seg, bk=bk: e.scalar_tensor_tensor(out=seg, in0=seg, scalar=ALPHA, in1=bk.t[:],
                                                                                 op0=ALU.mult, op1=ALU.add),
                         reads=[xb.r, bk.r], writes=[xb.r])
                w_done(ok_)
            layer_norm(xb, 0)
            transpose_to_hT(xb, h1T)
            yield

        def ffn_gen(ti):
            xb = xt[ti % NXT]
            h1T = hbufs[ti % 3]

            for ub in range(8):
                S.tag = ('f', ti, 'I')
                bank_pool[0] = 'f'
                ublk, uk = w_acquire("UP", ub)
                uv = ublk.t[:].rearrange("p (k n) -> p k n", k=8)
                for pr in range(2):
                    bk = bank()
                    for kk in range(2):
                        fcl = pr * 2 + kk
                        mm(bk.t[:, kk * 256:(kk + 1) * 256], [(uv[:, kc, fcl * 128:(fcl + 1) * 128], h1T.t[:, kc, :]) for kc in range(8)],
                           [ublk.r, h1T.r], [bk.r])
                    fc0 = ub * 4 + pr * 2
                    rt = sg.t[:].rearrange("p a b -> p (a b)")
                    S.op("act", lambda e, bk=bk, rt=rt: e.activation(out=rt, in_=bk.t[:], func=AF.Relu), reads=[bk.r], writes=[sg.r])
                    S.op("dve", lambda e, fc0=fc0, rt=rt: e.tensor_tensor(out=aT.t[:, fc0:fc0 + 2, :].rearrange("p a b -> p (a b)"),
                                                                        in0=rt, in1=rt, op=ALU.mult),
                         reads=[sg.r], writes=[aT.r])
                yield

            for half in range(2):
                S.tag = ('f', ti, 'J')
                bank_pool[0] = 'f'
                accs = [bank() for _ in range(2)]
                for j in range(4):
                    dblk, dk_ = w_acquire("DN", half * 4 + j)
                    dv = dblk.t[:].rearrange("p (k n) -> p k n", k=8)
                    for s in range(2):
                        ab_ = accs[s]

                        for f0 in range(0, 8, 4):
                            def fn(e, ab_=ab_, j=j, s=s, dv=dv, f0=f0):
                                ins = None
                                for f in range(f0, f0 + 4):
                                    ins = e.matmul(out=ab_.t[:], lhsT=aT.t[:, j * 8 + f, s * 128:(s + 1) * 128], rhs=dv[:, f, :],
                                                   start=(j == 0 and f == 0), stop=(j == 3 and f == 7))
                                return ins
                            S.op("pe", fn, reads=[aT.r, dblk.r], writes=[ab_.r])
                for s in range(2):
                    seg = xb.t[:, s, half * 512:(half + 1) * 512]
                    ab_ = accs[s]
                    S.op("dve", lambda e, seg=seg, ab_=ab_: e.scalar_tensor_tensor(out=seg, in0=seg, scalar=ALPHA, in1=ab_.t[:],
                                                                                   op0=ALU.mult, op1=ALU.add),
                         reads=[xb.r, ab_.r], writes=[xb.r])
                yield
            layer_norm(xb, 2)
            dst = out[ti * T:(ti + 1) * T, :].rearrange("(s p) d -> p s d", p=128)
            S.op("sp", lambda e, dst=dst, xb=xb: e.dma_start(out=dst, in_=xb.t[:]), reads=[xb.r], dma=d_o[ti % NXT])

            yield

        for it in range(NT + 1):
            gens = []
            if it < NT:
                gens.append(mixer_gen(it))
            if it >= 1:
                gens.append(ffn_gen(it - 1))
            while gens:
                for gnr in list(gens):
                    try:
                        next(gnr)
                    except StopIteration:
                        gens.remove(gnr)

        if os.environ.get("DBGW"):
            dbg_w = nc.dram_tensor("dbg_w", [NB, 128, 4096], BF16, kind="ExternalOutput").ap()
            d_dbg = S.dma_sem("dbg")
            S.op("sp", lambda e: e.dma_start(out=dbg_w, in_=wsc), reads=[r for pair in wres for r in pair], dma=d_dbg)
        if LIST_SCHED:
            S.schedule()
        S.finalize()
        S.check_deadlock()
        with nc.Block() as block:
            S.emit(block)
        with nc.Block() as block2:
            @block2.sync
            def _(e):
                if os.environ.get("DBGW"):
                    e.wait_ge(S.sems[d_dbg], S.final_counts[d_dbg])
                for i in range(3):
                    if d_o[i] in S.final_counts:
                        e.wait_ge(S.sems[d_o[i]], S.final_counts[d_o[i]])
    return nc


def _pool_mats():
    pm = np.zeros((12, 128, 128), np.float32)
    tt = np.arange(128)
    for wi, w in enumerate((2, 4, 8, 16)):
        d = tt[None, :] - tt[:, None]
        inwin = (d >= 0) & (d < w)
        pm[wi] = inwin * (1.0 / w) - np.eye(128)
        cnt = np.minimum(tt + 1, w).astype(np.float32)
        pm[4 + wi] = inwin * (1.0 / cnt)[None, :] - np.eye(128)
        dp = tt[None, :] + 128 - tt[:, None]
        pm[8 + wi] = ((dp >= 0) & (dp < w)) * (1.0 / w)
    return np.ascontiguousarray(pm.transpose(1, 0, 2).reshape(128, 12 * 128))


def _mask_c():
    tt = np.arange(128)
    tri = (tt[None, :] >= tt[:, None]).astype(np.float32)
    return np.ascontiguousarray(np.concatenate([tri, np.ones((128, 128), np.float32), tri], axis=1))


def _kblock(w, cols):
    sub = w[:, cols]
    kcs = sub.shape[0] // 128
    return np.ascontiguousarray(sub.reshape(kcs, 128, -1).transpose(1, 0, 2).reshape(128, -1))


def _prep_shared(inp):
    w_in = inp["w_in"][0]
    ar = np.arange
    blocks = []
    for kind, i in WB:
        if kind == "U":
            blk = _kblock(w_in, 512 * i + ar(512))
        elif kind == "Z":
            blk = _kblock(w_in, 1024 + 512 * i + ar(512))
        elif kind == "X":
            cols = np.concatenate([3072 + 256 * i + ar(256), 5120 + 128 * i + ar(128), 6144 + 128 * i + ar(128)])
            blk = _kblock(w_in, cols)
        elif kind == "GG":
            cols = np.concatenate([7200 + 256 * i + ar(256), 8224 + 256 * i + ar(256)])
            blk = _kblock(w_in, cols)
        elif kind == "S":
            blk = _kblock(inp["w_ssd_proj"][0], 256 * i + ar(256))
        elif kind == "O":
            blk = _kblock(inp["w_out"][0], 512 * i + ar(512))
        elif kind == "UP":
            blk = _kblock(inp["w_up"][0], 512 * i + ar(512))
        elif kind == "DN":
            half, j = divmod(i, 4)
            blk = _kblock(inp["w_down"][0][1024 * j:1024 * (j + 1), :], 512 * half + ar(512))
        assert blk.shape == (128, 4096), (kind, blk.shape)
        blocks.append(blk)
    wall = np.ascontiguousarray(np.stack(blocks, 0), dtype=np.float32)
    wdt_h = _kblock(w_in, 7168 + ar(32))
    wp = inp["w_pool_group"][0]
    wpool_h = np.ascontiguousarray(wp.reshape(4, 2, 128, 256).transpose(2, 0, 1, 3).reshape(128, 2048))
    cw = inp["conv_w"][0]
    convw_h = np.ascontiguousarray(cw.reshape(4, 32, 128).transpose(2, 1, 0).reshape(128, 128))
    convb_h = np.ascontiguousarray(inp["conv_b"][0].reshape(32, 128).T)
    bg_h = np.ascontiguousarray(inp["b_gates"][0].reshape(16, 128).T)
    pscale_h = np.ascontiguousarray(inp["pool_scale"][0].reshape(8, 128).T)
    normw_h = np.ascontiguousarray(inp["ssd_norm_w"][0].reshape(16, 128).T)
    dtb_h = np.ascontiguousarray(inp["dt_bias"][0].reshape(32, 1))
    alog_h = np.ascontiguousarray(inp["a_log"][0].reshape(32, 1))
    drep_h = np.ascontiguousarray(np.broadcast_to(inp["d_skip"][0].reshape(1, 8, 1, 4), (128, 8, 2, 4)).reshape(128, 64))
    ln_h = np.ascontiguousarray(np.stack([inp["ln1_g"][0], inp["ln1_b"][0], inp["ln2_g"][0], inp["ln2_b"][0]], 0))
    shared = dict(wall=wall, wdt_h=wdt_h, wpool_h=wpool_h, pmat_h=_pool_mats(), maskc_h=_mask_c(),
                  ident_h=np.eye(128, dtype=np.float32), convw_h=convw_h, convb_h=convb_h, bg_h=bg_h, pscale_h=pscale_h,
                  normw_h=normw_h, dtb_h=dtb_h, alog_h=alog_h, drep_h=drep_h, ln_h=ln_h)
    return {k: np.ascontiguousarray(v, dtype=np.float32) for k, v in shared.items()}


_NC_CACHE = {}


def kernel(**inputs):
    inp = {k: np.asarray(v, dtype=np.float32) for k, v in inputs.items()}
    x = inp["x"]
    shared = _prep_shared(inp)
    if "nc" not in _NC_CACHE:
        _NC_CACHE["nc"] = build_nc(NT_FULL)
    nc = _NC_CACHE["nc"]
    in_maps = []
    for c in range(NCORES):
        m = dict(shared)
        m["xin"] = np.ascontiguousarray(x[NSEQ * c:NSEQ * (c + 1)].reshape(NSEQ * SEQ, D))
        in_maps.append(m)
    res = run_bass_kernel_spmd(nc, in_maps, core_ids=list(range(NCORES)))
    outs = [np.asarray(r["out"]).reshape(NSEQ, SEQ, D) for r in res.results]
    return np.concatenate(outs, axis=0).astype(np.float32)
```

```python
import numpy as np
from contextlib import ExitStack
import concourse.bass as bass
import concourse.mybir as mybir
from concourse.bass_utils import run_bass_kernel_spmd

F32 = mybir.dt.float32
BF16 = mybir.dt.bfloat16
AF = mybir.ActivationFunctionType
ALU = mybir.AluOpType

NCORES = 8
SEQ = 2048
D = 1024
T = 256
NCH = SEQ // T
NSEQ = 4
NT_FULL = NSEQ * NCH
ALPHA = float(2.0 ** 0.25)
LN_EPS = 1e-5
RMS_EPS = 1e-5
import os
NSLOT = 5
NM_SLOT = int(os.environ.get('NM_SLOT', '3'))
SEQ_E = bool(int(os.environ.get('SEQ_E', '0')))
TG = os.environ.get('TG', 'p')
HOP_LAT = float(os.environ.get('HOP_LAT', '0.6'))
SAME_LAT = float(os.environ.get('SAME_LAT', '0.1'))
ACT_SCALE = float(os.environ.get('ACT_SCALE', '1.0'))
PRIO_MODE = os.environ.get('PRIO_MODE', 'blevel')
MM_CHUNK = int(os.environ.get('MM_CHUNK', '4'))
TABLE_AWARE = int(os.environ.get('TABLE_AWARE', '1'))
TBL_US = 1.28
TBL_PEN = float(os.environ.get('TBL_PEN', '0.8'))
PE_SCALE = float(os.environ.get('PE_SCALE', '0.9'))
DVE_SCALE = float(os.environ.get('DVE_SCALE', '0.9'))
LIST_SCHED = bool(int(os.environ.get('LIST_SCHED', '1')))

WB = [("U", 0), ("U", 1)]
for _j in range(4):
    WB += [("Z", _j), ("X", 2 * _j), ("X", 2 * _j + 1)]
for _q in range(4):
    WB += [("GG", _q), ("S", _q)]
WB += [("O", 0), ("O", 1)] + [("UP", i) for i in range(8)] + [("DN", i) for i in range(8)]
NB = len(WB)


class Res:
    __slots__ = ("name", "last_w", "readers")

    def __init__(self, name):
        self.name = name
        self.last_w = None
        self.readers = []


class Op:
    __slots__ = ("eng", "fn", "deps", "signal", "sigval", "owner", "inc", "is_dma", "idx", "cost", "users", "nbytes", "tag", "t0", "t1", "prio", "family")


class _CostProbe:
    def __init__(self, eng):
        self.eng = eng
        self.cost = 0.0
        self.nbytes = 0
        self.family = None

    def __getattr__(self, name):
        def f(*a, **kw):
            if name == "matmul":
                r = kw["rhs"]
                mult = 4.0 if r.dtype == F32 else 1.0
                self.cost += (r.free_size() * mult / 2200.0 + 0.02) * PE_SCALE
            elif name == "transpose":
                self.cost += 0.27 if kw["in_"].dtype == F32 else 0.1
            elif name == "dma_start":
                self.nbytes += kw["out"].nbytes()
            elif name == "nop":
                self.cost += 0.05
            else:
                if name == "activation":
                    fnm = str(kw.get("func", a[2] if len(a) > 2 else "")).split(".")[-1]
                    fam = {"Exp": "E", "Ln": "E", "Silu": "S", "Sigmoid": "G"}.get(fnm)
                    if fam is not None:
                        self.family = fam
                out = kw.get("out", None)
                if out is None:
                    out = kw.get("ap", a[0] if a else None)
                cols = out.free_size() if out is not None else 256
                if self.eng == "act":
                    self.cost += (0.25 + cols / 1400.0) * ACT_SCALE
                elif self.eng == "dve":
                    self.cost += (0.15 + cols / 800.0) * DVE_SCALE
                else:
                    self.cost += 0.3 + cols * 0.0032
            return self
        return f


class Sched:
    ENGS = ("pe", "act", "dve", "pool", "sp")
    PRIO = {"BCD": 0, "E": 0, "F": 1, "G": 1, "I": 2, "J": 2, "PRE": 3}
    DEFAULT_COST = {"pe": 0.15, "act": 0.45, "dve": 0.4, "pool": 0.6, "sp": 0.05}

    def __init__(self, nc, es):
        self.nc = nc
        self.es = es
        self.engs = {k: [] for k in self.ENGS}
        self.sems = {}
        self.all_ops = []
        for k in self.ENGS:
            self.sems[k] = es.enter_context(nc.semaphore("s_" + k))

    def dma_sem(self, name):
        s = self.es.enter_context(self.nc.semaphore("d_" + name))
        self.sems["d_" + name] = s
        return "d_" + name

    def op(self, eng, fn, reads=(), writes=(), dma=None, cost=None, nbytes=0):
        if eng == 'pool' and 'p' in TG:
            eng = 'dve'
        if eng == 'pool!':
            eng = 'pool' if 'q' in TG else 'dve'
        o = Op()
        o.family = None
        if cost is None:
            pr = _CostProbe(eng)
            fn(pr)
            cost = max(pr.cost, 0.03)
            nbytes = pr.nbytes
            o.family = pr.family
        o.cost = cost
        o.nbytes = nbytes
        o.users = []
        o.tag = getattr(self, 'tag', None)
        o.prio = self.PRIO.get(o.tag[2], 1) if o.tag else 1
        o.t0 = o.t1 = 0.0
        o.eng = eng
        o.fn = fn
        o.is_dma = dma is not None
        o.owner = dma if dma is not None else eng
        o.inc = 16 if dma is not None else 1
        o.signal = dma is not None
        o.sigval = None
        deps = []
        for r in reads:
            if r.last_w is not None:
                deps.append(r.last_w)
        for w in writes:
            if w.last_w is not None:
                deps.append(w.last_w)
            deps.extend(w.readers)
        for r in reads:
            r.readers.append(o)
        for w in writes:
            w.last_w = o
            w.readers = []
        o.deps = sorted({id(d): d for d in deps if d is not o}.values(), key=lambda d: d.idx)
        o.idx = len(self.all_ops)
        self.all_ops.append(o)
        self.engs[eng].append(o)
        return o

    @staticmethod
    def _needs_wait(o, d):
        if d.is_dma or o.is_dma:
            return True
        if d.eng != o.eng:
            return True
        return o.eng != "pe"

    def finalize(self):
        for o in self.all_ops:
            for d in o.deps:
                if self._needs_wait(o, d):
                    d.signal = True
        cnt = {}
        for eng in self.ENGS:
            for o in self.engs[eng]:
                if o.signal:
                    cnt[o.owner] = cnt.get(o.owner, 0) + o.inc
                    o.sigval = cnt[o.owner]
        self.final_counts = cnt

    def schedule(self):
        ops = self.all_ops
        for o in ops:
            for d in o.deps:
                d.users.append(o)
        unmet = [len(o.deps) for o in ops]
        blevel = [0.0] * len(ops)
        if PRIO_MODE == "blevel":
            for o in reversed(ops):
                b = 0.0
                for u in o.users:
                    lat = SAME_LAT if (u.eng == o.eng and not o.is_dma) else HOP_LAT
                    if blevel[u.idx] + lat > b:
                        b = blevel[u.idx] + lat
                blevel[o.idx] = b + (2.0 + o.nbytes / 280e3 if o.is_dma else o.cost)
            for o in ops:
                o.prio = -blevel[o.idx]
        ready_t = [0.0] * len(ops)
        fin_t = [0.0] * len(ops)
        free = {k: 0.0 for k in self.ENGS}
        ready = {k: [] for k in self.ENGS}
        for o in ops:
            if unmet[o.idx] == 0:
                ready[o.eng].append(o)
        newq = {k: [] for k in self.ENGS}
        dma_free = 0.0
        act_fam = [None]
        n_done = 0
        total = len(ops)
        while n_done < total:
            best = None
            for k in self.ENGS:
                if not ready[k]:
                    continue
                f = free[k]
                if k == "act" and TABLE_AWARE:
                    cand = min(ready[k], key=lambda o: (max(f, ready_t[o.idx]) + (TBL_PEN if (TABLE_AWARE == 1 and o.family and o.family != act_fam[0]) else 0.0),
                                                        o.prio, o.idx))
                    st = max(f, ready_t[cand.idx]) + (TBL_US if (cand.family and cand.family != act_fam[0]) else 0.0)
                else:
                    cand = min(ready[k], key=lambda o: (max(f, ready_t[o.idx]), o.prio, o.idx))
                    st = max(f, ready_t[cand.idx])
                if best is None or (st, cand.idx) < (best[0], best[1].idx):
                    best = (st, cand)
            st, o = best
            if o.eng == 'act' and TABLE_AWARE and o.family:
                act_fam[0] = o.family
            ready[o.eng].remove(o)
            newq[o.eng].append(o)
            if o.is_dma:
                free[o.eng] = st + 0.06
                t0 = max(st, dma_free)
                dma_free = t0 + o.nbytes / 280e3
                fin = dma_free + 2.0
            else:
                fin = st + o.cost
                free[o.eng] = fin
            fin_t[o.idx] = fin
            o.t0, o.t1 = st, fin
            n_done += 1
            for u in o.users:
                unmet[u.idx] -= 1
                lat = (0.0 if o.eng == 'pe' else SAME_LAT) if (u.eng == o.eng and not o.is_dma) else HOP_LAT
                if fin + lat > ready_t[u.idx]:
                    ready_t[u.idx] = fin + lat
                if unmet[u.idx] == 0:
                    ready[u.eng].append(u)
        self.engs = newq
        self.sim_time = max(fin_t) if fin_t else 0.0

    def check_deadlock(self):
        pos = {k: 0 for k in self.engs}
        val = {}
        progress = True
        while progress:
            progress = False
            for k, ops in self.engs.items():
                while pos[k] < len(ops):
                    o = ops[pos[k]]
                    ok = all((not self._needs_wait(o, d)) or val.get(d.owner, 0) >= d.sigval for d in o.deps)
                    if not ok:
                        break
                    if o.signal:
                        val[o.owner] = val.get(o.owner, 0) + o.inc
                        assert val[o.owner] == o.sigval, (o.owner, val[o.owner], o.sigval)
                    pos[k] += 1
                    progress = True
        stuck = {k: (pos[k], len(ops)) for k, ops in self.engs.items() if pos[k] < len(ops)}
        assert not stuck, "deadlock: %s" % stuck

    def emit(self, block):
        sems = self.sems

        def run(eng_name, e):
            waited = {}
            for o in self.engs[eng_name]:
                need = {}
                for d in o.deps:
                    if not self._needs_wait(o, d):
                        continue
                    if d.sigval > waited.get(d.owner, 0):
                        need[d.owner] = max(need.get(d.owner, 0), d.sigval)
                for owner, v in need.items():
                    e.wait_ge(sems[owner], v)
                    waited[owner] = v
                ins = o.fn(e)
                if o.signal:
                    ins.then_inc(sems[o.owner], o.inc)

        @block.tensor
        def _(e):
            run("pe", e)

        @block.scalar
        def _(e):
            run("act", e)

        @block.vector
        def _(e):
            run("dve", e)

        @block.gpsimd
        def _(e):
            run("pool", e)

        @block.sync
        def _(e):
            run("sp", e)


class Buf:
    __slots__ = ("t", "r", "tb")

    def __init__(self, t, name):
        self.t = t
        self.r = Res(name)
        self.tb = None


def build_nc(NT=NT_FULL, PH=99):
    nc = bass.Bass("TRN2", target_bir_lowering=False)

    def din(name, shape, dt=F32):
        return nc.dram_tensor(name, shape, dt, kind="ExternalInput").ap()

    xin = din("xin", [NT * T, D])
    wall = din("wall", [NB, 128, 4096])
    wdt_h = din("wdt_h", [128, 256])
    wpool_h = din("wpool_h", [128, 2048])
    pmat_h = din("pmat_h", [128, 12 * 128])
    maskc_h = din("maskc_h", [128, 384])
    ident_h = din("ident_h", [128, 128])
    convw_h = din("convw_h", [128, 128])
    convb_h = din("convb_h", [128, 32])
    bg_h = din("bg_h", [128, 16])
    pscale_h = din("pscale_h", [128, 8])
    normw_h = din("normw_h", [128, 16])
    dtb_h = din("dtb_h", [32, 1])
    alog_h = din("alog_h", [32, 1])
    drep_h = din("drep_h", [128, 64])
    ln_h = din("ln_h", [4, 1024])
    out = nc.dram_tensor("out", [NT * T, D], F32, kind="ExternalOutput").ap()
    wsc = nc.dram_tensor("wsc", [NB, 128, 4096], BF16).ap()
    acs_scr = nc.dram_tensor("acs_scr", [NT, 32, 256], F32).ap()

    with ExitStack() as es:
        S = Sched(nc, es)

        def sb(name, shape, dt=F32):
            return Buf(es.enter_context(nc.sbuf_tensor(name, shape, dt)), name)

        ident = sb("ident", [128, 128])
        identb = sb("identb", [128, 128], BF16)
        pmat = sb("pmat", [128, 12 * 128], BF16)
        maskc = sb("maskc", [128, 384])
        convw = sb("convw", [128, 128])
        convb = sb("convb", [128, 32])
        bg = sb("bg", [128, 16])
        pscale = sb("pscale", [128, 8])
        normw = sb("normw", [128, 16])
        dtb = sb("dtb", [32, 1])
        aneg = sb("aneg", [32, 1])
        drep = sb("drep", [128, 64])
        lnrep = sb("lnrep", [128, 4, 1024])
        wdt = sb("wdt", [128, 8, 32], BF16)
        wpool = sb("wpool", [128, 8, 256], BF16)
        ones32 = sb("ones32", [32, 256])
        CONST = Res("consts_ready")

        ring = [sb("wring%d" % i, [128, 4096], BF16) for i in range(NSLOT)]
        NXT = int(os.environ.get("NXT", "3"))
        xt = [sb("xt%d" % i, [128, 2, 1024]) for i in range(NXT)]
        hT = sb("hT", [128, 8, 256], BF16)
        h1T = sb("h1T", [128, 8, 256], BF16)
        h2T = sb("h2T", [128, 8, 256], BF16)
        hbufs = [hT, h1T, h2T]
        u_tm = sb("u_tm", [128, 3, 1024], BF16)
        uslot = [Res("uslot%d" % i) for i in range(3)]
        pooledT = sb("pooledT", [128, 8, 256], BF16)
        zs = [sb("zs%d" % i, [128, 2, 256], BF16) for i in range(3)]
        xr = [sb("xr%d" % i, [128, 2, 259]) for i in range(2)]
        xr_h = [Res("xr_h%d" % h) for h in range(2)]
        xr_c = [Res("xr_c%d" % h) for h in range(2)]
        carry = sb("carry", [128, 8, 4, 3])
        carry_r = [Res("carry%d" % g) for g in range(8)]
        cacc = [sb("cacc0", [128, 4, 256])] * 2
        cacc_r = [[Res("cacc_%d" % c) for c in range(4)]] * 2
        xc = [sb("xc0", [128, 4, 256], BF16)] * 2
        xs_tm = [sb("xs_tm%d" % i, [128, 2, 256], BF16) for i in range(2)]
        xw_tm = [sb("xw_tm%d" % i, [128, 2, 256], BF16) for i in range(2)]
        xd_tm = [sb("xd_tm%d" % i, [128, 2, 256], BF16) for i in range(2)]
        b_tm = [sb("b_tm%d" % i, [128, 2, 128], BF16) for i in range(2)]
        dt_e = sb("dt_e", [32, 256])
        dt_v = sb("dt_v", [32, 256])
        dt_ecs = sb("dt_ecs", [32, 256])
        dt_q = sb("dt_q", [32, 3, 256])
        dgm = sb("dgm", [32, 32])
        tmq2 = [sb("tmq%d" % i, [128, 3, 8, 2, 4]) for i in range(2)]
        cdB2 = [sb("cdB%d" % i, [128, 32]) for i in range(2)]
        acsB = [sb("acsB%d" % i, [128, 2, 256]) for i in range(2)]
        cbm = [sb("cbm%d" % i, [128, 384]) for i in range(2)]
        lt = [sb("lt%d" % i, [128, 384]) for i in range(2)]
        lt_r = [[Res("lt%d_%d" % (i, c)) for c in range(2)] for i in range(2)]
        mt = [sb("mt%d" % i, [128, 384], BF16) for i in range(2)]
        Sst = sb("Sst", [128, 2048])
        Sst_r = [Res("Sst%d" % g) for g in range(8)]
        Sbf_r = [Res("Sbf%d" % g) for g in range(8)]
        Sbf = sb("Sbf", [128, 2048], BF16)
        ytmp = [sb("ytmp%d" % i, [128, 2, 256]) for i in range(2)]
        ybuf = [sb("ybuf0", [128, 2, 256])] * 2
        sqj = sb("sqj", [128, 256], BF16)
        ssq = [sb("ssq%d" % i, [128, 2]) for i in range(2)]
        rstd = [sb("rstd%d" % i, [128, 2]) for i in range(2)]
        ygs = [sb("ygs%d" % i, [128, 2, 256], BF16) for i in range(2)]
        ygs_r = [[Res("ygs%d_%d" % (i, c)) for c in range(2)] for i in range(2)]
        ygT = sb("ygT", [128, 16, 256], BF16)
        ygT_r = [Res("ygT%d" % g) for g in range(8)]
        sg = sb("sg", [128, 2, 256])
        mP = sb("mP", [128, 256])
        mS = sb("mS", [128, 256])
        mergedT = sb("mergedT", [128, 8, 256], BF16)
        aT_f32 = sb("aT", [128, 4096])
        aT = Buf(aT_f32.t[:].bitcast(BF16).rearrange("p (a b) -> p a b", a=32), "aT_bf")
        aT.r = aT_f32.r
        lnst = sb("lnst", [128, 2, 6])
        lnmv = sb("lnmv", [128, 2])
        lnr = sb("lnr", [128, 1])

        banks = []
        for i in range(8):
            b = Buf(es.enter_context(nc.psum_tensor("psb%d" % i, [128, 512], F32)), "psb%d" % i)
            b.tb = b.t[:].bitcast(BF16)
            banks.append(b)
        POOLS = {'b': [0], 'e': [1, 2, 3, 4, 5], 'f': [6, 7]}
        bank_ctr = {k: 0 for k in POOLS}
        bank_pool = ["e"]

        def bank():
            p = bank_pool[0]
            lst = POOLS[p]
            b = banks[lst[bank_ctr[p] % len(lst)]]
            bank_ctr[p] += 1
            return b

        d_ps = [S.dma_sem("ps%d" % i) for i in range(NSLOT)]
        d_c = S.dma_sem("c")
        d_w = [S.dma_sem("w%d" % i) for i in range(NSLOT)]
        d_x = [S.dma_sem("x%d" % i) for i in range(3)]
        d_o = [S.dma_sem("o%d" % i) for i in range(3)]
        d_aw = S.dma_sem("aw")
        d_ab = [S.dma_sem("ab%d" % i) for i in range(2)]

        pre_res = []

        def cload(eng, dst_ap, src_ap, sem, name):
            r = Res(name)
            pre_res.append(r)
            S.op(eng, lambda e: e.dma_start(out=dst_ap, in_=src_ap), writes=[r], dma=sem)

        cload("sp", ident.t[:], ident_h, d_c, "c_ident")
        cload("sp", maskc.t[:], maskc_h, d_c, "c_maskc")
        cload("sp", convw.t[:], convw_h, d_c, "c_convw")
        cload("sp", convb.t[:], convb_h, d_c, "c_convb")
        cload("sp", bg.t[:], bg_h, d_c, "c_bg")
        cload("sp", pscale.t[:], pscale_h, d_c, "c_pscale")
        cload("sp", normw.t[:], normw_h, d_c, "c_normw")
        cload("sp", dtb.t[:], dtb_h, d_c, "c_dtb")
        cload("sp", aneg.t[:], alog_h, d_c, "c_alog")
        cload("sp", drep.t[:], drep_h, d_c, "c_drep")
        for i in range(4):
            cload("sp", lnrep.t[:, i, :], ln_h[i, :].partition_broadcast(128), d_c, "c_ln%d" % i)
        def stage_cast(dst_ap, src_ap, ncols, name):
            stv = xt[0].t[:].rearrange("p a b -> p (a b)")[:, 0:ncols]
            S.op("sp", lambda e: e.dma_start(out=stv, in_=src_ap), writes=[xt[0].r], dma=d_x[0])
            r = Res(name)
            pre_res.append(r)
            S.op("dve", lambda e: e.tensor_copy(out=dst_ap, in_=stv), reads=[xt[0].r], writes=[r])

        stage_cast(wdt.t[:].rearrange("p a b -> p (a b)"), wdt_h, 256, "c_wdt")
        stage_cast(wpool.t[:].rearrange("p a b -> p (a b)"), wpool_h, 2048, "c_wpool")
        stage_cast(pmat.t[:], pmat_h, 1536, "c_pmat")
        wres = [[Res("wsc%d_%d" % (b, hf)) for hf in range(2)] for b in range(NB)]
        S.tag = ('p', 0, 'PRE')
        st_f32 = aT_f32.t[:]
        st_r = [Res("stg%d" % i) for i in range(2)]
        xt2b = xt[2].t[:].rearrange("p a b -> p (a b)").bitcast(BF16)
        cast_dst = [xt2b[:, 0:2048], xt2b[:, 2048:4096], h1T.t[:].rearrange("p a b -> p (a b)")]
        if 'B' in TG:
            cast_dst = [ring[0].t[:, 0:2048], ring[1].t[:, 0:2048], ring[2].t[:, 0:2048]]
            cast_r = [ring[0].r, ring[1].r, ring[2].r]
        cast_r = [Res("cst%d" % i) for i in range(3)]
        if 'B' in TG:
            cast_r = [ring[0].r, ring[1].r, ring[2].r]
        d_st = [S.dma_sem("st%d" % i) for i in range(2)]
        cast_engs = ("act", "dve")
        for b in range(NB):
            for hf in range(2):
                i = 2 * b + hf
                stv = st_f32[:, (i % 2) * 2048:(i % 2 + 1) * 2048]
                S.op("sp", lambda e, stv=stv, b=b, hf=hf: e.dma_start(out=stv, in_=wall[b][:, hf * 2048:(hf + 1) * 2048]),
                     writes=[st_r[i % 2]], dma=d_st[i % 2])
                cs = i % 3
                dstv = cast_dst[cs]
                eng = cast_engs[i % 2]
                if eng == "act":
                    S.op("act", lambda e, dstv=dstv, stv=stv: e.activation(out=dstv, in_=stv, func=AF.Copy),
                         reads=[st_r[i % 2]], writes=[cast_r[cs]])
                else:
                    S.op("dve", lambda e, dstv=dstv, stv=stv: e.tensor_copy(out=dstv, in_=stv), reads=[st_r[i % 2]], writes=[cast_r[cs]])
                S.op("sp", lambda e, dstv=dstv, b=b, hf=hf: e.dma_start(out=wsc[b][:, hf * 2048:(hf + 1) * 2048], in_=dstv),
                     reads=[cast_r[cs]], writes=[wres[b][hf]], dma=d_ps[cs])
        S.op("dve", lambda e: e.nop(nofuse=True), writes=st_r + [aT.r])
        S.op("dve", lambda e: e.nop(nofuse=True), writes=cast_r[0:2] + [xt[2].r])
        S.op("dve", lambda e: e.nop(nofuse=True), writes=cast_r[2:3] + [h1T.r])
        S.tag = None
        S.op("sp", lambda e: e.nop(nofuse=True), reads=pre_res, writes=[CONST])
        S.op("dve", lambda e: e.tensor_copy(out=identb.t[:], in_=ident.t[:]), reads=[CONST], writes=[identb.r])
        S.op("act", lambda e: e.activation(out=aneg.t[:], in_=aneg.t[:], func=AF.Exp), reads=[CONST], writes=[aneg.r])
        S.op("dve", lambda e: e.tensor_scalar(out=aneg.t[:], in0=aneg.t[:], scalar1=-1.0, scalar2=None, op0=ALU.mult),
             reads=[aneg.r], writes=[aneg.r])
        S.op("dve", lambda e: e.memset(ones32.t[:], 1.0), writes=[ones32.r])

        wstate = {"m": 0, "f": 0}

        def w_acquire(kind, idx):
            st = "f" if kind in ("UP", "DN") else "m"
            k = wstate[st]
            wstate[st] = k + 1
            slot = (k % NM_SLOT) if st == "m" else NM_SLOT + (k % (NSLOT - NM_SLOT))
            bidx = WB.index((kind, idx))
            src = wsc[bidx]
            dst = ring[slot].t[:]
            S.op("sp", lambda e: e.dma_start(out=dst, in_=src), reads=wres[bidx], writes=[ring[slot].r], dma=d_w[slot])
            return ring[slot], k

        def w_done(k):
            pass

        def mm(out_ap, pairs, reads, writes):
            pairs = list(pairs)
            n = len(pairs)
            last = None
            for c0 in range(0, n, MM_CHUNK):
                sub = list(enumerate(pairs))[c0:c0 + MM_CHUNK]

                def fn(e, sub=sub):
                    ins = None
                    for i, (l, r) in sub:
                        ins = e.matmul(out=out_ap, lhsT=l, rhs=r, start=(i == 0), stop=(i == n - 1))
                    return ins
                last = S.op("pe", fn, reads=reads, writes=writes)
            return last

        def tr(out_ap, in_ap, idt_ep, reads, writes):
            return S.op("pe", lambda e: e.transpose(out=out_ap, in_=in_ap, identity=idt_ep), reads=reads, writes=writes)

        def x_load(ti):
            xb = xt[ti % NXT]
            src = xin[ti * T:(ti + 1) * T, :].rearrange("(s p) d -> p s d", p=128)
            S.op("sp", lambda e: e.dma_start(out=xb.t[:], in_=src), reads=([aT.r, h1T.r] if (ti == 0 and 'W' in TG) else []),
                 writes=[xb.r], dma=d_x[ti % NXT])

        def transpose_to_hT(xb, hdst):
            for pair in range(4):
                bk = bank()
                for kk in range(2):
                    kc = pair * 2 + kk
                    for s in range(2):
                        tr(bk.t[:, kk * 256 + s * 128: kk * 256 + (s + 1) * 128], xb.t[:, s, kc * 128:(kc + 1) * 128],
                           ident.t[:], [xb.r, CONST], [bk.r])
                dst = hdst.t[:, pair * 2:pair * 2 + 2, :].rearrange("p a b -> p (a b)")
                S.op("act", lambda e, dst=dst, bk=bk: e.activation(out=dst, in_=bk.t[:], func=AF.Copy),
                     reads=[bk.r], writes=[hdst.r])

        def layer_norm(xb, gi):
            for s in range(2):
                row = xb.t[:, s, :]
                for hh in range(2):
                    S.op("dve", lambda e, hh=hh, row=row: e.bn_stats(out=lnst.t[:, hh, :], in_=row[:, hh * 512:(hh + 1) * 512]),
                         reads=[xb.r], writes=[lnst.r])
                S.op("dve", lambda e: e.bn_aggr(out=lnmv.t[:], in_=lnst.t[:].rearrange("p a b -> p (a b)")),
                     reads=[lnst.r], writes=[lnmv.r])
                S.op("act", lambda e: e.activation(out=lnr.t[:], in_=lnmv.t[:, 1:2], func=AF.Ln, bias=LN_EPS),
                     reads=[lnmv.r], writes=[lnr.r])
                S.op("act", lambda e: e.activation(out=lnr.t[:], in_=lnr.t[:], func=AF.Exp, scale=-0.5),
                     reads=[lnr.r], writes=[lnr.r])
                S.op("dve", lambda e: e.tensor_scalar(out=lnmv.t[:, 0:1], in0=lnmv.t[:, 0:1], scalar1=lnr.t[:, 0:1], scalar2=-1.0,
                                                      op0=ALU.mult, op1=ALU.mult), reads=[lnmv.r, lnr.r], writes=[lnmv.r])
                S.op("act", lambda e, row=row: e.activation(out=row, in_=row, func=AF.Identity, scale=lnr.t[:, 0:1], bias=lnmv.t[:, 0:1]),
                     reads=[xb.r, lnmv.r, lnr.r], writes=[xb.r])
                S.op("pool!", lambda e, row=row: e.tensor_tensor(out=row, in0=row, in1=lnrep.t[:, gi, :], op=ALU.mult),
                     reads=[xb.r, CONST], writes=[xb.r])
                S.op("pool!", lambda e, row=row: e.tensor_tensor(out=row, in0=row, in1=lnrep.t[:, gi + 1, :], op=ALU.add),
                     reads=[xb.r, CONST], writes=[xb.r])

        def finish(ti, xb):
            dst = out[ti * T:(ti + 1) * T, :].rearrange("(s p) d -> p s d", p=128)
            S.op("sp", lambda e: e.dma_start(out=dst, in_=xb.t[:]), reads=[xb.r], dma=d_o[ti % 2])
            if ti + 1 < NT and PH < 6:
                x_load(ti + 1)

        def mixer_gen(ti):
            seq, ch = divmod(ti, NCH)
            first = ch == 0
            xb = xt[ti % NXT]
            S.tag = ('m', ti, 'BCD')
            bank_pool[0] = 'b'
            tmq = tmq2[ti % 2]
            cdB = cdB2[ti % 2]
            hT = hbufs[ti % 3]
            h1T = hT
            x_load(ti)
            if first:
                S.op("pool", lambda e: e.memset(Sst.t[:], 0.0), writes=Sst_r)
                S.op("pool", lambda e: e.memset(Sbf.t[:], 0.0), writes=Sbf_r)
                S.op("pool", lambda e: e.memset(carry.t[:], 0.0), writes=carry_r)

            transpose_to_hT(xb, hT)

            bk = bank()
            mm(bk.t[0:32, 0:256], [(wdt.t[:, kc, :], hT.t[:, kc, :]) for kc in range(8)], [CONST, hT.r], [bk.r])
            S.op("act", lambda e, bk=bk: e.activation(out=dt_e.t[:], in_=bk.t[0:32, 0:256], func=AF.Exp, bias=dtb.t[:, 0:1]),
                 reads=[bk.r, CONST], writes=[dt_e.r])
            S.op("act", lambda e: e.activation(out=dt_v.t[:], in_=dt_e.t[:], func=AF.Ln, bias=1.0),
                 reads=[dt_e.r], writes=[dt_v.r])
            S.op("dve", lambda e: e.tensor_scalar(out=dt_e.t[:], in0=dt_v.t[:], scalar1=aneg.t[:, 0:1], scalar2=None, op0=ALU.mult),
                 reads=[dt_v.r, aneg.r], writes=[dt_e.r])
            S.op("dve", lambda e: e.tensor_tensor_scan(out=dt_ecs.t[:], data0=ones32.t[:], data1=dt_e.t[:], initial=0.0,
                                                       op0=ALU.mult, op1=ALU.add),
                 reads=[ones32.r, dt_e.r], writes=[dt_ecs.r])
            scr = acs_scr[ti]
            scr_r = Res("acs_scr%d" % ti)
            S.op("sp", lambda e, scr=scr: e.dma_start(out=scr, in_=dt_ecs.t[:]), reads=[dt_ecs.r], writes=[scr_r], dma=d_aw)

            def bcast_load(gh, scr=scr, scr_r=scr_r):
                ab = acsB[gh % 2]
                src = scr[2 * gh:2 * gh + 2, :].rearrange("h t -> (h t)").partition_broadcast(128)
                S.op("sp", lambda e: e.dma_start(out=ab.t[:].rearrange("p h t -> p (h t)"), in_=src),
                     reads=[scr_r], writes=[ab.r], dma=d_ab[gh % 2])

            S.op("act", lambda e: e.activation(out=dt_e.t[:], in_=dt_v.t[:], func=AF.Ln), reads=[dt_v.r], writes=[dt_e.r])
            S.op("dve", lambda e: e.tensor_tensor(out=dt_q.t[:, 0, :], in0=dt_e.t[:], in1=dt_ecs.t[:], op=ALU.subtract),
                 reads=[dt_e.r, dt_ecs.r], writes=[dt_q.r])
            S.op("act", lambda e: e.activation(out=dt_q.t[:, 1, :], in_=dt_ecs.t[:], func=AF.Exp), reads=[dt_ecs.r], writes=[dt_q.r])
            S.op("act", lambda e: e.activation(out=dt_e.t[:], in_=dt_ecs.t[:], func=AF.Exp, bias=dt_ecs.t[:, 255:256], scale=-1.0),
                 reads=[dt_ecs.r, dt_e.r], writes=[dt_e.r])
            S.op("dve", lambda e: e.tensor_tensor(out=dt_q.t[:, 2, :], in0=dt_e.t[:], in1=dt_v.t[:], op=ALU.mult),
                 reads=[dt_e.r, dt_v.r], writes=[dt_q.r])
            bk = bank()
            for s in range(2):
                for q in range(3):
                    tr(bk.t[:, (s * 3 + q) * 32:(s * 3 + q + 1) * 32], dt_q.t[:, q, s * 128:(s + 1) * 128], ident.t[0:32, 0:32],
                       [dt_q.r, CONST], [bk.r])
            for s in range(2):
                for q in range(3):
                    S.op("dve", lambda e, bk=bk, s=s, q=q: e.tensor_copy(
                        out=tmq.t[:, q, :, s, :], in_=bk.t[:, (s * 3 + q) * 32:(s * 3 + q + 1) * 32].rearrange("p (g h) -> p g h", g=8)),
                         reads=[bk.r], writes=[tmq.r])
            S.op("dve", lambda e: e.tensor_scalar(out=dgm.t[:], in0=ident.t[0:32, 0:32], scalar1=dt_q.t[:, 1, 255:256], scalar2=None,
                                                  op0=ALU.mult), reads=[CONST, dt_q.r], writes=[dgm.r])
            bk = bank()
            mm(bk.t[:, 0:32], [(ones32.t[:, 0:128], dgm.t[:])], [ones32.r, dgm.r], [bk.r])
            S.op("dve", lambda e, bk=bk: e.tensor_copy(out=cdB.t[:], in_=bk.t[:, 0:32]), reads=[bk.r], writes=[cdB.r])

            yield
            zinfo = {}

            def stage_a(g):
                i2 = g % 2
                if g % 2 == 0:
                    zinfo["z"] = w_acquire("Z", g // 2)
                zblk, zk = zinfo["z"]
                zv = zblk.t[:].rearrange("p (k n) -> p k n", k=8)
                bkz = bank()
                for s in range(2):
                    mm(bkz.t[:, s * 256:(s + 1) * 256],
                       [(hT.t[:, kc, s * 128:(s + 1) * 128], zv[:, kc, (g % 2) * 256:(g % 2 + 1) * 256]) for kc in range(8)],
                       [hT.r, zblk.r], [bkz.r])
                zsb = zs[g % 3]
                S.op("act", lambda e: e.activation(out=zsb.t[:].rearrange("p a b -> p (a b)"), in_=bkz.t[:], func=AF.Silu),
                     reads=[bkz.r], writes=[zsb.r])
                xblk, xk = w_acquire("X", g)
                xv = xblk.t[:].rearrange("p (k n) -> p k n", k=8)
                for half in range(2):
                    xrb = xr[half]
                    S.op("dve", lambda e, xrb=xrb, half=half: e.tensor_copy(out=xrb.t[:, :, 0:3], in_=carry.t[:, g, 2 * half:2 * half + 2, :]),
                         reads=[carry_r[g]], writes=[xr_c[half]])
                    bkx = bank()
                    for kk in range(2):
                        ct = half * 2 + kk
                        mm(bkx.t[:, kk * 256:(kk + 1) * 256],
                           [(xv[:, kc, ct * 128:(ct + 1) * 128], hT.t[:, kc, :]) for kc in range(8)], [hT.r, xblk.r], [bkx.r])
                    S.op("act", lambda e, xrb=xrb, bkx=bkx: e.activation(
                        out=xrb.t[:, :, 3:259], in_=bkx.t[:].rearrange("p (a b) -> p a b", a=2), func=AF.Copy),
                         reads=[bkx.r], writes=[xr_h[half]])
                w_done(xk)
                if g % 2 == 1:
                    w_done(zk)
                for half in range(2):
                    xrb = xr[half]
                    S.op("dve", lambda e, xrb=xrb, half=half: e.tensor_copy(out=carry.t[:, g, 2 * half:2 * half + 2, :], in_=xrb.t[:, :, 256:259]),
                         reads=[xr_h[half]], writes=[carry_r[g]])
                cab = cacc[i2]

                def cch_of(ct):
                    return (g * 2 + ct) if ct < 2 else (16 + g if ct == 2 else 24 + g)
                for ct in range(4):
                    cch = cch_of(ct)
                    xrb = xr[ct // 2]
                    S.op("act", lambda e, ct=ct, cch=cch, xrb=xrb: e.activation(
                        out=cab.t[:, ct, :], in_=xrb.t[:, ct % 2, 3:259], func=AF.Identity,
                        bias=convb.t[:, cch:cch + 1], scale=convw.t[:, cch * 4 + 3:cch * 4 + 4]),
                         reads=[xr_h[ct // 2], CONST], writes=[cacc_r[i2][ct]])
                for k in range(3):
                    for ct in range(4):
                        cch = cch_of(ct)
                        xrb = xr[ct // 2]
                        S.op("dve", lambda e, ct=ct, cch=cch, k=k, xrb=xrb: e.scalar_tensor_tensor(
                            out=cab.t[:, ct, :], in0=xrb.t[:, ct % 2, k:k + 256], scalar=convw.t[:, cch * 4 + k:cch * 4 + k + 1],
                            in1=cab.t[:, ct, :], op0=ALU.mult, op1=ALU.add),
                             reads=[xr_h[ct // 2], xr_c[ct // 2], CONST, cacc_r[i2][ct]], writes=[cacc_r[i2][ct]])
                xcb = xc[i2]
                S.op("act", lambda e: e.activation(out=xcb.t[:].rearrange("p a b -> p (a b)"),
                                                   in_=cab.t[:].rearrange("p a b -> p (a b)"), func=AF.Silu),
                     reads=cacc_r[i2], writes=[xcb.r])

            def stage_b(g):
                i2 = g % 2
                xcb = xc[i2]
                xsb, xwb, xdb, btb = xs_tm[i2], xw_tm[i2], xd_tm[i2], b_tm[i2]
                bkt = bank()
                for s in range(2):
                    for ct in range(2):
                        tr(bkt.tb[:, s * 256 + ct * 128: s * 256 + (ct + 1) * 128], xcb.t[:, ct, s * 128:(s + 1) * 128], identb.t[:],
                           [xcb.r, identb.r], [bkt.r])
                    tr(bkt.tb[:, 512 + s * 128: 512 + (s + 1) * 128], xcb.t[:, 2, s * 128:(s + 1) * 128], identb.t[:],
                       [xcb.r, identb.r], [bkt.r])
                S.op("act", lambda e: e.activation(out=xsb.t[:].rearrange("p a b -> p (a b)"), in_=bkt.tb[:, 0:512], func=AF.Copy),
                     reads=[bkt.r], writes=[xsb.r])
                if True:
                    S.op("act", lambda e: e.activation(out=btb.t[:].rearrange("p a b -> p (a b)"), in_=bkt.tb[:, 512:768], func=AF.Copy),
                         reads=[bkt.r], writes=[btb.r])
                else:
                    S.op("dve", lambda e: e.tensor_copy(out=btb.t[:].rearrange("p a b -> p (a b)"), in_=bkt.tb[:, 512:768]),
                         reads=[bkt.r], writes=[btb.r])
                S.op("pool", lambda e: e.tensor_tensor(
                    out=xwb.t[:].rearrange("p s (h d) -> p (s h) d", h=4),
                    in0=xsb.t[:].rearrange("p s (h d) -> p (s h) d", h=4),
                    in1=tmq.t[:, 2, g, :, :].rearrange("p s h -> p (s h)").unsqueeze(2).to_broadcast([128, 8, 64]), op=ALU.mult),
                     reads=[xsb.r, tmq.r], writes=[xwb.r])
                S.op("pool", lambda e: e.tensor_tensor(
                    out=xdb.t[:].rearrange("p s (h d) -> p (s h) d", h=4),
                    in0=xsb.t[:].rearrange("p s (h d) -> p (s h) d", h=4),
                    in1=drep.t[:, g * 8:(g + 1) * 8].unsqueeze(2).to_broadcast([128, 8, 64]), op=ALU.mult),
                     reads=[xsb.r, CONST], writes=[xdb.r])
                bkc = bank()
                mm(bkc.t[:, 0:256], [(xcb.t[:, 2, 0:128], xcb.t[:, 3, 0:256])], [xcb.r], [bkc.r])
                mm(bkc.t[:, 256:384], [(xcb.t[:, 2, 128:256], xcb.t[:, 3, 128:256])], [xcb.r], [bkc.r])
                cb = cbm[i2]
                S.op("dve", lambda e: e.tensor_tensor(out=cb.t[:], in0=bkc.t[:, 0:384], in1=maskc.t[:], op=ALU.mult),
                     reads=[bkc.r, CONST], writes=[cb.r])
                bko = bank()
                for ls in range(2):
                    mm(bko.t[:, ls * 256:(ls + 1) * 256], [(xcb.t[:, 3, ls * 128:(ls + 1) * 128], Sbf.t[:, g * 256:(g + 1) * 256])],
                       [xcb.r, Sbf_r[g]], [bko.r])
                yt = ytmp[g % 2]
                S.op("dve", lambda e: e.tensor_tensor(
                    out=yt.t[:].rearrange("p s (h d) -> p (s h) d", h=4),
                    in0=bko.t[:].rearrange("p (a d) -> p a d", a=8),
                    in1=tmq.t[:, 1, g, :, :].rearrange("p s h -> p (s h)").unsqueeze(2).to_broadcast([128, 8, 64]), op=ALU.mult),
                     reads=[bko.r, tmq.r], writes=[yt.r])
                bks = bank()
                mm(bks.t[:, 0:256], [(btb.t[:, sc, :], xwb.t[:, sc, :]) for sc in range(2)], [btb.r, xwb.r], [bks.r])
                Sg = Sst.t[:, g * 256:(g + 1) * 256]
                S.op("dve", lambda e: e.tensor_tensor(
                    out=Sg.rearrange("p (h d) -> p h d", h=4), in0=Sg.rearrange("p (h d) -> p h d", h=4),
                    in1=cdB.t[:, 4 * g:4 * g + 4].unsqueeze(2).to_broadcast([128, 4, 64]), op=ALU.mult),
                     reads=[Sst_r[g], cdB.r], writes=[Sst_r[g]])
                S.op("dve", lambda e: e.tensor_tensor(out=Sg, in0=Sg, in1=bks.t[:, 0:256], op=ALU.add),
                     reads=[Sst_r[g], bks.r], writes=[Sst_r[g]])
                S.op("act", lambda e: e.activation(out=Sbf.t[:, g * 256:(g + 1) * 256], in_=Sg, func=AF.Copy),
                     reads=[Sst_r[g]], writes=[Sbf_r[g]])

            ydiag_banks = {}

            def stage_c(g):
                i2 = g % 2
                xsb, xdb = xs_tm[i2], xd_tm[i2]
                cb = cbm[i2]
                bcast_load(2 * g)
                bcast_load(2 * g + 1)
                bkd = bank()
                ydiag_banks[g] = bkd
                for h in range(4):
                    L = lt[h % 2]
                    M = mt[h % 2]
                    ab = acsB[h // 2]
                    S.op("act", lambda e, L=L, h=h, ab=ab: e.activation(
                        out=L.t[:, 0:256], in_=ab.t[:, h % 2, :], func=AF.Exp, bias=tmq.t[:, 0, g, 0, h:h + 1]),
                         reads=[ab.r, tmq.r], writes=[lt_r[h % 2][0]])
                    S.op("act", lambda e, L=L, h=h, ab=ab: e.activation(
                        out=L.t[:, 256:384], in_=ab.t[:, h % 2, 128:256], func=AF.Exp, bias=tmq.t[:, 0, g, 1, h:h + 1]),
                         reads=[ab.r, tmq.r], writes=[lt_r[h % 2][1]])
                    S.op("dve", lambda e, L=L, M=M: e.scalar_tensor_tensor(out=M.t[:], in0=L.t[:], scalar=1e30, in1=cb.t[:],
                                                                          op0=ALU.min, op1=ALU.mult),
                         reads=[lt_r[h % 2][0], lt_r[h % 2][1], cb.r], writes=[M.r])
                    hs = slice(h * 64, (h + 1) * 64)
                    mm(bkd.t[:, h * 64:(h + 1) * 64],
                       [(M.t[:, 0:128], xsb.t[:, 0, hs]), (identb.t[:], xdb.t[:, 0, hs])],
                       [M.r, xsb.r, xdb.r, identb.r], [bkd.r])
                    mm(bkd.t[:, 256 + h * 64:256 + (h + 1) * 64],
                       [(M.t[:, 128:256], xsb.t[:, 0, hs]), (M.t[:, 256:384], xsb.t[:, 1, hs]), (identb.t[:], xdb.t[:, 1, hs])],
                       [M.r, xsb.r, xdb.r, identb.r], [bkd.r])

            def stage_d(g):
                i2 = g % 2
                bkd = ydiag_banks.pop(g)
                yt = ytmp[g % 2]
                yb = ybuf[i2]
                zsb = zs[g % 3]
                S.op("dve", lambda e: e.tensor_tensor(out=yb.t[:].rearrange("p a b -> p (a b)"),
                                                      in0=yt.t[:].rearrange("p a b -> p (a b)"), in1=bkd.t[:], op=ALU.add),
                     reads=[yt.r, bkd.r], writes=[yb.r])
                S.op("pool", lambda e: e.tensor_tensor(out=yb.t[:], in0=yb.t[:], in1=zsb.t[:], op=ALU.mult),
                     reads=[yb.r, zsb.r], writes=[yb.r])
                sq = ssq[i2]
                rs = rstd[i2]
                for ls in range(2):
                    S.op("act", lambda e, ls=ls: e.activation(out=sqj.t[:], in_=yb.t[:, ls, :], func=AF.Square,
                                                              accum_out=sq.t[:, ls:ls + 1]),
                         reads=[yb.r], writes=[sqj.r, sq.r])
                S.op("act", lambda e: e.activation(out=rs.t[:], in_=sq.t[:], func=AF.Ln, scale=1.0 / 256.0, bias=RMS_EPS),
                     reads=[sq.r], writes=[rs.r])
                S.op("act", lambda e: e.activation(out=rs.t[:], in_=rs.t[:], func=AF.Exp, scale=-0.5),
                     reads=[rs.r], writes=[rs.r])
                yg = ygs[i2]
                for ls in range(2):
                    S.op("act", lambda e, ls=ls: e.activation(out=yg.t[:, ls, :], in_=yb.t[:, ls, :], func=AF.Identity,
                                                              scale=rs.t[:, ls:ls + 1]),
                         reads=[yb.r, rs.r], writes=[ygs_r[i2][ls]])
                bky = bank()
                for fc in range(2):
                    for ls in range(2):
                        tr(bky.tb[:, fc * 256 + ls * 128: fc * 256 + (ls + 1) * 128], yg.t[:, ls, fc * 128:(fc + 1) * 128], identb.t[:],
                           [ygs_r[i2][ls], identb.r], [bky.r])
                for fc in range(2):
                    S.op("act", lambda e, fc=fc: e.activation(
                        out=ygT.t[:, 2 * g + fc, :], in_=bky.tb[:, fc * 256:(fc + 1) * 256], func=AF.Identity,
                        scale=normw.t[:, 2 * g + fc:2 * g + fc + 1]),
                         reads=[bky.r, CONST], writes=[ygT_r[g]])

            for g in range(8):
                S.tag = ('m', ti, 'E', g, 'a')
                bank_pool[0] = 'e'
                stage_a(g)
                S.tag = ('m', ti, 'E', g, 'b')
                stage_b(g)
                S.tag = ('m', ti, 'E', g, 'c')
                stage_c(g)
                S.tag = ('m', ti, 'E', g, 'd')
                stage_d(g)
                yield

            S.tag = ('m', ti, 'F')
            bank_pool[0] = 'f'
            us = [(2 * ti) % 3, (2 * ti + 1) % 3, (2 * ti + 2) % 3]
            for ub in range(2):
                wb, wk = w_acquire("U", ub)
                wv = wb.t[:].rearrange("p (k n) -> p k n", k=8)
                for s in range(2):
                    bk = bank()
                    mm(bk.t[:], [(hT.t[:, kc, s * 128:(s + 1) * 128], wv[:, kc, :]) for kc in range(8)], [hT.r, wb.r], [bk.r])
                    dst = u_tm.t[:, us[1 + s], ub * 512:(ub + 1) * 512]
                    S.op("act", lambda e, dst=dst, bk=bk: e.activation(out=dst, in_=bk.t[:], func=AF.Copy),
                         reads=[bk.r], writes=[uslot[us[1 + s]]])
                w_done(wk)
            for pair in range(4):
                bk = bank()
                for kk in range(2):
                    cc = pair * 2 + kk
                    wi = cc // 2
                    for s in range(2):
                        o_ap = bk.t[:, kk * 256 + s * 128: kk * 256 + (s + 1) * 128]
                        pairs = []
                        rds = [CONST]
                        src_prev = us[s]
                        src_cur = us[1 + s]
                        if not (first and s == 0):
                            pairs.append((u_tm.t[:, src_prev, cc * 128:(cc + 1) * 128], pmat.t[:, (8 + wi) * 128:(9 + wi) * 128]))
                            rds.append(uslot[src_prev])
                            dk = 0
                        else:
                            dk = 4
                        pairs.append((u_tm.t[:, src_cur, cc * 128:(cc + 1) * 128], pmat.t[:, (dk + wi) * 128:(dk + wi + 1) * 128]))
                        rds.append(uslot[src_cur])
                        mm(o_ap, pairs, rds, [bk.r])
                dst = pooledT.t[:, pair * 2:pair * 2 + 2, :].rearrange("p a b -> p (a b)")
                S.op("act", lambda e, dst=dst, bk=bk: e.activation(out=dst, in_=bk.t[:], func=AF.Copy),
                     reads=[bk.r], writes=[pooledT.r])

            yield
            for q in range(4):
                S.tag = ('m', ti, 'F')
                bank_pool[0] = 'f'
                gblk, gk = w_acquire("GG", q)
                gv = gblk.t[:].rearrange("p (k n) -> p k n", k=8)
                sblk, sk = w_acquire("S", q)
                sv = sblk.t[:].rearrange("p (k n) -> p k n", k=16)
                for jj in range(2):
                    j = 2 * q + jj
                    bkg = bank()
                    mm(bkg.t[:, 0:256], [(gv[:, kc, jj * 128:(jj + 1) * 128], hT.t[:, kc, :]) for kc in range(8)], [gblk.r, hT.r], [bkg.r])
                    mm(bkg.t[:, 256:512], [(gv[:, kc, 256 + jj * 128:256 + (jj + 1) * 128], hT.t[:, kc, :]) for kc in range(8)],
                       [gblk.r, hT.r], [bkg.r])
                    bkm = bank()
                    gi, dd = divmod(j, 2)
                    mm(bkm.t[:, 0:256], [(wpool.t[:, gi * 2 + c2, dd * 128:(dd + 1) * 128], pooledT.t[:, gi * 2 + c2, :]) for c2 in range(2)],
                       [CONST, pooledT.r], [bkm.r])
                    mm(bkm.t[:, 256:512], [(sv[:, fc, jj * 128:(jj + 1) * 128], ygT.t[:, fc, :]) for fc in range(16)],
                       [sblk.r] + ygT_r, [bkm.r])
                    S.op("act", lambda e, bkg=bkg, j=j: e.activation(out=sg.t[:, 0, :], in_=bkg.t[:, 0:256], func=AF.Sigmoid,
                                                                     bias=bg.t[:, j:j + 1]), reads=[bkg.r, CONST], writes=[sg.r])
                    S.op("act", lambda e, bkg=bkg, j=j: e.activation(out=sg.t[:, 1, :], in_=bkg.t[:, 256:512], func=AF.Sigmoid,
                                                                     bias=bg.t[:, 8 + j:9 + j]), reads=[bkg.r, CONST], writes=[sg.r])
                    S.op("dve", lambda e, bkm=bkm, j=j: e.scalar_tensor_tensor(out=mP.t[:], in0=bkm.t[:, 0:256], scalar=pscale.t[:, j:j + 1],
                                                                               in1=sg.t[:, 0, :], op0=ALU.mult, op1=ALU.mult),
                         reads=[bkm.r, sg.r, CONST], writes=[mP.r])
                    S.op("dve", lambda e, bkm=bkm: e.tensor_tensor(out=mS.t[:], in0=bkm.t[:, 256:512], in1=sg.t[:, 1, :], op=ALU.mult),
                         reads=[bkm.r, sg.r], writes=[mS.r])
                    S.op("pool", lambda e, j=j: e.tensor_tensor(out=mergedT.t[:, j, :], in0=mP.t[:], in1=mS.t[:], op=ALU.add),
                         reads=[mP.r, mS.r], writes=[mergedT.r])
                yield

            S.tag = ('m', ti, 'G')
            bank_pool[0] = 'f'
            for half in range(2):
                oblk, ok_ = w_acquire("O", half)
                ov = oblk.t[:].rearrange("p (k n) -> p k n", k=8)
                for s in range(2):
                    bk = bank()
                    mm(bk.t[:], [(mergedT.t[:, dc, s * 128:(s + 1) * 128], ov[:, dc, :]) for dc in range(8)], [mergedT.r, oblk.r], [bk.r])
                    seg = xb.t[:, s, half * 512:(half + 1) * 512]
                    S.op("dve", lambda e, seg=seg, bk=bk: e.scalar_tensor_tensor(out=seg, in0=seg, scalar=ALPHA, in1=bk.t[:],
                                                                                 op0=ALU.mult, op1=ALU.add),
                         reads=[xb.r, bk.r], writes=[xb.r])
                w_done(ok_)
            layer_norm(xb, 0)
            transpose_to_hT(xb, h1T)
            yield

        def ffn_gen(ti):
            xb = xt[ti % NXT]
            h1T = hbufs[ti % 3]

            for ub in range(8):
                S.tag = ('f', ti, 'I')
                bank_pool[0] = 'f'
                ublk, uk = w_acquire("UP", ub)
                uv = ublk.t[:].rearrange("p (k n) -> p k n", k=8)
                for pr in range(2):
                    bk = bank()
                    for kk in range(2):
                        fcl = pr * 2 + kk
                        mm(bk.t[:, kk * 256:(kk + 1) * 256], [(uv[:, kc, fcl * 128:(fcl + 1) * 128], h1T.t[:, kc, :]) for kc in range(8)],
                           [ublk.r, h1T.r], [bk.r])
                    fc0 = ub * 4 + pr * 2
                    rt = sg.t[:].rearrange("p a b -> p (a b)")
                    S.op("act", lambda e, bk=bk, rt=rt: e.activation(out=rt, in_=bk.t[:], func=AF.Relu), reads=[bk.r], writes=[sg.r])
                    S.op("dve", lambda e, fc0=fc0, rt=rt: e.tensor_tensor(out=aT.t[:, fc0:fc0 + 2, :].rearrange("p a b -> p (a b)"),
                                                                        in0=rt, in1=rt, op=ALU.mult),
                         reads=[sg.r], writes=[aT.r])
                yield

            for half in range(2):
                S.tag = ('f', ti, 'J')
                bank_pool[0] = 'f'
                accs = [bank() for _ in range(2)]
                for j in range(4):
                    dblk, dk_ = w_acquire("DN", half * 4 + j)
                    dv = dblk.t[:].rearrange("p (k n) -> p k n", k=8)
                    for s in range(2):
                        ab_ = accs[s]

                        for f0 in range(0, 8, 4):
                            def fn(e, ab_=ab_, j=j, s=s, dv=dv, f0=f0):
                                ins = None
                                for f in range(f0, f0 + 4):
                                    ins = e.matmul(out=ab_.t[:], lhsT=aT.t[:, j * 8 + f, s * 128:(s + 1) * 128], rhs=dv[:, f, :],
                                                   start=(j == 0 and f == 0), stop=(j == 3 and f == 7))
                                return ins
                            S.op("pe", fn, reads=[aT.r, dblk.r], writes=[ab_.r])
                for s in range(2):
                    seg = xb.t[:, s, half * 512:(half + 1) * 512]
                    ab_ = accs[s]
                    S.op("dve", lambda e, seg=seg, ab_=ab_: e.scalar_tensor_tensor(out=seg, in0=seg, scalar=ALPHA, in1=ab_.t[:],
                                                                                   op0=ALU.mult, op1=ALU.add),
                         reads=[xb.r, ab_.r], writes=[xb.r])
                yield
            layer_norm(xb, 2)
            dst = out[ti * T:(ti + 1) * T, :].rearrange("(s p) d -> p s d", p=128)
            S.op("sp", lambda e, dst=dst, xb=xb: e.dma_start(out=dst, in_=xb.t[:]), reads=[xb.r], dma=d_o[ti % NXT])

            yield

        for it in range(NT + 1):
            gens = []
            if it < NT:
                gens.append(mixer_gen(it))
            if it >= 1:
                gens.append(ffn_gen(it - 1))
            while gens:
                for gnr in list(gens):
                    try:
                        next(gnr)
                    except StopIteration:
                        gens.remove(gnr)

        if os.environ.get("DBGW"):
            dbg_w = nc.dram_tensor("dbg_w", [NB, 128, 4096], BF16, kind="ExternalOutput").ap()
            d_dbg = S.dma_sem("dbg")
            S.op("sp", lambda e: e.dma_start(out=dbg_w, in_=wsc), reads=[r for pair in wres for r in pair], dma=d_dbg)
        if LIST_SCHED:
            S.schedule()
        S.finalize()
        S.check_deadlock()
        with nc.Block() as block:
            S.emit(block)
        with nc.Block() as block2:
            @block2.sync
            def _(e):
                if os.environ.get("DBGW"):
                    e.wait_ge(S.sems[d_dbg], S.final_counts[d_dbg])
                for i in range(3):
                    if d_o[i] in S.final_counts:
                        e.wait_ge(S.sems[d_o[i]], S.final_counts[d_o[i]])
    return nc


def _pool_mats():
    pm = np.zeros((12, 128, 128), np.float32)
    tt = np.arange(128)
    for wi, w in enumerate((2, 4, 8, 16)):
        d = tt[None, :] - tt[:, None]
        inwin = (d >= 0) & (d < w)
        pm[wi] = inwin * (1.0 / w) - np.eye(128)
        cnt = np.minimum(tt + 1, w).astype(np.float32)
        pm[4 + wi] = inwin * (1.0 / cnt)[None, :] - np.eye(128)
        dp = tt[None, :] + 128 - tt[:, None]
        pm[8 + wi] = ((dp >= 0) & (dp < w)) * (1.0 / w)
    return np.ascontiguousarray(pm.transpose(1, 0, 2).reshape(128, 12 * 128))


def _mask_c():
    tt = np.arange(128)
    tri = (tt[None, :] >= tt[:, None]).astype(np.float32)
    return np.ascontiguousarray(np.concatenate([tri, np.ones((128, 128), np.float32), tri], axis=1))


def _kblock(w, cols):
    sub = w[:, cols]
    kcs = sub.shape[0] // 128
    return np.ascontiguousarray(sub.reshape(kcs, 128, -1).transpose(1, 0, 2).reshape(128, -1))


def _prep_shared(inp):
    w_in = inp["w_in"][0]
    ar = np.arange
    blocks = []
    for kind, i in WB:
        if kind == "U":
            blk = _kblock(w_in, 512 * i + ar(512))
        elif kind == "Z":
            blk = _kblock(w_in, 1024 + 512 * i + ar(512))
        elif kind == "X":
            cols = np.concatenate([3072 + 256 * i + ar(256), 5120 + 128 * i + ar(128), 6144 + 128 * i + ar(128)])
            blk = _kblock(w_in, cols)
        elif kind == "GG":
            cols = np.concatenate([7200 + 256 * i + ar(256), 8224 + 256 * i + ar(256)])
            blk = _kblock(w_in, cols)
        elif kind == "S":
            blk = _kblock(inp["w_ssd_proj"][0], 256 * i + ar(256))
        elif kind == "O":
            blk = _kblock(inp["w_out"][0], 512 * i + ar(512))
        elif kind == "UP":
            blk = _kblock(inp["w_up"][0], 512 * i + ar(512))
        elif kind == "DN":
            half, j = divmod(i, 4)
            blk = _kblock(inp["w_down"][0][1024 * j:1024 * (j + 1), :], 512 * half + ar(512))
        assert blk.shape == (128, 4096), (kind, blk.shape)
        blocks.append(blk)
    wall = np.ascontiguousarray(np.stack(blocks, 0), dtype=np.float32)
    wdt_h = _kblock(w_in, 7168 + ar(32))
    wp = inp["w_pool_group"][0]
    wpool_h = np.ascontiguousarray(wp.reshape(4, 2, 128, 256).transpose(2, 0, 1, 3).reshape(128, 2048))
    cw = inp["conv_w"][0]
    convw_h = np.ascontiguousarray(cw.reshape(4, 32, 128).transpose(2, 1, 0).reshape(128, 128))
    convb_h = np.ascontiguousarray(inp["conv_b"][0].reshape(32, 128).T)
    bg_h = np.ascontiguousarray(inp["b_gates"][0].reshape(16, 128).T)
    pscale_h = np.ascontiguousarray(inp["pool_scale"][0].reshape(8, 128).T)
    normw_h = np.ascontiguousarray(inp["ssd_norm_w"][0].reshape(16, 128).T)
    dtb_h = np.ascontiguousarray(inp["dt_bias"][0].reshape(32, 1))
    alog_h = np.ascontiguousarray(inp["a_log"][0].reshape(32, 1))
    drep_h = np.ascontiguousarray(np.broadcast_to(inp["d_skip"][0].reshape(1, 8, 1, 4), (128, 8, 2, 4)).reshape(128, 64))
    ln_h = np.ascontiguousarray(np.stack([inp["ln1_g"][0], inp["ln1_b"][0], inp["ln2_g"][0], inp["ln2_b"][0]], 0))
    shared = dict(wall=wall, wdt_h=wdt_h, wpool_h=wpool_h, pmat_h=_pool_mats(), maskc_h=_mask_c(),
                  ident_h=np.eye(128, dtype=np.float32), convw_h=convw_h, convb_h=convb_h, bg_h=bg_h, pscale_h=pscale_h,
                  normw_h=normw_h, dtb_h=dtb_h, alog_h=alog_h, drep_h=drep_h, ln_h=ln_h)
    return {k: np.ascontiguousarray(v, dtype=np.float32) for k, v in shared.items()}


_NC_CACHE = {}


def kernel(**inputs):
    inp = {k: np.asarray(v, dtype=np.float32) for k, v in inputs.items()}
    x = inp["x"]
    shared = _prep_shared(inp)
    if "nc" not in _NC_CACHE:
        _NC_CACHE["nc"] = build_nc(NT_FULL)
    nc = _NC_CACHE["nc"]
    in_maps = []
    for c in range(NCORES):
        m = dict(shared)
        m["xin"] = np.ascontiguousarray(x[NSEQ * c:NSEQ * (c + 1)].reshape(NSEQ * SEQ, D))
        in_maps.append(m)
    res = run_bass_kernel_spmd(nc, in_maps, core_ids=list(range(NCORES)))
    outs = [np.asarray(r["out"]).reshape(NSEQ, SEQ, D) for r in res.results]
    return np.concatenate(outs, axis=0).astype(np.float32)
```

```python
import numpy as np
from contextlib import ExitStack
import concourse.bass as bass
import concourse.mybir as mybir
from concourse.bass_utils import run_bass_kernel_spmd

F32 = mybir.dt.float32
BF16 = mybir.dt.bfloat16
AF = mybir.ActivationFunctionType
ALU = mybir.AluOpType

NCORES = 8
SEQ = 2048
D = 1024
T = 256
NCH = SEQ // T
NSEQ = 4
NT_FULL = NSEQ * NCH
ALPHA = float(2.0 ** 0.25)
LN_EPS = 1e-5
RMS_EPS = 1e-5
import os
NSLOT = 5
NM_SLOT = int(os.environ.get('NM_SLOT', '3'))
SEQ_E = bool(int(os.environ.get('SEQ_E', '0')))
TG = os.environ.get('TG', 'p')
HOP_LAT = float(os.environ.get('HOP_LAT', '0.6'))
SAME_LAT = float(os.environ.get('SAME_LAT', '0.1'))
ACT_SCALE = float(os.environ.get('ACT_SCALE', '1.0'))
PRIO_MODE = os.environ.get('PRIO_MODE', 'blevel')
MM_CHUNK = int(os.environ.get('MM_CHUNK', '4'))
TABLE_AWARE = int(os.environ.get('TABLE_AWARE', '1'))
TBL_US = 1.28
TBL_PEN = float(os.environ.get('TBL_PEN', '1.28'))
PE_SCALE = float(os.environ.get('PE_SCALE', '0.9'))
DVE_SCALE = float(os.environ.get('DVE_SCALE', '0.9'))
EMBED_WAIT = bool(int(os.environ.get('EMBED_WAIT', '1')))
LIST_SCHED = bool(int(os.environ.get('LIST_SCHED', '1')))

WB = [("U", 0), ("U", 1)]
for _j in range(4):
    WB += [("Z", _j), ("X", 2 * _j), ("X", 2 * _j + 1)]
for _q in range(4):
    WB += [("GG", _q), ("S", _q)]
WB += [("O", 0), ("O", 1)] + [("UP", i) for i in range(8)] + [("DN", i) for i in range(8)]
NB = len(WB)


class Res:
    __slots__ = ("name", "last_w", "readers")

    def __init__(self, name):
        self.name = name
        self.last_w = None
        self.readers = []


class Op:
    __slots__ = ("eng", "fn", "deps", "signal", "sigval", "owner", "inc", "is_dma", "idx", "cost", "users", "nbytes", "tag", "t0", "t1", "prio", "family")


class _CostProbe:
    def __init__(self, eng):
        self.eng = eng
        self.cost = 0.0
        self.nbytes = 0
        self.family = None

    def __getattr__(self, name):
        def f(*a, **kw):
            if name == "matmul":
                r = kw["rhs"]
                mult = 4.0 if r.dtype == F32 else 1.0
                self.cost += (r.free_size() * mult / 2200.0 + 0.02) * PE_SCALE
            elif name == "transpose":
                self.cost += 0.27 if kw["in_"].dtype == F32 else 0.1
            elif name == "dma_start":
                self.nbytes += kw["out"].nbytes()
            elif name == "nop":
                self.cost += 0.05
            else:
                if name == "activation":
                    fnm = str(kw.get("func", a[2] if len(a) > 2 else "")).split(".")[-1]
                    fam = {"Exp": "E", "Ln": "E", "Silu": "S", "Sigmoid": "G"}.get(fnm)
                    if fam is not None:
                        self.family = fam
                out = kw.get("out", None)
                if out is None:
                    out = kw.get("ap", a[0] if a else None)
                cols = out.free_size() if out is not None else 256
                if self.eng == "act":
                    self.cost += (0.25 + cols / 1400.0) * ACT_SCALE
                elif self.eng == "dve":
                    self.cost += (0.15 + cols / 800.0) * DVE_SCALE
                else:
                    self.cost += 0.3 + cols * 0.0032
            return self
        return f


class _EmbedFirst:
    SAFE = ("matmul", "transpose", "activation", "tensor_tensor", "tensor_scalar", "scalar_tensor_tensor", "tensor_copy",
            "memset", "bn_stats", "bn_aggr", "tensor_tensor_scan")

    def __init__(self, e, sem, val):
        self._e = e
        self._sem = sem
        self._val = val
        self._done = False

    def __getattr__(self, name):
        attr = getattr(self._e, name)
        if self._done or not callable(attr):
            return attr

        def f(*a, **kw):
            if self._done:
                return attr(*a, **kw)
            self._done = True
            if name in self.SAFE and kw.get("accum_out", None) is None:
                ins = attr(*a, **kw)
                ins._wait_ge(self._sem, self._val)
                return ins
            self._e.wait_ge(self._sem, self._val)
            return attr(*a, **kw)
        return f


class Sched:
    ENGS = ("pe", "act", "dve", "pool", "sp")
    PRIO = {"BCD": 0, "E": 0, "F": 1, "G": 1, "I": 2, "J": 2, "PRE": 3}
    DEFAULT_COST = {"pe": 0.15, "act": 0.45, "dve": 0.4, "pool": 0.6, "sp": 0.05}

    def __init__(self, nc, es):
        self.nc = nc
        self.es = es
        self.engs = {k: [] for k in self.ENGS}
        self.sems = {}
        self.all_ops = []
        for k in self.ENGS:
            self.sems[k] = es.enter_context(nc.semaphore("s_" + k))

    def dma_sem(self, name):
        s = self.es.enter_context(self.nc.semaphore("d_" + name))
        self.sems["d_" + name] = s
        return "d_" + name

    def op(self, eng, fn, reads=(), writes=(), dma=None, cost=None, nbytes=0):
        if eng == 'pool' and 'p' in TG:
            eng = 'dve'
        if eng == 'pool!':
            eng = 'pool' if 'q' in TG else 'dve'
        o = Op()
        o.family = None
        if cost is None:
            pr = _CostProbe(eng)
            fn(pr)
            cost = max(pr.cost, 0.03)
            nbytes = pr.nbytes
            o.family = pr.family
        o.cost = cost
        o.nbytes = nbytes
        o.users = []
        o.tag = getattr(self, 'tag', None)
        o.prio = self.PRIO.get(o.tag[2], 1) if o.tag else 1
        o.t0 = o.t1 = 0.0
        o.eng = eng
        o.fn = fn
        o.is_dma = dma is not None
        o.owner = dma if dma is not None else eng
        o.inc = 16 if dma is not None else 1
        o.signal = dma is not None
        o.sigval = None
        deps = []
        for r in reads:
            if r.last_w is not None:
                deps.append(r.last_w)
        for w in writes:
            if w.last_w is not None:
                deps.append(w.last_w)
            deps.extend(w.readers)
        for r in reads:
            r.readers.append(o)
        for w in writes:
            w.last_w = o
            w.readers = []
        o.deps = sorted({id(d): d for d in deps if d is not o}.values(), key=lambda d: d.idx)
        o.idx = len(self.all_ops)
        self.all_ops.append(o)
        self.engs[eng].append(o)
        return o

    @staticmethod
    def _needs_wait(o, d):
        if d.is_dma or o.is_dma:
            return True
        if d.eng != o.eng:
            return True
        return o.eng != "pe"

    def finalize(self):
        for o in self.all_ops:
            for d in o.deps:
                if self._needs_wait(o, d):
                    d.signal = True
        cnt = {}
        for eng in self.ENGS:
            for o in self.engs[eng]:
                if o.signal:
                    cnt[o.owner] = cnt.get(o.owner, 0) + o.inc
                    o.sigval = cnt[o.owner]
        self.final_counts = cnt

    def schedule(self):
        ops = self.all_ops
        for o in ops:
            for d in o.deps:
                d.users.append(o)
        unmet = [len(o.deps) for o in ops]
        blevel = [0.0] * len(ops)
        if PRIO_MODE == "blevel":
            for o in reversed(ops):
                b = 0.0
                for u in o.users:
                    lat = SAME_LAT if (u.eng == o.eng and not o.is_dma) else HOP_LAT
                    if blevel[u.idx] + lat > b:
                        b = blevel[u.idx] + lat
                blevel[o.idx] = b + (2.0 + o.nbytes / 280e3 if o.is_dma else o.cost)
            for o in ops:
                o.prio = -blevel[o.idx]
        ready_t = [0.0] * len(ops)
        fin_t = [0.0] * len(ops)
        free = {k: 0.0 for k in self.ENGS}
        ready = {k: [] for k in self.ENGS}
        for o in ops:
            if unmet[o.idx] == 0:
                ready[o.eng].append(o)
        newq = {k: [] for k in self.ENGS}
        dma_free = 0.0
        act_fam = [None]
        n_done = 0
        total = len(ops)
        while n_done < total:
            best = None
            for k in self.ENGS:
                if not ready[k]:
                    continue
                f = free[k]
                if k == "act" and TABLE_AWARE:
                    cand = min(ready[k], key=lambda o: (max(f, ready_t[o.idx]) + (TBL_PEN if (TABLE_AWARE == 1 and o.family and o.family != act_fam[0]) else 0.0),
                                                        o.prio, o.idx))
                    st = max(f, ready_t[cand.idx]) + (TBL_US if (cand.family and cand.family != act_fam[0]) else 0.0)
                else:
                    cand = min(ready[k], key=lambda o: (max(f, ready_t[o.idx]), o.prio, o.idx))
                    st = max(f, ready_t[cand.idx])
                if best is None or (st, cand.idx) < (best[0], best[1].idx):
                    best = (st, cand)
            st, o = best
            if o.eng == 'act' and TABLE_AWARE and o.family:
                act_fam[0] = o.family
            ready[o.eng].remove(o)
            newq[o.eng].append(o)
            if o.is_dma:
                free[o.eng] = st + 0.06
                t0 = max(st, dma_free)
                dma_free = t0 + o.nbytes / 280e3
                fin = dma_free + 2.0
            else:
                fin = st + o.cost
                free[o.eng] = fin
            fin_t[o.idx] = fin
            o.t0, o.t1 = st, fin
            n_done += 1
            for u in o.users:
                unmet[u.idx] -= 1
                lat = (0.0 if o.eng == 'pe' else SAME_LAT) if (u.eng == o.eng and not o.is_dma) else HOP_LAT
                if fin + lat > ready_t[u.idx]:
                    ready_t[u.idx] = fin + lat
                if unmet[u.idx] == 0:
                    ready[u.eng].append(u)
        self.engs = newq
        self.sim_time = max(fin_t) if fin_t else 0.0

    def check_deadlock(self):
        pos = {k: 0 for k in self.engs}
        val = {}
        progress = True
        while progress:
            progress = False
            for k, ops in self.engs.items():
                while pos[k] < len(ops):
                    o = ops[pos[k]]
                    ok = all((not self._needs_wait(o, d)) or val.get(d.owner, 0) >= d.sigval for d in o.deps)
                    if not ok:
                        break
                    if o.signal:
                        val[o.owner] = val.get(o.owner, 0) + o.inc
                        assert val[o.owner] == o.sigval, (o.owner, val[o.owner], o.sigval)
                    pos[k] += 1
                    progress = True
        stuck = {k: (pos[k], len(ops)) for k, ops in self.engs.items() if pos[k] < len(ops)}
        assert not stuck, "deadlock: %s" % stuck

    def emit(self, block):
        sems = self.sems

        def run(eng_name, e):
            waited = {}
            for o in self.engs[eng_name]:
                need = {}
                for d in o.deps:
                    if not self._needs_wait(o, d):
                        continue
                    if d.sigval > waited.get(d.owner, 0):
                        need[d.owner] = max(need.get(d.owner, 0), d.sigval)
                items = list(need.items())
                for owner, v in items:
                    waited[owner] = v
                embed = None
                if items and EMBED_WAIT and (not o.is_dma) and eng_name in ("pe", "act", "dve"):
                    embed = items.pop()
                for owner, v in items:
                    e.wait_ge(sems[owner], v)
                if embed is not None:
                    ins = o.fn(_EmbedFirst(e, sems[embed[0]], embed[1]))
                else:
                    ins = o.fn(e)
                if o.signal:
                    ins.then_inc(sems[o.owner], o.inc)

        @block.tensor
        def _(e):
            run("pe", e)

        @block.scalar
        def _(e):
            run("act", e)

        @block.vector
        def _(e):
            run("dve", e)

        @block.gpsimd
        def _(e):
            run("pool", e)

        @block.sync
        def _(e):
            run("sp", e)


class Buf:
    __slots__ = ("t", "r", "tb")

    def __init__(self, t, name):
        self.t = t
        self.r = Res(name)
        self.tb = None


def build_nc(NT=NT_FULL, PH=99):
    nc = bass.Bass("TRN2", target_bir_lowering=False)

    def din(name, shape, dt=F32):
        return nc.dram_tensor(name, shape, dt, kind="ExternalInput").ap()

    xin = din("xin", [NT * T, D])
    wall = din("wall", [NB, 128, 4096])
    wdt_h = din("wdt_h", [128, 256])
    wpool_h = din("wpool_h", [128, 2048])
    pmat_h = din("pmat_h", [128, 12 * 128])
    maskc_h = din("maskc_h", [128, 384])
    ident_h = din("ident_h", [128, 128])
    convw_h = din("convw_h", [128, 128])
    convb_h = din("convb_h", [128, 32])
    bg_h = din("bg_h", [128, 16])
    pscale_h = din("pscale_h", [128, 8])
    normw_h = din("normw_h", [128, 16])
    dtb_h = din("dtb_h", [32, 1])
    alog_h = din("alog_h", [32, 1])
    drep_h = din("drep_h", [128, 64])
    ln_h = din("ln_h", [4, 1024])
    out = nc.dram_tensor("out", [NT * T, D], F32, kind="ExternalOutput").ap()
    wsc = nc.dram_tensor("wsc", [NB, 128, 4096], BF16).ap()
    acs_scr = nc.dram_tensor("acs_scr", [NT, 32, 256], F32).ap()

    with ExitStack() as es:
        S = Sched(nc, es)

        def sb(name, shape, dt=F32):
            return Buf(es.enter_context(nc.sbuf_tensor(name, shape, dt)), name)

        ident = sb("ident", [128, 128])
        identb = sb("identb", [128, 128], BF16)
        pmat = sb("pmat", [128, 12 * 128], BF16)
        maskc = sb("maskc", [128, 384])
        convw = sb("convw", [128, 128])
        convb = sb("convb", [128, 32])
        bg = sb("bg", [128, 16])
        pscale = sb("pscale", [128, 8])
        normw = sb("normw", [128, 16])
        dtb = sb("dtb", [32, 1])
        aneg = sb("aneg", [32, 1])
        drep = sb("drep", [128, 64])
        lnrep = sb("lnrep", [128, 4, 1024])
        wdt = sb("wdt", [128, 8, 32], BF16)
        wpool = sb("wpool", [128, 8, 256], BF16)
        ones32 = sb("ones32", [32, 256])
        CONST = Res("consts_ready")

        ring = [sb("wring%d" % i, [128, 4096], BF16) for i in range(NSLOT)]
        NXT = int(os.environ.get("NXT", "3"))
        xt = [sb("xt%d" % i, [128, 2, 1024]) for i in range(NXT)]
        hT = sb("hT", [128, 8, 256], BF16)
        h1T = sb("h1T", [128, 8, 256], BF16)
        h2T = sb("h2T", [128, 8, 256], BF16)
        hbufs = [hT, h1T, h2T]
        u_tm = sb("u_tm", [128, 3, 1024], BF16)
        uslot = [Res("uslot%d" % i) for i in range(3)]
        pooledT = sb("pooledT", [128, 8, 256], BF16)
        zs = [sb("zs%d" % i, [128, 2, 256], BF16) for i in range(3)]
        xr = [sb("xr%d" % i, [128, 2, 259]) for i in range(2)]
        xr_h = [Res("xr_h%d" % h) for h in range(2)]
        xr_c = [Res("xr_c%d" % h) for h in range(2)]
        carry = sb("carry", [128, 8, 4, 3])
        carry_r = [Res("carry%d" % g) for g in range(8)]
        cacc = [sb("cacc0", [128, 4, 256])] * 2
        cacc_r = [[Res("cacc_%d" % c) for c in range(4)]] * 2
        xc = [sb("xc0", [128, 4, 256], BF16)] * 2
        xs_tm = [sb("xs_tm%d" % i, [128, 2, 256], BF16) for i in range(2)]
        xw_tm = [sb("xw_tm%d" % i, [128, 2, 256], BF16) for i in range(2)]
        xd_tm = [sb("xd_tm%d" % i, [128, 2, 256], BF16) for i in range(2)]
        b_tm = [sb("b_tm%d" % i, [128, 2, 128], BF16) for i in range(2)]
        dt_e = sb("dt_e", [32, 256])
        dt_v = sb("dt_v", [32, 256])
        dt_ecs = sb("dt_ecs", [32, 256])
        dt_q = sb("dt_q", [32, 3, 256])
        dgm = sb("dgm", [32, 32])
        tmq2 = [sb("tmq%d" % i, [128, 3, 8, 2, 4]) for i in range(2)]
        cdB2 = [sb("cdB%d" % i, [128, 32]) for i in range(2)]
        acsB = [sb("acsB%d" % i, [128, 2, 256]) for i in range(2)]
        cbm = [sb("cbm%d" % i, [128, 384]) for i in range(2)]
        lt = [sb("lt%d" % i, [128, 384]) for i in range(2)]
        lt_r = [[Res("lt%d_%d" % (i, c)) for c in range(2)] for i in range(2)]
        mt = [sb("mt%d" % i, [128, 384], BF16) for i in range(2)]
        Sst = sb("Sst", [128, 2048])
        Sst_r = [Res("Sst%d" % g) for g in range(8)]
        Sbf_r = [Res("Sbf%d" % g) for g in range(8)]
        Sbf = sb("Sbf", [128, 2048], BF16)
        ytmp = [sb("ytmp%d" % i, [128, 2, 256]) for i in range(2)]
        ybuf = [sb("ybuf0", [128, 2, 256])] * 2
        sqj = sb("sqj", [128, 256], BF16)
        ssq = [sb("ssq%d" % i, [128, 2]) for i in range(2)]
        rstd = [sb("rstd%d" % i, [128, 2]) for i in range(2)]
        ygs = [sb("ygs%d" % i, [128, 2, 256], BF16) for i in range(2)]
        ygs_r = [[Res("ygs%d_%d" % (i, c)) for c in range(2)] for i in range(2)]
        ygT = sb("ygT", [128, 16, 256], BF16)
        ygT_r = [Res("ygT%d" % g) for g in range(8)]
        sg = sb("sg", [128, 2, 256])
        mP = sb("mP", [128, 256])
        mS = sb("mS", [128, 256])
        mergedT = sb("mergedT", [128, 8, 256], BF16)
        aT_f32 = sb("aT", [128, 4096])
        aT = Buf(aT_f32.t[:].bitcast(BF16).rearrange("p (a b) -> p a b", a=32), "aT_bf")
        aT.r = aT_f32.r
        lnst = sb("lnst", [128, 2, 6])
        lnmv = sb("lnmv", [128, 2])
        lnr = sb("lnr", [128, 1])

        banks = []
        for i in range(8):
            b = Buf(es.enter_context(nc.psum_tensor("psb%d" % i, [128, 512], F32)), "psb%d" % i)
            b.tb = b.t[:].bitcast(BF16)
            banks.append(b)
        POOLS = {'b': [0], 'e': [1, 2, 3, 4, 5], 'f': [6, 7]}
        bank_ctr = {k: 0 for k in POOLS}
        bank_pool = ["e"]

        def bank():
            p = bank_pool[0]
            lst = POOLS[p]
            b = banks[lst[bank_ctr[p] % len(lst)]]
            bank_ctr[p] += 1
            return b

        d_ps = [S.dma_sem("ps%d" % i) for i in range(NSLOT)]
        d_c = S.dma_sem("c")
        d_w = [S.dma_sem("w%d" % i) for i in range(NSLOT)]
        d_x = [S.dma_sem("x%d" % i) for i in range(3)]
        d_o = [S.dma_sem("o%d" % i) for i in range(3)]
        d_aw = S.dma_sem("aw")
        d_ab = [S.dma_sem("ab%d" % i) for i in range(2)]

        pre_res = []

        def cload(eng, dst_ap, src_ap, sem, name):
            r = Res(name)
            pre_res.append(r)
            S.op(eng, lambda e: e.dma_start(out=dst_ap, in_=src_ap), writes=[r], dma=sem)

        cload("sp", ident.t[:], ident_h, d_c, "c_ident")
        cload("sp", maskc.t[:], maskc_h, d_c, "c_maskc")
        cload("sp", convw.t[:], convw_h, d_c, "c_convw")
        cload("sp", convb.t[:], convb_h, d_c, "c_convb")
        cload("sp", bg.t[:], bg_h, d_c, "c_bg")
        cload("sp", pscale.t[:], pscale_h, d_c, "c_pscale")
        cload("sp", normw.t[:], normw_h, d_c, "c_normw")
        cload("sp", dtb.t[:], dtb_h, d_c, "c_dtb")
        cload("sp", aneg.t[:], alog_h, d_c, "c_alog")
        cload("sp", drep.t[:], drep_h, d_c, "c_drep")
        for i in range(4):
            cload("sp", lnrep.t[:, i, :], ln_h[i, :].partition_broadcast(128), d_c, "c_ln%d" % i)
        def stage_cast(dst_ap, src_ap, ncols, name):
            stv = xt[0].t[:].rearrange("p a b -> p (a b)")[:, 0:ncols]
            S.op("sp", lambda e: e.dma_start(out=stv, in_=src_ap), writes=[xt[0].r], dma=d_x[0])
            r = Res(name)
            pre_res.append(r)
            S.op("dve", lambda e: e.tensor_copy(out=dst_ap, in_=stv), reads=[xt[0].r], writes=[r])

        stage_cast(wdt.t[:].rearrange("p a b -> p (a b)"), wdt_h, 256, "c_wdt")
        stage_cast(wpool.t[:].rearrange("p a b -> p (a b)"), wpool_h, 2048, "c_wpool")
        stage_cast(pmat.t[:], pmat_h, 1536, "c_pmat")
        wres = [[Res("wsc%d_%d" % (b, hf)) for hf in range(2)] for b in range(NB)]
        S.tag = ('p', 0, 'PRE')
        st_f32 = aT_f32.t[:]
        st_r = [Res("stg%d" % i) for i in range(2)]
        xt2b = xt[2].t[:].rearrange("p a b -> p (a b)").bitcast(BF16)
        cast_dst = [xt2b[:, 0:2048], xt2b[:, 2048:4096], h1T.t[:].rearrange("p a b -> p (a b)")]
        if 'B' in TG:
            cast_dst = [ring[0].t[:, 0:2048], ring[1].t[:, 0:2048], ring[2].t[:, 0:2048]]
            cast_r = [ring[0].r, ring[1].r, ring[2].r]
        cast_r = [Res("cst%d" % i) for i in range(3)]
        if 'B' in TG:
            cast_r = [ring[0].r, ring[1].r, ring[2].r]
        d_st = [S.dma_sem("st%d" % i) for i in range(2)]
        cast_engs = ("act", "dve")
        for b in range(NB):
            for hf in range(2):
                i = 2 * b + hf
                stv = st_f32[:, (i % 2) * 2048:(i % 2 + 1) * 2048]
                S.op("sp", lambda e, stv=stv, b=b, hf=hf: e.dma_start(out=stv, in_=wall[b][:, hf * 2048:(hf + 1) * 2048]),
                     writes=[st_r[i % 2]], dma=d_st[i % 2])
                cs = i % 3
                dstv = cast_dst[cs]
                eng = cast_engs[i % 2]
                if eng == "act":
                    S.op("act", lambda e, dstv=dstv, stv=stv: e.activation(out=dstv, in_=stv, func=AF.Copy),
                         reads=[st_r[i % 2]], writes=[cast_r[cs]])
                else:
                    S.op("dve", lambda e, dstv=dstv, stv=stv: e.tensor_copy(out=dstv, in_=stv), reads=[st_r[i % 2]], writes=[cast_r[cs]])
                S.op("sp", lambda e, dstv=dstv, b=b, hf=hf: e.dma_start(out=wsc[b][:, hf * 2048:(hf + 1) * 2048], in_=dstv),
                     reads=[cast_r[cs]], writes=[wres[b][hf]], dma=d_ps[cs])
        S.op("dve", lambda e: e.nop(nofuse=True), writes=st_r + [aT.r])
        S.op("dve", lambda e: e.nop(nofuse=True), writes=cast_r[0:2] + [xt[2].r])
        S.op("dve", lambda e: e.nop(nofuse=True), writes=cast_r[2:3] + [h1T.r])
        S.tag = None
        S.op("sp", lambda e: e.nop(nofuse=True), reads=pre_res, writes=[CONST])
        S.op("dve", lambda e: e.tensor_copy(out=identb.t[:], in_=ident.t[:]), reads=[CONST], writes=[identb.r])
        S.op("act", lambda e: e.activation(out=aneg.t[:], in_=aneg.t[:], func=AF.Exp), reads=[CONST], writes=[aneg.r])
        S.op("dve", lambda e: e.tensor_scalar(out=aneg.t[:], in0=aneg.t[:], scalar1=-1.0, scalar2=None, op0=ALU.mult),
             reads=[aneg.r], writes=[aneg.r])
        S.op("dve", lambda e: e.memset(ones32.t[:], 1.0), writes=[ones32.r])

        wstate = {"m": 0, "f": 0}

        def w_acquire(kind, idx):
            st = "f" if kind in ("UP", "DN") else "m"
            k = wstate[st]
            wstate[st] = k + 1
            slot = (k % NM_SLOT) if st == "m" else NM_SLOT + (k % (NSLOT - NM_SLOT))
            bidx = WB.index((kind, idx))
            src = wsc[bidx]
            dst = ring[slot].t[:]
            S.op("sp", lambda e: e.dma_start(out=dst, in_=src), reads=wres[bidx], writes=[ring[slot].r], dma=d_w[slot])
            return ring[slot], k

        def w_done(k):
            pass

        def mm(out_ap, pairs, reads, writes):
            pairs = list(pairs)
            n = len(pairs)
            last = None
            for c0 in range(0, n, MM_CHUNK):
                sub = list(enumerate(pairs))[c0:c0 + MM_CHUNK]

                def fn(e, sub=sub):
                    ins = None
                    for i, (l, r) in sub:
                        ins = e.matmul(out=out_ap, lhsT=l, rhs=r, start=(i == 0), stop=(i == n - 1))
                    return ins
                last = S.op("pe", fn, reads=reads, writes=writes)
            return last

        def tr(out_ap, in_ap, idt_ep, reads, writes):
            return S.op("pe", lambda e: e.transpose(out=out_ap, in_=in_ap, identity=idt_ep), reads=reads, writes=writes)

        def x_load(ti):
            xb = xt[ti % NXT]
            src = xin[ti * T:(ti + 1) * T, :].rearrange("(s p) d -> p s d", p=128)
            S.op("sp", lambda e: e.dma_start(out=xb.t[:], in_=src), reads=([aT.r, h1T.r] if (ti == 0 and 'W' in TG) else []),
                 writes=[xb.r], dma=d_x[ti % NXT])

        def transpose_to_hT(xb, hdst):
            for pair in range(4):
                bk = bank()
                for kk in range(2):
                    kc = pair * 2 + kk
                    for s in range(2):
                        tr(bk.t[:, kk * 256 + s * 128: kk * 256 + (s + 1) * 128], xb.t[:, s, kc * 128:(kc + 1) * 128],
                           ident.t[:], [xb.r, CONST], [bk.r])
                dst = hdst.t[:, pair * 2:pair * 2 + 2, :].rearrange("p a b -> p (a b)")
                S.op("act", lambda e, dst=dst, bk=bk: e.activation(out=dst, in_=bk.t[:], func=AF.Copy),
                     reads=[bk.r], writes=[hdst.r])

        def layer_norm(xb, gi):
            for s in range(2):
                row = xb.t[:, s, :]
                for hh in range(2):
                    S.op("dve", lambda e, hh=hh, row=row: e.bn_stats(out=lnst.t[:, hh, :], in_=row[:, hh * 512:(hh + 1) * 512]),
                         reads=[xb.r], writes=[lnst.r])
                S.op("dve", lambda e: e.bn_aggr(out=lnmv.t[:], in_=lnst.t[:].rearrange("p a b -> p (a b)")),
                     reads=[lnst.r], writes=[lnmv.r])
                S.op("act", lambda e: e.activation(out=lnr.t[:], in_=lnmv.t[:, 1:2], func=AF.Ln, bias=LN_EPS),
                     reads=[lnmv.r], writes=[lnr.r])
                S.op("act", lambda e: e.activation(out=lnr.t[:], in_=lnr.t[:], func=AF.Exp, scale=-0.5),
                     reads=[lnr.r], writes=[lnr.r])
                S.op("dve", lambda e: e.tensor_scalar(out=lnmv.t[:, 0:1], in0=lnmv.t[:, 0:1], scalar1=lnr.t[:, 0:1], scalar2=-1.0,
                                                      op0=ALU.mult, op1=ALU.mult), reads=[lnmv.r, lnr.r], writes=[lnmv.r])
                S.op("act", lambda e, row=row: e.activation(out=row, in_=row, func=AF.Identity, scale=lnr.t[:, 0:1], bias=lnmv.t[:, 0:1]),
                     reads=[xb.r, lnmv.r, lnr.r], writes=[xb.r])
                S.op("pool!", lambda e, row=row: e.tensor_tensor(out=row, in0=row, in1=lnrep.t[:, gi, :], op=ALU.mult),
                     reads=[xb.r, CONST], writes=[xb.r])
                S.op("pool!", lambda e, row=row: e.tensor_tensor(out=row, in0=row, in1=lnrep.t[:, gi + 1, :], op=ALU.add),
                     reads=[xb.r, CONST], writes=[xb.r])

        def finish(ti, xb):
            dst = out[ti * T:(ti + 1) * T, :].rearrange("(s p) d -> p s d", p=128)
            S.op("sp", lambda e: e.dma_start(out=dst, in_=xb.t[:]), reads=[xb.r], dma=d_o[ti % 2])
            if ti + 1 < NT and PH < 6:
                x_load(ti + 1)

        def mixer_gen(ti):
            seq, ch = divmod(ti, NCH)
            first = ch == 0
            xb = xt[ti % NXT]
            S.tag = ('m', ti, 'BCD')
            bank_pool[0] = 'b'
            tmq = tmq2[ti % 2]
            cdB = cdB2[ti % 2]
            hT = hbufs[ti % 3]
            h1T = hT
            x_load(ti)
            if first:
                S.op("pool", lambda e: e.memset(Sst.t[:], 0.0), writes=Sst_r)
                S.op("pool", lambda e: e.memset(Sbf.t[:], 0.0), writes=Sbf_r)
                S.op("pool", lambda e: e.memset(carry.t[:], 0.0), writes=carry_r)

            transpose_to_hT(xb, hT)

            bk = bank()
            mm(bk.t[0:32, 0:256], [(wdt.t[:, kc, :], hT.t[:, kc, :]) for kc in range(8)], [CONST, hT.r], [bk.r])
            S.op("act", lambda e, bk=bk: e.activation(out=dt_e.t[:], in_=bk.t[0:32, 0:256], func=AF.Exp, bias=dtb.t[:, 0:1]),
                 reads=[bk.r, CONST], writes=[dt_e.r])
            S.op("act", lambda e: e.activation(out=dt_v.t[:], in_=dt_e.t[:], func=AF.Ln, bias=1.0),
                 reads=[dt_e.r], writes=[dt_v.r])
            S.op("dve", lambda e: e.tensor_scalar(out=dt_e.t[:], in0=dt_v.t[:], scalar1=aneg.t[:, 0:1], scalar2=None, op0=ALU.mult),
                 reads=[dt_v.r, aneg.r], writes=[dt_e.r])
            S.op("dve", lambda e: e.tensor_tensor_scan(out=dt_ecs.t[:], data0=ones32.t[:], data1=dt_e.t[:], initial=0.0,
                                                       op0=ALU.mult, op1=ALU.add),
                 reads=[ones32.r, dt_e.r], writes=[dt_ecs.r])
            scr = acs_scr[ti]
            scr_r = Res("acs_scr%d" % ti)
            S.op("sp", lambda e, scr=scr: e.dma_start(out=scr, in_=dt_ecs.t[:]), reads=[dt_ecs.r], writes=[scr_r], dma=d_aw)

            def bcast_load(gh, scr=scr, scr_r=scr_r):
                ab = acsB[gh % 2]
                src = scr[2 * gh:2 * gh + 2, :].rearrange("h t -> (h t)").partition_broadcast(128)
                S.op("sp", lambda e: e.dma_start(out=ab.t[:].rearrange("p h t -> p (h t)"), in_=src),
                     reads=[scr_r], writes=[ab.r], dma=d_ab[gh % 2])

            S.op("act", lambda e: e.activation(out=dt_e.t[:], in_=dt_v.t[:], func=AF.Ln), reads=[dt_v.r], writes=[dt_e.r])
            S.op("dve", lambda e: e.tensor_tensor(out=dt_q.t[:, 0, :], in0=dt_e.t[:], in1=dt_ecs.t[:], op=ALU.subtract),
                 reads=[dt_e.r, dt_ecs.r], writes=[dt_q.r])
            S.op("act", lambda e: e.activation(out=dt_q.t[:, 1, :], in_=dt_ecs.t[:], func=AF.Exp), reads=[dt_ecs.r], writes=[dt_q.r])
            S.op("act", lambda e: e.activation(out=dt_e.t[:], in_=dt_ecs.t[:], func=AF.Exp, bias=dt_ecs.t[:, 255:256], scale=-1.0),
                 reads=[dt_ecs.r, dt_e.r], writes=[dt_e.r])
            S.op("dve", lambda e: e.tensor_tensor(out=dt_q.t[:, 2, :], in0=dt_e.t[:], in1=dt_v.t[:], op=ALU.mult),
                 reads=[dt_e.r, dt_v.r], writes=[dt_q.r])
            bk = bank()
            for s in range(2):
                for q in range(3):
                    tr(bk.t[:, (s * 3 + q) * 32:(s * 3 + q + 1) * 32], dt_q.t[:, q, s * 128:(s + 1) * 128], ident.t[0:32, 0:32],
                       [dt_q.r, CONST], [bk.r])
            for s in range(2):
                for q in range(3):
                    S.op("dve", lambda e, bk=bk, s=s, q=q: e.tensor_copy(
                        out=tmq.t[:, q, :, s, :], in_=bk.t[:, (s * 3 + q) * 32:(s * 3 + q + 1) * 32].rearrange("p (g h) -> p g h", g=8)),
                         reads=[bk.r], writes=[tmq.r])
            S.op("dve", lambda e: e.tensor_scalar(out=dgm.t[:], in0=ident.t[0:32, 0:32], scalar1=dt_q.t[:, 1, 255:256], scalar2=None,
                                                  op0=ALU.mult), reads=[CONST, dt_q.r], writes=[dgm.r])
            bk = bank()
            mm(bk.t[:, 0:32], [(ones32.t[:, 0:128], dgm.t[:])], [ones32.r, dgm.r], [bk.r])
            S.op("dve", lambda e, bk=bk: e.tensor_copy(out=cdB.t[:], in_=bk.t[:, 0:32]), reads=[bk.r], writes=[cdB.r])

            yield
            zinfo = {}

            def stage_a(g):
                i2 = g % 2
                if g % 2 == 0:
                    zinfo["z"] = w_acquire("Z", g // 2)
                zblk, zk = zinfo["z"]
                zv = zblk.t[:].rearrange("p (k n) -> p k n", k=8)
                bkz = bank()
                for s in range(2):
                    mm(bkz.t[:, s * 256:(s + 1) * 256],
                       [(hT.t[:, kc, s * 128:(s + 1) * 128], zv[:, kc, (g % 2) * 256:(g % 2 + 1) * 256]) for kc in range(8)],
                       [hT.r, zblk.r], [bkz.r])
                zsb = zs[g % 3]
                S.op("act", lambda e: e.activation(out=zsb.t[:].rearrange("p a b -> p (a b)"), in_=bkz.t[:], func=AF.Silu),
                     reads=[bkz.r], writes=[zsb.r])
                xblk, xk = w_acquire("X", g)
                xv = xblk.t[:].rearrange("p (k n) -> p k n", k=8)
                for half in range(2):
                    xrb = xr[half]
                    S.op("dve", lambda e, xrb=xrb, half=half: e.tensor_copy(out=xrb.t[:, :, 0:3], in_=carry.t[:, g, 2 * half:2 * half + 2, :]),
                         reads=[carry_r[g]], writes=[xr_c[half]])
                    bkx = bank()
                    for kk in range(2):
                        ct = half * 2 + kk
                        mm(bkx.t[:, kk * 256:(kk + 1) * 256],
                           [(xv[:, kc, ct * 128:(ct + 1) * 128], hT.t[:, kc, :]) for kc in range(8)], [hT.r, xblk.r], [bkx.r])
                    S.op("act", lambda e, xrb=xrb, bkx=bkx: e.activation(
                        out=xrb.t[:, :, 3:259], in_=bkx.t[:].rearrange("p (a b) -> p a b", a=2), func=AF.Copy),
                         reads=[bkx.r], writes=[xr_h[half]])
                w_done(xk)
                if g % 2 == 1:
                    w_done(zk)
                for half in range(2):
                    xrb = xr[half]
                    S.op("dve", lambda e, xrb=xrb, half=half: e.tensor_copy(out=carry.t[:, g, 2 * half:2 * half + 2, :], in_=xrb.t[:, :, 256:259]),
                         reads=[xr_h[half]], writes=[carry_r[g]])
                cab = cacc[i2]

                def cch_of(ct):
                    return (g * 2 + ct) if ct < 2 else (16 + g if ct == 2 else 24 + g)
                for ct in range(4):
                    cch = cch_of(ct)
                    xrb = xr[ct // 2]
                    S.op("act", lambda e, ct=ct, cch=cch, xrb=xrb: e.activation(
                        out=cab.t[:, ct, :], in_=xrb.t[:, ct % 2, 3:259], func=AF.Identity,
                        bias=convb.t[:, cch:cch + 1], scale=convw.t[:, cch * 4 + 3:cch * 4 + 4]),
                         reads=[xr_h[ct // 2], CONST], writes=[cacc_r[i2][ct]])
                for k in range(3):
                    for ct in range(4):
                        cch = cch_of(ct)
                        xrb = xr[ct // 2]
                        S.op("dve", lambda e, ct=ct, cch=cch, k=k, xrb=xrb: e.scalar_tensor_tensor(
                            out=cab.t[:, ct, :], in0=xrb.t[:, ct % 2, k:k + 256], scalar=convw.t[:, cch * 4 + k:cch * 4 + k + 1],
                            in1=cab.t[:, ct, :], op0=ALU.mult, op1=ALU.add),
                             reads=[xr_h[ct // 2], xr_c[ct // 2], CONST, cacc_r[i2][ct]], writes=[cacc_r[i2][ct]])
                xcb = xc[i2]
                S.op("act", lambda e: e.activation(out=xcb.t[:].rearrange("p a b -> p (a b)"),
                                                   in_=cab.t[:].rearrange("p a b -> p (a b)"), func=AF.Silu),
                     reads=cacc_r[i2], writes=[xcb.r])

            def stage_b(g):
                i2 = g % 2
                xcb = xc[i2]
                xsb, xwb, xdb, btb = xs_tm[i2], xw_tm[i2], xd_tm[i2], b_tm[i2]
                bkt = bank()
                for s in range(2):
                    for ct in range(2):
                        tr(bkt.tb[:, s * 256 + ct * 128: s * 256 + (ct + 1) * 128], xcb.t[:, ct, s * 128:(s + 1) * 128], identb.t[:],
                           [xcb.r, identb.r], [bkt.r])
                    tr(bkt.tb[:, 512 + s * 128: 512 + (s + 1) * 128], xcb.t[:, 2, s * 128:(s + 1) * 128], identb.t[:],
                       [xcb.r, identb.r], [bkt.r])
                S.op("act", lambda e: e.activation(out=xsb.t[:].rearrange("p a b -> p (a b)"), in_=bkt.tb[:, 0:512], func=AF.Copy),
                     reads=[bkt.r], writes=[xsb.r])
                if True:
                    S.op("act", lambda e: e.activation(out=btb.t[:].rearrange("p a b -> p (a b)"), in_=bkt.tb[:, 512:768], func=AF.Copy),
                         reads=[bkt.r], writes=[btb.r])
                else:
                    S.op("dve", lambda e: e.tensor_copy(out=btb.t[:].rearrange("p a b -> p (a b)"), in_=bkt.tb[:, 512:768]),
                         reads=[bkt.r], writes=[btb.r])
                S.op("pool", lambda e: e.tensor_tensor(
                    out=xwb.t[:].rearrange("p s (h d) -> p (s h) d", h=4),
                    in0=xsb.t[:].rearrange("p s (h d) -> p (s h) d", h=4),
                    in1=tmq.t[:, 2, g, :, :].rearrange("p s h -> p (s h)").unsqueeze(2).to_broadcast([128, 8, 64]), op=ALU.mult),
                     reads=[xsb.r, tmq.r], writes=[xwb.r])
                S.op("pool", lambda e: e.tensor_tensor(
                    out=xdb.t[:].rearrange("p s (h d) -> p (s h) d", h=4),
                    in0=xsb.t[:].rearrange("p s (h d) -> p (s h) d", h=4),
                    in1=drep.t[:, g * 8:(g + 1) * 8].unsqueeze(2).to_broadcast([128, 8, 64]), op=ALU.mult),
                     reads=[xsb.r, CONST], writes=[xdb.r])
                bkc = bank()
                mm(bkc.t[:, 0:256], [(xcb.t[:, 2, 0:128], xcb.t[:, 3, 0:256])], [xcb.r], [bkc.r])
                mm(bkc.t[:, 256:384], [(xcb.t[:, 2, 128:256], xcb.t[:, 3, 128:256])], [xcb.r], [bkc.r])
                cb = cbm[i2]
                S.op("dve", lambda e: e.tensor_tensor(out=cb.t[:], in0=bkc.t[:, 0:384], in1=maskc.t[:], op=ALU.mult),
                     reads=[bkc.r, CONST], writes=[cb.r])
                bko = bank()
                for ls in range(2):
                    mm(bko.t[:, ls * 256:(ls + 1) * 256], [(xcb.t[:, 3, ls * 128:(ls + 1) * 128], Sbf.t[:, g * 256:(g + 1) * 256])],
                       [xcb.r, Sbf_r[g]], [bko.r])
                yt = ytmp[g % 2]
                S.op("dve", lambda e: e.tensor_tensor(
                    out=yt.t[:].rearrange("p s (h d) -> p (s h) d", h=4),
                    in0=bko.t[:].rearrange("p (a d) -> p a d", a=8),
                    in1=tmq.t[:, 1, g, :, :].rearrange("p s h -> p (s h)").unsqueeze(2).to_broadcast([128, 8, 64]), op=ALU.mult),
                     reads=[bko.r, tmq.r], writes=[yt.r])
                bks = bank()
                mm(bks.t[:, 0:256], [(btb.t[:, sc, :], xwb.t[:, sc, :]) for sc in range(2)], [btb.r, xwb.r], [bks.r])
                Sg = Sst.t[:, g * 256:(g + 1) * 256]
                S.op("dve", lambda e: e.tensor_tensor(
                    out=Sg.rearrange("p (h d) -> p h d", h=4), in0=Sg.rearrange("p (h d) -> p h d", h=4),
                    in1=cdB.t[:, 4 * g:4 * g + 4].unsqueeze(2).to_broadcast([128, 4, 64]), op=ALU.mult),
                     reads=[Sst_r[g], cdB.r], writes=[Sst_r[g]])
                S.op("dve", lambda e: e.tensor_tensor(out=Sg, in0=Sg, in1=bks.t[:, 0:256], op=ALU.add),
                     reads=[Sst_r[g], bks.r], writes=[Sst_r[g]])
                S.op("act", lambda e: e.activation(out=Sbf.t[:, g * 256:(g + 1) * 256], in_=Sg, func=AF.Copy),
                     reads=[Sst_r[g]], writes=[Sbf_r[g]])

            ydiag_banks = {}

            def stage_c(g):
                i2 = g % 2
                xsb, xdb = xs_tm[i2], xd_tm[i2]
                cb = cbm[i2]
                bcast_load(2 * g)
                bcast_load(2 * g + 1)
                bkd = bank()
                ydiag_banks[g] = bkd
                for h in range(4):
                    L = lt[h % 2]
                    M = mt[h % 2]
                    ab = acsB[h // 2]
                    S.op("act", lambda e, L=L, h=h, ab=ab: e.activation(
                        out=L.t[:, 0:256], in_=ab.t[:, h % 2, :], func=AF.Exp, bias=tmq.t[:, 0, g, 0, h:h + 1]),
                         reads=[ab.r, tmq.r], writes=[lt_r[h % 2][0]])
                    S.op("act", lambda e, L=L, h=h, ab=ab: e.activation(
                        out=L.t[:, 256:384], in_=ab.t[:, h % 2, 128:256], func=AF.Exp, bias=tmq.t[:, 0, g, 1, h:h + 1]),
                         reads=[ab.r, tmq.r], writes=[lt_r[h % 2][1]])
                    S.op("dve", lambda e, L=L, M=M: e.scalar_tensor_tensor(out=M.t[:], in0=L.t[:], scalar=1e30, in1=cb.t[:],
                                                                          op0=ALU.min, op1=ALU.mult),
                         reads=[lt_r[h % 2][0], lt_r[h % 2][1], cb.r], writes=[M.r])
                    hs = slice(h * 64, (h + 1) * 64)
                    mm(bkd.t[:, h * 64:(h + 1) * 64],
                       [(M.t[:, 0:128], xsb.t[:, 0, hs]), (identb.t[:], xdb.t[:, 0, hs])],
                       [M.r, xsb.r, xdb.r, identb.r], [bkd.r])
                    mm(bkd.t[:, 256 + h * 64:256 + (h + 1) * 64],
                       [(M.t[:, 128:256], xsb.t[:, 0, hs]), (M.t[:, 256:384], xsb.t[:, 1, hs]), (identb.t[:], xdb.t[:, 1, hs])],
                       [M.r, xsb.r, xdb.r, identb.r], [bkd.r])

            def stage_d(g):
                i2 = g % 2
                bkd = ydiag_banks.pop(g)
                yt = ytmp[g % 2]
                yb = ybuf[i2]
                zsb = zs[g % 3]
                S.op("dve", lambda e: e.tensor_tensor(out=yb.t[:].rearrange("p a b -> p (a b)"),
                                                      in0=yt.t[:].rearrange("p a b -> p (a b)"), in1=bkd.t[:], op=ALU.add),
                     reads=[yt.r, bkd.r], writes=[yb.r])
                S.op("pool", lambda e: e.tensor_tensor(out=yb.t[:], in0=yb.t[:], in1=zsb.t[:], op=ALU.mult),
                     reads=[yb.r, zsb.r], writes=[yb.r])
                sq = ssq[i2]
                rs = rstd[i2]
                for ls in range(2):
                    S.op("act", lambda e, ls=ls: e.activation(out=sqj.t[:], in_=yb.t[:, ls, :], func=AF.Square,
                                                              accum_out=sq.t[:, ls:ls + 1]),
                         reads=[yb.r], writes=[sqj.r, sq.r])
                S.op("act", lambda e: e.activation(out=rs.t[:], in_=sq.t[:], func=AF.Ln, scale=1.0 / 256.0, bias=RMS_EPS),
                     reads=[sq.r], writes=[rs.r])
                S.op("act", lambda e: e.activation(out=rs.t[:], in_=rs.t[:], func=AF.Exp, scale=-0.5),
                     reads=[rs.r], writes=[rs.r])
                yg = ygs[i2]
                for ls in range(2):
                    S.op("act", lambda e, ls=ls: e.activation(out=yg.t[:, ls, :], in_=yb.t[:, ls, :], func=AF.Identity,
                                                              scale=rs.t[:, ls:ls + 1]),
                         reads=[yb.r, rs.r], writes=[ygs_r[i2][ls]])
                bky = bank()
                for fc in range(2):
                    for ls in range(2):
                        tr(bky.tb[:, fc * 256 + ls * 128: fc * 256 + (ls + 1) * 128], yg.t[:, ls, fc * 128:(fc + 1) * 128], identb.t[:],
                           [ygs_r[i2][ls], identb.r], [bky.r])
                for fc in range(2):
                    S.op("act", lambda e, fc=fc: e.activation(
                        out=ygT.t[:, 2 * g + fc, :], in_=bky.tb[:, fc * 256:(fc + 1) * 256], func=AF.Identity,
                        scale=normw.t[:, 2 * g + fc:2 * g + fc + 1]),
                         reads=[bky.r, CONST], writes=[ygT_r[g]])

            for g in range(8):
                S.tag = ('m', ti, 'E', g, 'a')
                bank_pool[0] = 'e'
                stage_a(g)
                S.tag = ('m', ti, 'E', g, 'b')
                stage_b(g)
                S.tag = ('m', ti, 'E', g, 'c')
                stage_c(g)
                S.tag = ('m', ti, 'E', g, 'd')
                stage_d(g)
                yield

            S.tag = ('m', ti, 'F')
            bank_pool[0] = 'f'
            us = [(2 * ti) % 3, (2 * ti + 1) % 3, (2 * ti + 2) % 3]
            for ub in range(2):
                wb, wk = w_acquire("U", ub)
                wv = wb.t[:].rearrange("p (k n) -> p k n", k=8)
                for s in range(2):
                    bk = bank()
                    mm(bk.t[:], [(hT.t[:, kc, s * 128:(s + 1) * 128], wv[:, kc, :]) for kc in range(8)], [hT.r, wb.r], [bk.r])
                    dst = u_tm.t[:, us[1 + s], ub * 512:(ub + 1) * 512]
                    S.op("act", lambda e, dst=dst, bk=bk: e.activation(out=dst, in_=bk.t[:], func=AF.Copy),
                         reads=[bk.r], writes=[uslot[us[1 + s]]])
                w_done(wk)
            for pair in range(4):
                bk = bank()
                for kk in range(2):
                    cc = pair * 2 + kk
                    wi = cc // 2
                    for s in range(2):
                        o_ap = bk.t[:, kk * 256 + s * 128: kk * 256 + (s + 1) * 128]
                        pairs = []
                        rds = [CONST]
                        src_prev = us[s]
                        src_cur = us[1 + s]
                        if not (first and s == 0):
                            pairs.append((u_tm.t[:, src_prev, cc * 128:(cc + 1) * 128], pmat.t[:, (8 + wi) * 128:(9 + wi) * 128]))
                            rds.append(uslot[src_prev])
                            dk = 0
                        else:
                            dk = 4
                        pairs.append((u_tm.t[:, src_cur, cc * 128:(cc + 1) * 128], pmat.t[:, (dk + wi) * 128:(dk + wi + 1) * 128]))
                        rds.append(uslot[src_cur])
                        mm(o_ap, pairs, rds, [bk.r])
                dst = pooledT.t[:, pair * 2:pair * 2 + 2, :].rearrange("p a b -> p (a b)")
                S.op("act", lambda e, dst=dst, bk=bk: e.activation(out=dst, in_=bk.t[:], func=AF.Copy),
                     reads=[bk.r], writes=[pooledT.r])

            yield
            for q in range(4):
                S.tag = ('m', ti, 'F')
                bank_pool[0] = 'f'
                gblk, gk = w_acquire("GG", q)
                gv = gblk.t[:].rearrange("p (k n) -> p k n", k=8)
                sblk, sk = w_acquire("S", q)
                sv = sblk.t[:].rearrange("p (k n) -> p k n", k=16)
                for jj in range(2):
                    j = 2 * q + jj
                    bkg = bank()
                    mm(bkg.t[:, 0:256], [(gv[:, kc, jj * 128:(jj + 1) * 128], hT.t[:, kc, :]) for kc in range(8)], [gblk.r, hT.r], [bkg.r])
                    mm(bkg.t[:, 256:512], [(gv[:, kc, 256 + jj * 128:256 + (jj + 1) * 128], hT.t[:, kc, :]) for kc in range(8)],
                       [gblk.r, hT.r], [bkg.r])
                    bkm = bank()
                    gi, dd = divmod(j, 2)
                    mm(bkm.t[:, 0:256], [(wpool.t[:, gi * 2 + c2, dd * 128:(dd + 1) * 128], pooledT.t[:, gi * 2 + c2, :]) for c2 in range(2)],
                       [CONST, pooledT.r], [bkm.r])
                    mm(bkm.t[:, 256:512], [(sv[:, fc, jj * 128:(jj + 1) * 128], ygT.t[:, fc, :]) for fc in range(16)],
                       [sblk.r] + ygT_r, [bkm.r])
                    S.op("act", lambda e, bkg=bkg, j=j: e.activation(out=sg.t[:, 0, :], in_=bkg.t[:, 0:256], func=AF.Sigmoid,
                                                                     bias=bg.t[:, j:j + 1]), reads=[bkg.r, CONST], writes=[sg.r])
                    S.op("act", lambda e, bkg=bkg, j=j: e.activation(out=sg.t[:, 1, :], in_=bkg.t[:, 256:512], func=AF.Sigmoid,
                                                                     bias=bg.t[:, 8 + j:9 + j]), reads=[bkg.r, CONST], writes=[sg.r])
                    S.op("dve", lambda e, bkm=bkm, j=j: e.scalar_tensor_tensor(out=mP.t[:], in0=bkm.t[:, 0:256], scalar=pscale.t[:, j:j + 1],
                                                                               in1=sg.t[:, 0, :], op0=ALU.mult, op1=ALU.mult),
                         reads=[bkm.r, sg.r, CONST], writes=[mP.r])
                    S.op("dve", lambda e, bkm=bkm: e.tensor_tensor(out=mS.t[:], in0=bkm.t[:, 256:512], in1=sg.t[:, 1, :], op=ALU.mult),
                         reads=[bkm.r, sg.r], writes=[mS.r])
                    S.op("pool", lambda e, j=j: e.tensor_tensor(out=mergedT.t[:, j, :], in0=mP.t[:], in1=mS.t[:], op=ALU.add),
                         reads=[mP.r, mS.r], writes=[mergedT.r])
                yield

            S.tag = ('m', ti, 'G')
            bank_pool[0] = 'f'
            for half in range(2):
                oblk, ok_ = w_acquire("O", half)
                ov = oblk.t[:].rearrange("p (k n) -> p k n", k=8)
                for s in range(2):
                    bk = bank()
                    mm(bk.t[:], [(mergedT.t[:, dc, s * 128:(s + 1) * 128], ov[:, dc, :]) for dc in range(8)], [mergedT.r, oblk.r], [bk.r])
                    seg = xb.t[:, s, half * 512:(half + 1) * 512]
                    S.op("dve", lambda e, seg=seg, bk=bk: e.scalar_tensor_tensor(out=seg, in0=seg, scalar=ALPHA, in1=bk.t[:],
                                                                                 op0=ALU.mult, op1=ALU.add),
                         reads=[xb.r, bk.r], writes=[xb.r])
                w_done(ok_)
            layer_norm(xb, 0)
            transpose_to_hT(xb, h1T)
            yield

        def ffn_gen(ti):
            xb = xt[ti % NXT]
            h1T = hbufs[ti % 3]

            for ub in range(8):
                S.tag = ('f', ti, 'I')
                bank_pool[0] = 'f'
                ublk, uk = w_acquire("UP", ub)
                uv = ublk.t[:].rearrange("p (k n) -> p k n", k=8)
                for pr in range(2):
                    bk = bank()
                    for kk in range(2):
                        fcl = pr * 2 + kk
                        mm(bk.t[:, kk * 256:(kk + 1) * 256], [(uv[:, kc, fcl * 128:(fcl + 1) * 128], h1T.t[:, kc, :]) for kc in range(8)],
                           [ublk.r, h1T.r], [bk.r])
                    fc0 = ub * 4 + pr * 2
                    rt = sg.t[:].rearrange("p a b -> p (a b)")
                    S.op("act", lambda e, bk=bk, rt=rt: e.activation(out=rt, in_=bk.t[:], func=AF.Relu), reads=[bk.r], writes=[sg.r])
                    S.op("dve", lambda e, fc0=fc0, rt=rt: e.tensor_tensor(out=aT.t[:, fc0:fc0 + 2, :].rearrange("p a b -> p (a b)"),
                                                                        in0=rt, in1=rt, op=ALU.mult),
                         reads=[sg.r], writes=[aT.r])
                yield

            for half in range(2):
                S.tag = ('f', ti, 'J')
                bank_pool[0] = 'f'
                accs = [bank() for _ in range(2)]
                for j in range(4):
                    dblk, dk_ = w_acquire("DN", half * 4 + j)
                    dv = dblk.t[:].rearrange("p (k n) -> p k n", k=8)
                    for s in range(2):
                        ab_ = accs[s]

                        for f0 in range(0, 8, 4):
                            def fn(e, ab_=ab_, j=j, s=s, dv=dv, f0=f0):
                                ins = None
                                for f in range(f0, f0 + 4):
                                    ins = e.matmul(out=ab_.t[:], lhsT=aT.t[:, j * 8 + f, s * 128:(s + 1) * 128], rhs=dv[:, f, :],
                                                   start=(j == 0 and f == 0), stop=(j == 3 and f == 7))
                                return ins
                            S.op("pe", fn, reads=[aT.r, dblk.r], writes=[ab_.r])
                for s in range(2):
                    seg = xb.t[:, s, half * 512:(half + 1) * 512]
                    ab_ = accs[s]
                    S.op("dve", lambda e, seg=seg, ab_=ab_: e.scalar_tensor_tensor(out=seg, in0=seg, scalar=ALPHA, in1=ab_.t[:],
                                                                                   op0=ALU.mult, op1=ALU.add),
                         reads=[xb.r, ab_.r], writes=[xb.r])
                yield
            layer_norm(xb, 2)
            dst = out[ti * T:(ti + 1) * T, :].rearrange("(s p) d -> p s d", p=128)
            S.op("sp", lambda e, dst=dst, xb=xb: e.dma_start(out=dst, in_=xb.t[:]), reads=[xb.r], dma=d_o[ti % NXT])

            yield

        for it in range(NT + 1):
            gens = []
            if it < NT:
                gens.append(mixer_gen(it))
            if it >= 1:
                gens.append(ffn_gen(it - 1))
            while gens:
                for gnr in list(gens):
                    try:
                        next(gnr)
                    except StopIteration:
                        gens.remove(gnr)

        if os.environ.get("DBGW"):
            dbg_w = nc.dram_tensor("dbg_w", [NB, 128, 4096], BF16, kind="ExternalOutput").ap()
            d_dbg = S.dma_sem("dbg")
            S.op("sp", lambda e: e.dma_start(out=dbg_w, in_=wsc), reads=[r for pair in wres for r in pair], dma=d_dbg)
        if LIST_SCHED:
            S.schedule()
        S.finalize()
        S.check_deadlock()
        with nc.Block() as block:
            S.emit(block)
        with nc.Block() as block2:
            @block2.sync
            def _(e):
                if os.environ.get("DBGW"):
                    e.wait_ge(S.sems[d_dbg], S.final_counts[d_dbg])
                for i in range(3):
                    if d_o[i] in S.final_counts:
                        e.wait_ge(S.sems[d_o[i]], S.final_counts[d_o[i]])
    return nc


def _pool_mats():
    pm = np.zeros((12, 128, 128), np.float32)
    tt = np.arange(128)
    for wi, w in enumerate((2, 4, 8, 16)):
        d = tt[None, :] - tt[:, None]
        inwin = (d >= 0) & (d < w)
        pm[wi] = inwin * (1.0 / w) - np.eye(128)
        cnt = np.minimum(tt + 1, w).astype(np.float32)
        pm[4 + wi] = inwin * (1.0 / cnt)[None, :] - np.eye(128)
        dp = tt[None, :] + 128 - tt[:, None]
        pm[8 + wi] = ((dp >= 0) & (dp < w)) * (1.0 / w)
    return np.ascontiguousarray(pm.transpose(1, 0, 2).reshape(128, 12 * 128))


def _mask_c():
    tt = np.arange(128)
    tri = (tt[None, :] >= tt[:, None]).astype(np.float32)
    return np.ascontiguousarray(np.concatenate([tri, np.ones((128, 128), np.float32), tri], axis=1))


def _kblock(w, cols):
    sub = w[:, cols]
    kcs = sub.shape[0] // 128
    return np.ascontiguousarray(sub.reshape(kcs, 128, -1).transpose(1, 0, 2).reshape(128, -1))


def _prep_shared(inp):
    w_in = inp["w_in"][0]
    ar = np.arange
    blocks = []
    for kind, i in WB:
        if kind == "U":
            blk = _kblock(w_in, 512 * i + ar(512))
        elif kind == "Z":
            blk = _kblock(w_in, 1024 + 512 * i + ar(512))
        elif kind == "X":
            cols = np.concatenate([3072 + 256 * i + ar(256), 5120 + 128 * i + ar(128), 6144 + 128 * i + ar(128)])
            blk = _kblock(w_in, cols)
        elif kind == "GG":
            cols = np.concatenate([7200 + 256 * i + ar(256), 8224 + 256 * i + ar(256)])
            blk = _kblock(w_in, cols)
        elif kind == "S":
            blk = _kblock(inp["w_ssd_proj"][0], 256 * i + ar(256))
        elif kind == "O":
            blk = _kblock(inp["w_out"][0], 512 * i + ar(512))
        elif kind == "UP":
            blk = _kblock(inp["w_up"][0], 512 * i + ar(512))
        elif kind == "DN":
            half, j = divmod(i, 4)
            blk = _kblock(inp["w_down"][0][1024 * j:1024 * (j + 1), :], 512 * half + ar(512))
        assert blk.shape == (128, 4096), (kind, blk.shape)
        blocks.append(blk)
    wall = np.ascontiguousarray(np.stack(blocks, 0), dtype=np.float32)
    wdt_h = _kblock(w_in, 7168 + ar(32))
    wp = inp["w_pool_group"][0]
    wpool_h = np.ascontiguousarray(wp.reshape(4, 2, 128, 256).transpose(2, 0, 1, 3).reshape(128, 2048))
    cw = inp["conv_w"][0]
    convw_h = np.ascontiguousarray(cw.reshape(4, 32, 128).transpose(2, 1, 0).reshape(128, 128))
    convb_h = np.ascontiguousarray(inp["conv_b"][0].reshape(32, 128).T)
    bg_h = np.ascontiguousarray(inp["b_gates"][0].reshape(16, 128).T)
    pscale_h = np.ascontiguousarray(inp["pool_scale"][0].reshape(8, 128).T)
    normw_h = np.ascontiguousarray(inp["ssd_norm_w"][0].reshape(16, 128).T)
    dtb_h = np.ascontiguousarray(inp["dt_bias"][0].reshape(32, 1))
    alog_h = np.ascontiguousarray(inp["a_log"][0].reshape(32, 1))
    drep_h = np.ascontiguousarray(np.broadcast_to(inp["d_skip"][0].reshape(1, 8, 1, 4), (128, 8, 2, 4)).reshape(128, 64))
    ln_h = np.ascontiguousarray(np.stack([inp["ln1_g"][0], inp["ln1_b"][0], inp["ln2_g"][0], inp["ln2_b"][0]], 0))
    shared = dict(wall=wall, wdt_h=wdt_h, wpool_h=wpool_h, pmat_h=_pool_mats(), maskc_h=_mask_c(),
                  ident_h=np.eye(128, dtype=np.float32), convw_h=convw_h, convb_h=convb_h, bg_h=bg_h, pscale_h=pscale_h,
                  normw_h=normw_h, dtb_h=dtb_h, alog_h=alog_h, drep_h=drep_h, ln_h=ln_h)
    return {k: np.ascontiguousarray(v, dtype=np.float32) for k, v in shared.items()}


_NC_CACHE = {}


def kernel(**inputs):
    inp = {k: np.asarray(v, dtype=np.float32) for k, v in inputs.items()}
    x = inp["x"]
    shared = _prep_shared(inp)
    if "nc" not in _NC_CACHE:
        _NC_CACHE["nc"] = build_nc(NT_FULL)
    nc = _NC_CACHE["nc"]
    in_maps = []
    for c in range(NCORES):
        m = dict(shared)
        m["xin"] = np.ascontiguousarray(x[NSEQ * c:NSEQ * (c + 1)].reshape(NSEQ * SEQ, D))
        in_maps.append(m)
    res = run_bass_kernel_spmd(nc, in_maps, core_ids=list(range(NCORES)))
    outs = [np.asarray(r["out"]).reshape(NSEQ, SEQ, D) for r in res.results]
    return np.concatenate(outs, axis=0).astype(np.float32)
```
